# Optimizing a Trainium2 kernel written in Bass

```python
import math
import jax
import jax.numpy as jnp
from jax import lax
import numpy as np

D_MODEL = 1024
BATCH = 4
SEQ = 4096
DEPTH = 4

GRID_W = 64
CTX_LEN = 256
N_GROUPS = 4
D_MIX = D_MODEL
GROUP_W = D_MIX // N_GROUPS
HEAD_DIM = 64
GROUP_HEADS = GROUP_W // HEAD_DIM
D_FF = 4 * D_MODEL
CHUNK = 64
RWKV_DECAY_RANK = 64
RWKV_AAA_RANK = 64
RWKV_GATE_RANK = 128
GDN_CONV_W = 5
LN_EPS = 1e-5
RWKV_GN_EPS = 64e-5
NORM_EPS = 1e-6
DEEPNORM_ALPHA = (2.0 * DEPTH) ** 0.25
DEEPNORM_BETA = (8.0 * DEPTH) ** -0.25
A_COLS = (GROUP_W, GROUP_W, 2 * GROUP_W, GROUP_W)
B_COLS = (GROUP_W, GROUP_W, GROUP_W, 2 * RWKV_DECAY_RANK, 2 * RWKV_AAA_RANK, RWKV_GATE_RANK)
C_COLS = (3 * GROUP_W, 2 * GROUP_HEADS, 2 * GROUP_HEADS, GROUP_W)
D_COLS = (3 * GROUP_W, 2 * GROUP_HEADS, 2 * GROUP_HEADS, GROUP_W)
GROUP_COLS = (sum(A_COLS), sum(B_COLS), sum(C_COLS), sum(D_COLS))
D_IN = sum(GROUP_COLS)

kernel_name = "hybrid_parallel_heads_flow_block"


def split_cols(t, widths):
    offsets = [int(o) for o in np.cumsum(widths)[:-1]]
    return jnp.split(t, offsets, axis=-1)


def layer_norm(t, g, b):
    tf = t.astype(jnp.float32)
    mu = jnp.mean(tf, -1, keepdims=True)
    var = jnp.mean(jnp.square(tf - mu), -1, keepdims=True)
    return ((tf - mu) * lax.rsqrt(var + LN_EPS) * g + b).astype(t.dtype)


def modulate(t, shift, scale):
    return t * (1 + scale) + shift


def squared_relu_mlp(h, w1, w2):
    return jnp.square(jax.nn.relu(h @ w1)) @ w2


def to_col_major(t):
    bsz, n, ch = t.shape
    rows = n // GRID_W
    return t.reshape(bsz, rows, GRID_W, ch).transpose(0, 2, 1, 3).reshape(bsz, n, ch)


def from_col_major(t):
    bsz, n, ch = t.shape
    rows = n // GRID_W
    return t.reshape(bsz, GRID_W, rows, ch).transpose(0, 2, 1, 3).reshape(bsz, n, ch)


def split_heads(t):
    return t.reshape(t.shape[:-1] + (GROUP_HEADS, HEAD_DIM))


def heads(t):
    return jnp.swapaxes(split_heads(t), -2, -3)


def unheads(t):
    t = jnp.swapaxes(t, -2, -3)
    return t.reshape(t.shape[:-2] + (GROUP_W,))


def head_scalars(t):
    return jnp.swapaxes(t, -1, -2)


def l2_normalize(t):
    return t * lax.rsqrt(jnp.sum(t * t, -1, keepdims=True) + 1e-12)


def head_l2(t):
    return l2_normalize(split_heads(t)).reshape(t.shape)


def head_rms_norm(t, g):
    th = split_heads(t)
    th = th * lax.rsqrt(jnp.mean(th * th, -1, keepdims=True) + NORM_EPS)
    return th.reshape(t.shape) * g


def group_norm(t, g, b):
    mu = jnp.mean(t, -1, keepdims=True)
    var = jnp.mean(jnp.square(t - mu), -1, keepdims=True)
    tn = (t - mu) * lax.rsqrt(var + RWKV_GN_EPS)
    return tn.reshape(t.shape[:-2] + (GROUP_W,)) * g + b


def dir_shared(t):
    return jnp.stack([t, jnp.flip(t, axis=1)])


def dir_own(t):
    fwd, bwd = jnp.split(t, 2, axis=-1)
    return jnp.stack([fwd, jnp.flip(bwd, axis=1)])


def dir_sum(t):
    return t[0] + jnp.flip(t[1], axis=1)


def to_chunks(t, axis):
    n = t.shape[axis]
    t = t.reshape(t.shape[:axis] + (n // CHUNK, CHUNK) + t.shape[axis + 1:])
    return jnp.moveaxis(t, axis, 0)


def from_chunks(t):
    t = jnp.moveaxis(t, 0, -3)
    return t.reshape(t.shape[:-3] + (t.shape[-3] * t.shape[-2], t.shape[-1]))


def token_shift(t, mu):
    pad = jnp.pad(t, ((0, 0), (1, 1), (0, 0)))
    return t + mu * (0.5 * (pad[:, :-2] + pad[:, 2:]) - t)


def depthwise_conv(t, w):
    k = w.shape[0]
    return lax.conv_general_dilated(t, w.astype(t.dtype)[:, None, :], window_strides=(1,),
                                    padding=[(k // 2, k // 2)],
                                    dimension_numbers=('NWC', 'WIO', 'NWC'),
                                    feature_group_count=t.shape[-1])


def chunk_gla(q, k, v, log_f, s0):
    causal = jnp.tril(jnp.ones((CHUNK, CHUNK), bool))[:, :, None]

    def step(s, inp):
        qc, kc, vc, lfc = inp
        cum = jnp.cumsum(lfc, axis=-2)
        rel = cum[..., :, None, :] - cum[..., None, :, :]
        decay = jnp.exp(jnp.where(causal, rel, -jnp.inf))
        scores = jnp.einsum('...ik,...ijk,...jk->...ij', qc, decay, kc)
        o = (jnp.einsum('...ik,...kv->...iv', qc * jnp.exp(cum), s)
             + jnp.einsum('...ij,...jv->...iv', scores, vc))
        last = cum[..., -1:, :]
        s = (jnp.exp(last[..., 0, :])[..., :, None] * s
             + jnp.einsum('...jk,...jv->...kv', kc * jnp.exp(last - cum), vc))
        return s, o

    xs = tuple(to_chunks(t, t.ndim - 2) for t in (q, k, v, log_f))
    s, o = lax.scan(step, s0, xs)
    return from_chunks(o), s


def chunk_gated_delta(q, k, v, beta, log_a, s0):
    eye = jnp.eye(CHUNK, dtype=jnp.float32)
    incl = jnp.tril(jnp.ones((CHUNK, CHUNK), bool))
    strict = jnp.tril(jnp.ones((CHUNK, CHUNK), bool), -1)

    def step(s, inp):
        qc, kc, vc, bc, lac = inp
        cum = jnp.cumsum(lac, axis=-1)
        gam = jnp.exp(jnp.where(incl, cum[..., :, None] - cum[..., None, :], -jnp.inf))
        kb = kc * bc[..., None]
        m = eye + jnp.where(strict, jnp.einsum('...id,...jd->...ij', kb, kc) * gam, 0.0)
        rhs = jnp.concatenate([vc * bc[..., None], kb * jnp.exp(cum)[..., None]], axis=-1)
        sol = lax.linalg.triangular_solve(m, rhs, left_side=True, lower=True, unit_diagonal=True)
        u, w = sol[..., :HEAD_DIM], sol[..., HEAD_DIM:]
        v_new = u - jnp.einsum('...ik,...kv->...iv', w, s)
        attn = jnp.einsum('...id,...jd->...ij', qc, kc) * gam
        o = (jnp.einsum('...ik,...kv->...iv', qc * jnp.exp(cum)[..., None], s)
             + jnp.einsum('...ij,...jv->...iv', attn, v_new))
        last = cum[..., -1:]
        s = (jnp.exp(last)[..., None] * s
             + jnp.einsum('...jk,...jv->...kv', kc * jnp.exp(last - cum)[..., None], v_new))
        return s, o

    xs = (to_chunks(q, q.ndim - 2), to_chunks(k, k.ndim - 2), to_chunks(v, v.ndim - 2),
          to_chunks(beta, beta.ndim - 1), to_chunks(log_a, log_a.ndim - 1))
    s, o = lax.scan(step, s0, xs)
    return from_chunks(o), s


def chunk_mlstm(q, k, v, log_i, log_f, state):
    incl = jnp.tril(jnp.ones((CHUNK, CHUNK), bool))

    def step(carry, inp):
        cmat, nvec, m = carry
        qc, kc, vc, lic, lfc = inp
        b = jnp.cumsum(lfc, axis=-1)
        dlog = jnp.where(incl, b[..., :, None] - b[..., None, :] + lic[..., None, :], -jnp.inf)
        inter = b + m[..., None]
        m_i = jnp.maximum(inter, jnp.max(dlog, -1))
        dw = jnp.exp(dlog - m_i[..., None])
        iw = jnp.exp(inter - m_i)
        sc = jnp.einsum('...id,...jd->...ij', qc, kc) * dw
        num = (iw[..., None] * jnp.einsum('...ik,...kv->...iv', qc, cmat)
               + jnp.einsum('...ij,...jv->...iv', sc, vc))
        den = iw * jnp.einsum('...ik,...k->...i', qc, nvec) + jnp.sum(sc, -1)
        h = num / jnp.maximum(jnp.abs(den), jnp.exp(-m_i))[..., None]
        bl = b[..., -1]
        src = bl[..., None] - b + lic
        m_new = jnp.maximum(bl + m, jnp.max(src, -1))
        carry_w = jnp.exp(bl + m - m_new)
        wj = jnp.exp(src - m_new[..., None])
        cmat = carry_w[..., None, None] * cmat + jnp.einsum('...jk,...jv->...kv', kc * wj[..., None], vc)
        nvec = carry_w[..., None] * nvec + jnp.einsum('...jk,...j->...k', kc, wj)
        return (cmat, nvec, m_new), h

    xs = (to_chunks(q, q.ndim - 2), to_chunks(k, k.ndim - 2), to_chunks(v, v.ndim - 2),
          to_chunks(log_i, log_i.ndim - 1), to_chunks(log_f, log_f.ndim - 1))
    state, h = lax.scan(step, state, xs)
    return from_chunks(h), state


def rwkv7_scan(r, decay, k, v, kk, a, s0):
    def step(s, inp):
        rt, wt, kt, vt, kkt, at = inp
        sa = jnp.einsum('...vk,...k->...v', s, kkt)
        s = (s * wt[..., None, :] - sa[..., :, None] * (kkt * at)[..., None, :]
             + vt[..., :, None] * kt[..., None, :])
        return s, jnp.einsum('...vk,...k->...v', s, rt)

    xs = tuple(jnp.moveaxis(t, 2, 0) for t in (r, decay, k, v, kk, a))
    s, o = lax.scan(step, s0, xs)
    return jnp.moveaxis(o, 0, 2), s


def hgrn2_mixer(cols_ctx, cols_lat, gamma, layer, norm_g):
    lb_cum = jnp.cumsum(jax.nn.softmax(gamma.astype(jnp.float32), axis=0), axis=0)
    lb = (lb_cum[layer] - lb_cum[0])[:, None, None, :]

    def prep(cols):
        q, i, f, g = split_cols(cols.astype(jnp.float32), A_COLS)
        log_f = jnp.logaddexp(jnp.log(lb), jnp.log1p(-lb) + jax.nn.log_sigmoid(dir_own(f)))
        k = -jnp.expm1(log_f)
        return (heads(dir_shared(q)), heads(k), heads(dir_shared(i)), heads(log_f)), g

    def finish(o, g):
        return head_rms_norm(dir_sum(unheads(o)), norm_g) * jax.nn.silu(g)

    xs_ctx, g_ctx = prep(cols_ctx)
    s0 = jnp.zeros((2, cols_ctx.shape[0], GROUP_HEADS, HEAD_DIM, HEAD_DIM), jnp.float32)
    o_ctx, s_ctx = chunk_gla(*xs_ctx, s0)
    xs_lat, g_lat = prep(cols_lat)
    o_lat, _ = chunk_gla(*xs_lat, s_ctx)
    return finish(o_ctx, g_ctx), finish(o_lat, g_lat)


def rwkv7_mixer(cols_ctx, cols_lat, mu, w0, w2, a0, a2, g2, k_k, k_a, r_k, ln_g, ln_b):
    def prep(cols):
        cols = token_shift(cols.astype(jnp.float32), mu)
        r, k, v, wd, ad, gd = split_cols(cols, B_COLS)
        w = -jax.nn.softplus(-(w0[:, None, None, :] + jnp.einsum('dbnr,drc->dbnc', jnp.tanh(dir_own(wd)), w2))) - 0.5
        decay = jnp.exp(-jnp.exp(w))
        a = jax.nn.sigmoid(a0[:, None, None, :] + jnp.einsum('dbnr,drc->dbnc', dir_own(ad), a2))
        kk = l2_normalize(split_heads(k * k_k))
        k_dir = dir_shared(k) * (1 + (a - 1) * k_a)
        gate = jax.nn.sigmoid(gd) @ g2
        xs = (split_heads(dir_shared(r)), split_heads(decay), split_heads(k_dir),
              split_heads(dir_shared(v)), dir_shared(kk), split_heads(a))
        return xs, gate

    def finish(o, xs, gate):
        r2, _, k2, v2, _, _ = xs
        bonus = jnp.sum(r2 * k2 * r_k, -1, keepdims=True) * v2
        y = group_norm(dir_sum(o), ln_g, ln_b) + dir_sum(bonus).reshape(gate.shape)
        return y * gate

    xs_ctx, g_ctx = prep(cols_ctx)
    s0 = jnp.zeros((2, cols_ctx.shape[0], GROUP_HEADS, HEAD_DIM, HEAD_DIM), jnp.float32)
    o_ctx, s_ctx = rwkv7_scan(*xs_ctx, s0)
    xs_lat, g_lat = prep(cols_lat)
    o_lat, _ = rwkv7_scan(*xs_lat, s_ctx)
    return finish(o_ctx, xs_ctx, g_ctx), finish(o_lat, xs_lat, g_lat)


def gated_deltanet_mixer(cols_ctx, cols_lat, conv_w, a_log, dt_bias, norm_g):
    def prep(cols):
        qkv, beta, alpha, g = split_cols(cols.astype(jnp.float32), C_COLS)
        q, k, v = jnp.split(jax.nn.silu(depthwise_conv(qkv, conv_w)), 3, axis=-1)
        q = head_l2(q) * HEAD_DIM ** -0.5
        k = head_l2(k)
        beta = jax.nn.sigmoid(dir_own(beta))
        log_a = -jnp.exp(a_log)[:, None, None, :] * jax.nn.softplus(dir_own(alpha) + dt_bias[:, None, None, :])
        xs = (heads(dir_shared(q)), heads(dir_shared(k)), heads(dir_shared(v)),
              head_scalars(beta), head_scalars(log_a))
        return xs, g

    def finish(o, g):
        return head_rms_norm(dir_sum(unheads(o)), norm_g) * jax.nn.silu(g)

    xs_ctx, g_ctx = prep(cols_ctx)
    s0 = jnp.zeros((2, cols_ctx.shape[0], GROUP_HEADS, HEAD_DIM, HEAD_DIM), jnp.float32)
    o_ctx, s_ctx = chunk_gated_delta(*xs_ctx, s0)
    xs_lat, g_lat = prep(cols_lat)
    o_lat, _ = chunk_gated_delta(*xs_lat, s_ctx)
    return finish(o_ctx, g_ctx), finish(o_lat, g_lat)


def mlstm_mixer(cols_ctx, cols_lat, i_bias, f_bias, norm_g):
    def prep(cols):
        qkv, ig, fg, og = split_cols(cols.astype(jnp.float32), D_COLS)
        q, k, v = jnp.split(qkv, 3, axis=-1)
        k = k * HEAD_DIM ** -0.5
        log_i = dir_own(ig) + i_bias[:, None, None, :]
        log_f = jax.nn.log_sigmoid(dir_own(fg) + f_bias[:, None, None, :])
        xs = (heads(dir_shared(q)), heads(dir_shared(k)), heads(dir_shared(v)),
              head_scalars(log_i), head_scalars(log_f))
        return xs, og

    def finish(h, og):
        return head_rms_norm(dir_sum(unheads(h)), norm_g) * jax.nn.sigmoid(og)

    xs_ctx, o_gate_ctx = prep(cols_ctx)
    bsz = cols_ctx.shape[0]
    state0 = (jnp.zeros((2, bsz, GROUP_HEADS, HEAD_DIM, HEAD_DIM), jnp.float32),
              jnp.zeros((2, bsz, GROUP_HEADS, HEAD_DIM), jnp.float32),
              jnp.zeros((2, bsz, GROUP_HEADS), jnp.float32))
    h_ctx, state_ctx = chunk_mlstm(*xs_ctx, state0)
    xs_lat, o_gate_lat = prep(cols_lat)
    h_lat, _ = chunk_mlstm(*xs_lat, state_ctx)
    return finish(h_ctx, o_gate_ctx), finish(h_lat, o_gate_lat)


def setup_inputs(seed: int = 0) -> dict:
    key = jax.random.key(seed)
    keys = iter(jax.random.split(key, 40))

    def nrm(shape, scale):
        return scale * jax.random.normal(next(keys), shape, jnp.float32)

    def uni(shape, lo, hi):
        return jax.random.uniform(next(keys), shape, jnp.float32, lo, hi)

    D, G, H, L = D_MODEL, GROUP_W, GROUP_HEADS, DEPTH
    dt = jnp.exp(uni((L, 2, H), math.log(1e-3), math.log(1e-1)))
    return {
        "x": nrm((BATCH, SEQ, D), 1.0),
        "c": nrm((BATCH, D), 1.0),
        "ctx": nrm((BATCH, CTX_LEN, D), 1.0),
        "c_ctx": nrm((D,), 1.0),
        "ada_w": nrm((L, D, 6 * D), 0.5 * D ** -0.5),
        "ada_b": nrm((L, 6 * D), 0.02),
        "w_in": nrm((L, D, D_IN), D ** -0.5),
        "w_out": nrm((L, D_MIX, D), DEEPNORM_BETA * D_MIX ** -0.5),
        "ln1_g": 1.0 + nrm((L, D), 0.05),
        "ln1_b": nrm((L, D), 0.02),
        "ln2_g": 1.0 + nrm((L, D), 0.05),
        "ln2_b": nrm((L, D), 0.02),
        "mlp_w1": nrm((L, D, D_FF), D ** -0.5),
        "mlp_w2": nrm((L, D_FF, D), DEEPNORM_BETA * D_FF ** -0.5),
        "hgrn_gamma": nrm((L, 2, G), 1.0),
        "hgrn_norm_g": 1.0 + nrm((L, G), 0.05),
        "rwkv_mu": uni((L, GROUP_COLS[1]), 0.0, 1.0),
        "rwkv_w0": nrm((L, 2, G), 0.5),
        "rwkv_w2": nrm((L, 2, RWKV_DECAY_RANK, G), 0.5 * RWKV_DECAY_RANK ** -0.5),
        "rwkv_a0": nrm((L, 2, G), 0.5),
        "rwkv_a2": nrm((L, 2, RWKV_AAA_RANK, G), 0.5 * RWKV_AAA_RANK ** -0.5),
        "rwkv_g2": nrm((L, RWKV_GATE_RANK, G), RWKV_GATE_RANK ** -0.5),
        "rwkv_k_k": 1.0 + nrm((L, G), 0.05),
        "rwkv_k_a": 1.0 + nrm((L, G), 0.05),
        "rwkv_r_k": nrm((L, H, HEAD_DIM), 0.1),
        "rwkv_ln_g": 1.0 + nrm((L, G), 0.05),
        "rwkv_ln_b": nrm((L, G), 0.02),
        "gdn_conv": nrm((L, GDN_CONV_W, 3 * G), GDN_CONV_W ** -0.5),
        "gdn_a_log": jnp.log(uni((L, 2, H), 1.0, 16.0)),
        "gdn_dt_bias": dt + jnp.log(-jnp.expm1(-dt)),
        "gdn_norm_g": 1.0 + nrm((L, G), 0.05),
        "mlstm_i_bias": nrm((L, 2, H), 0.1),
        "mlstm_f_bias": 3.0 + nrm((L, 2, H), 0.5),
        "mlstm_norm_g": 1.0 + nrm((L, G), 0.05),
    }


def reference(x, c, ctx, c_ctx, ada_w, ada_b, w_in, w_out, ln1_g, ln1_b, ln2_g, ln2_b,
              mlp_w1, mlp_w2, hgrn_gamma, hgrn_norm_g, rwkv_mu, rwkv_w0, rwkv_w2, rwkv_a0,
              rwkv_a2, rwkv_g2, rwkv_k_k, rwkv_k_a, rwkv_r_k, rwkv_ln_g, rwkv_ln_b, gdn_conv,
              gdn_a_log, gdn_dt_bias, gdn_norm_g, mlstm_i_bias, mlstm_f_bias, mlstm_norm_g):
    for layer in range(DEPTH):
        last = layer == DEPTH - 1
        mod_lat = jnp.split((jax.nn.silu(c) @ ada_w[layer] + ada_b[layer])[:, None, :], 6, axis=-1)
        mod_ctx = jnp.split((jax.nn.silu(c_ctx) @ ada_w[layer] + ada_b[layer])[None, None, :], 6, axis=-1)
        h_lat = modulate(x, mod_lat[0], mod_lat[1])
        h_ctx = modulate(ctx, mod_ctx[0], mod_ctx[1])
        p_ctx = split_cols(h_ctx @ w_in[layer], GROUP_COLS)
        p_lat = split_cols(h_lat @ w_in[layer], GROUP_COLS)
        odd = layer % 2 == 1
        col_major = (odd, odd, not odd, not odd)
        p_lat = [to_col_major(t) if cm else t for t, cm in zip(p_lat, col_major)]
        outs = [
            hgrn2_mixer(p_ctx[0], p_lat[0], hgrn_gamma, layer, hgrn_norm_g[layer]),
            rwkv7_mixer(p_ctx[1], p_lat[1], rwkv_mu[layer], rwkv_w0[layer], rwkv_w2[layer],
                        rwkv_a0[layer], rwkv_a2[layer], rwkv_g2[layer], rwkv_k_k[layer],
                        rwkv_k_a[layer], rwkv_r_k[layer], rwkv_ln_g[layer], rwkv_ln_b[layer]),
            gated_deltanet_mixer(p_ctx[2], p_lat[2], gdn_conv[layer], gdn_a_log[layer],
                                 gdn_dt_bias[layer], gdn_norm_g[layer]),
            mlstm_mixer(p_ctx[3], p_lat[3], mlstm_i_bias[layer], mlstm_f_bias[layer],
                        mlstm_norm_g[layer]),
        ]
        y_lat = jnp.concatenate([from_col_major(o[1]) if cm else o[1] for o, cm in zip(outs, col_major)], axis=-1)
        y_lat = y_lat.astype(x.dtype) @ w_out[layer]
        x = layer_norm(DEEPNORM_ALPHA * x + mod_lat[2] * y_lat, ln1_g[layer], ln1_b[layer])
        y_mlp = squared_relu_mlp(modulate(x, mod_lat[3], mod_lat[4]), mlp_w1[layer], mlp_w2[layer])
        x = layer_norm(DEEPNORM_ALPHA * x + mod_lat[5] * y_mlp, ln2_g[layer], ln2_b[layer])
        if not last:
            y_ctx = jnp.concatenate([o[0] for o in outs], axis=-1).astype(ctx.dtype) @ w_out[layer]
            ctx = layer_norm(DEEPNORM_ALPHA * ctx + mod_ctx[2] * y_ctx, ln1_g[layer], ln1_b[layer])
            y_mlp_ctx = squared_relu_mlp(modulate(ctx, mod_ctx[3], mod_ctx[4]), mlp_w1[layer], mlp_w2[layer])
            ctx = layer_norm(DEEPNORM_ALPHA * ctx + mod_ctx[5] * y_mlp_ctx, ln2_g[layer], ln2_b[layer])
    return x
```

```python
from contextlib import ExitStack
import math
import numpy as np
import concourse.bass as bass
import concourse.mybir as mybir
from concourse.bass_utils import run_bass_kernel_spmd

F32 = mybir.dt.float32
BF16 = mybir.dt.bfloat16
AF = mybir.ActivationFunctionType
ALU = mybir.AluOpType
AX = mybir.AxisListType


class Tl:
    def __init__(self, fw, t, kind):
        self.fw = fw
        self.t = t
        self.kind = kind
        self.w = None
        self.r = {}
        self.dsem = None
        self.dcnt = 0
        self.dw = 0
        self.dr = 0
        self.dram_w = {}
        self.dram_r = {}
        self.name = None
        self.dent = None
        self.dkey = None
        self.dbase = 0
        self.is_output = False

    def ap(self):
        return self.t.ap() if self.kind == 'dr' else self.t[:]

    def __getitem__(self, k):
        a = self.t.ap()[k] if self.kind == 'dr' else self.t[k]
        return V(self, a)


class V:
    def __init__(self, tl, ap):
        self.tl = tl
        self.ap = ap

    def __getitem__(self, k):
        return V(self.tl, self.ap[k])

    def re(self, s, **kw):
        return V(self.tl, self.ap.rearrange(s, **kw))

    def bc(self, shape):
        return V(self.tl, self.ap.to_broadcast(shape))


def _v(x):
    if isinstance(x, Tl):
        return V(x, x.ap())
    return x


class Eng:
    def __init__(self, name, e, sem):
        self.name = name
        self.e = e
        self.sem = sem
        self.cnt = 0
        self.seen = {}
        self.seen_d = {}


class FW:
    def __init__(self, nc, stack):
        self.nc = nc
        self.stack = stack
        self.E = {}
        for name, e in (('pe', nc.tensor), ('act', nc.scalar), ('dve', nc.vector),
                        ('pool', nc.gpsimd), ('sp', nc.sync)):
            sem = stack.enter_context(nc.semaphore('s_' + name))
            self.E[name] = Eng(name, e, sem)
        self.ntile = 0
        self.out_dmas = []
        self.dactive = {}
        self.dfree = []
        self.dall = []
        self.dwaited = {}

    def _attach_dsem(self, tl):
        nm = tl.name
        if nm not in self.dactive:
            if self.dfree:
                ent = self.dfree.pop()
            else:
                sem = self.stack.enter_context(self.nc.semaphore('d_%d' % len(self.dall)))
                ent = [sem, 0, len(self.dall)]
                self.dall.append(ent)
            self.dactive[nm] = ent
        ent = self.dactive[nm]
        tl.dent = ent
        tl.dsem = ent[0]
        tl.dcnt = ent[1]
        tl.dbase = ent[1]
        tl.dkey = ent[2]

    def barrier(self):
        sp = self.E['sp']
        for n, e in self.E.items():
            if n != 'sp' and e.cnt and sp.seen.get(n, 0) < e.cnt:
                sp.e.wait_ge(e.sem, e.cnt)
                sp.seen[n] = e.cnt
        for ent in self.dall:
            sem, cnt, key = ent
            if cnt and self.dwaited.get(key, 0) < cnt:
                sp.e.wait_ge(sem, 16 * cnt)
                self.dwaited[key] = cnt
        sp.e.sem_inc(sp.sem, 1)
        sp.cnt += 1
        for n, e in self.E.items():
            if n != 'sp':
                e.e.wait_ge(sp.sem, sp.cnt)
                e.seen['sp'] = sp.cnt
                for m, o in self.E.items():
                    if m != 'sp':
                        e.seen[m] = max(e.seen.get(m, 0), o.cnt)
        for nm, ent in self.dactive.items():
            self.dfree.append(ent)
        self.dactive = {}

    def sb(self, shape, dt=F32, name=None, stack=None):
        self.ntile += 1
        name = (name or 't') + f'_{self.ntile}'
        t = (stack or self.stack).enter_context(self.nc.sbuf_tensor(name, list(shape), dt))
        tl = Tl(self, t, 'sb')
        tl.name = name
        return tl

    def ps(self, shape, dt=F32, name=None, stack=None):
        self.ntile += 1
        name = (name or 'p') + f'_{self.ntile}'
        t = (stack or self.stack).enter_context(self.nc.psum_tensor(name, list(shape), dt))
        tl = Tl(self, t, 'ps')
        tl.name = name
        return tl

    def dram(self, name, shape, dt=F32, kind='Internal'):
        t = self.nc.dram_tensor(name, list(shape), dt, kind=kind)
        tl = Tl(self, t, 'dr')
        tl.is_output = (kind == 'ExternalOutput')
        return tl

    def _need(self, eng, reads, writes):
        waits = {}
        dwaits = {}

        def add(w):
            if w is None:
                return
            n, c = w
            if n == eng.name and n == 'pe':
                return
            if waits.get(n, 0) < c:
                waits[n] = c

        for v in reads:
            tl = v.tl
            add(tl.w)
            if tl.dw:
                dwaits[tl] = max(dwaits.get(tl, 0), tl.dw)
        for v in writes:
            tl = v.tl
            add(tl.w)
            for n, c in tl.r.items():
                add((n, c))
            if tl.dsem is not None and tl.dcnt > tl.dbase:
                dwaits[tl] = max(dwaits.get(tl, 0), tl.dcnt)
        return waits, dwaits

    def _emit_waits(self, eng, waits, dwaits):
        for n, c in waits.items():
            if eng.seen.get(n, 0) >= c:
                continue
            eng.e.wait_ge(self.E[n].sem, c)
            eng.seen[n] = c
        for tl, c in dwaits.items():
            k = tl.dkey
            if eng.seen_d.get(k, 0) >= c:
                continue
            eng.e.wait_ge(tl.dsem, 16 * c)
            eng.seen_d[k] = c

    def op(self, engname, fn, reads, writes, inc=True):
        eng = self.E[engname]
        reads = [_v(x) for x in reads if x is not None and not isinstance(x, (int, float))]
        writes = [_v(x) for x in writes]
        waits, dwaits = self._need(eng, reads, writes)
        self._emit_waits(eng, waits, dwaits)
        ins = fn()
        c = eng.cnt + 1
        if inc:
            ins.then_inc(eng.sem, 1)
            eng.cnt = c
        for v in writes:
            v.tl.w = (engname, c)
            v.tl.r = {}
        for v in reads:
            if v.tl.r.get(engname, 0) < c:
                v.tl.r[engname] = c
        return ins

    def dma(self, qname, out, in_):
        q = self.E[qname]
        out = _v(out)
        in_ = _v(in_)
        o_tl, i_tl = out.tl, in_.tl
        sb_tl = o_tl if o_tl.kind != 'dr' else i_tl
        assert sb_tl.kind != 'dr'
        if sb_tl.dsem is None or self.dactive.get(sb_tl.name) is not sb_tl.dent:
            self._attach_dsem(sb_tl)
            sb_tl.dw = 0
            sb_tl.dr = 0
        waits = {}
        dwaits = {}
        semwaits = {}

        def add(w):
            if w is None:
                return
            n, c = w
            if waits.get(n, 0) < c:
                waits[n] = c

        if o_tl.kind != 'dr':
            add(o_tl.w)
            for n, c in o_tl.r.items():
                add((n, c))
            if o_tl.dr:
                dwaits[o_tl] = o_tl.dr
        else:
            for k, (s, val) in list(o_tl.dram_r.items()) + list(o_tl.dram_w.items()):
                if semwaits.get(k, (None, 0))[1] < val:
                    semwaits[k] = (s, val)
        if i_tl.kind != 'dr':
            add(i_tl.w)
            if i_tl.dw:
                dwaits[i_tl] = max(dwaits.get(i_tl, 0), i_tl.dw)
        else:
            for k, (s, val) in i_tl.dram_w.items():
                if semwaits.get(k, (None, 0))[1] < val:
                    semwaits[k] = (s, val)
        self._emit_waits(q, waits, dwaits)
        for k, (s, val) in semwaits.items():
            if q.seen_d.get(k, 0) >= val:
                continue
            q.e.wait_ge(s, 16 * val)
            q.seen_d[k] = val
        ins = q.e.dma_start(out=out.ap, in_=in_.ap)
        sb_tl.dcnt += 1
        sb_tl.dent[1] = sb_tl.dcnt
        ins.then_inc(sb_tl.dsem, 16)
        key = sb_tl.dkey
        if o_tl.kind != 'dr':
            o_tl.dw = o_tl.dcnt
            o_tl.w = None
            o_tl.r = {}
            if i_tl.kind == 'dr':
                i_tl.dram_r[key] = (sb_tl.dsem, sb_tl.dcnt)
        else:
            i_tl.dr = i_tl.dcnt
            o_tl.dram_w[key] = (sb_tl.dsem, sb_tl.dcnt)
            o_tl.dram_r = {}
            if o_tl.is_output:
                self.out_dmas.append((sb_tl.dsem, sb_tl.dcnt))
        return ins

    def finish(self):
        self.barrier()
        sp = self.E['sp']
        for n, e in self.E.items():
            if n != 'sp' and e.cnt:
                if sp.seen.get(n, 0) < e.cnt:
                    sp.e.wait_ge(e.sem, e.cnt)
        for s, val in self.out_dmas:
            sp.e.wait_ge(s, 16 * val)
        for n in ('pool',):
            for s, val in self.out_dmas:
                self.E[n].e.wait_ge(s, 16 * val)

    def mm(self, out, lhsT, rhs, start=True, stop=True, inc=True):
        out, lhsT, rhs = _v(out), _v(lhsT), _v(rhs)
        return self.op('pe', lambda: self.nc.tensor.matmul(out.ap, lhsT.ap, rhs.ap, start=start, stop=stop),
                       [lhsT, rhs], [out], inc=inc)

    def tr(self, out, in_, ident):
        out, in_, ident = _v(out), _v(in_), _v(ident)
        return self.op('pe', lambda: self.nc.tensor.transpose(out.ap, in_.ap, ident.ap), [in_, ident], [out])

    def act(self, out, in_, func, bias=None, scale=1.0, accum=None):
        out, in_ = _v(out), _v(in_)
        rd = [in_]
        kw = {}
        if bias is not None:
            if isinstance(bias, (int, float)):
                kw['bias'] = float(bias)
            else:
                bias = _v(bias)
                rd.append(bias)
                kw['bias'] = bias.ap
        if isinstance(scale, (int, float)):
            kw['scale'] = float(scale)
        else:
            scale = _v(scale)
            rd.append(scale)
            kw['scale'] = scale.ap
        wr = [out]
        if accum is not None:
            accum = _v(accum)
            wr.append(accum)
            kw['accum_out'] = accum.ap
        return self.op('act', lambda: self.nc.scalar.activation(out=out.ap, in_=in_.ap, func=func, **kw), rd, wr)

    def _veng(self, e):
        return self.nc.vector if e == 'dve' else self.nc.gpsimd

    def tt(self, out, a, b, op, e='dve'):
        out, a, b = _v(out), _v(a), _v(b)
        return self.op(e, lambda: self._veng(e).tensor_tensor(out=out.ap, in0=a.ap, in1=b.ap, op=op), [a, b], [out])

    def ts(self, out, a, s1, op0, s2=None, op1=None, e='dve', accum=None):
        out, a = _v(out), _v(a)
        rd = [a]

        def cv(s):
            if s is None or isinstance(s, (int, float)):
                return None if s is None else float(s)
            s = _v(s)
            rd.append(s)
            return s.ap
        s1a = cv(s1)
        s2a = cv(s2)
        kw = {}
        wr = [out]
        if op1 is not None:
            kw['op1'] = op1
        if accum is not None:
            accum = _v(accum)
            wr.append(accum)
            kw['accum_out'] = accum.ap
        return self.op(e, lambda: self._veng(e).tensor_scalar(out=out.ap, in0=a.ap, scalar1=s1a, scalar2=s2a, op0=op0, **kw), rd, wr)

    def stt(self, out, a, s, b, op0, op1, accum=None):
        out, a, b = _v(out), _v(a), _v(b)
        rd = [a, b]
        if isinstance(s, (int, float)):
            sa = float(s)
        else:
            s = _v(s)
            rd.append(s)
            sa = s.ap
        kw = {}
        wr = [out]
        if accum is not None:
            accum = _v(accum)
            wr.append(accum)
            kw['accum_out'] = accum.ap
        return self.op('dve', lambda: self.nc.vector.scalar_tensor_tensor(out=out.ap, in0=a.ap, scalar=sa, in1=b.ap, op0=op0, op1=op1, **kw), rd, wr)

    def copy(self, out, in_, e='dve'):
        out, in_ = _v(out), _v(in_)
        if e == 'act':
            return self.act(out, in_, AF.Copy)
        return self.op(e, lambda: self._veng(e).tensor_copy(out=out.ap, in_=in_.ap), [in_], [out])

    def memset(self, out, val, e='dve'):
        out = _v(out)
        return self.op(e, lambda: self._veng(e).memset(out.ap, val), [], [out])

    def reduce(self, out, in_, op, axis=AX.X, e='dve'):
        out, in_ = _v(out), _v(in_)
        return self.op(e, lambda: self._veng(e).tensor_reduce(out=out.ap, in_=in_.ap, axis=axis, op=op), [in_], [out])

    def recip(self, out, in_):
        out, in_ = _v(out), _v(in_)
        return self.op('dve', lambda: self.nc.vector.reciprocal(out=out.ap, in_=in_.ap), [in_], [out])

    def scan(self, out, d0, d1, init, op0, op1):
        out, d0, d1 = _v(out), _v(d0), _v(d1)
        rd = [d0, d1]
        if isinstance(init, (int, float)):
            ia = float(init)
        else:
            init = _v(init)
            rd.append(init)
            ia = init.ap
        return self.op('dve', lambda: self.nc.vector.tensor_tensor_scan(out=out.ap, data0=d0.ap, data1=d1.ap, initial=ia, op0=op0, op1=op1), rd, [out])

from contextlib import ExitStack
import math

T = 4352
NCTX = 256
NLAT = 4096
D = 1024
DFF = 4096
ALPHA = 8.0 ** 0.25
LN_EPS = 1e-5

_A0, _B0, _C0, _D0 = 0, 1280, 2432, 3472


def _r(a, b):
    return list(range(a, b))


FM_BLOCKS = [
    ('A', 'q', _r(_A0 + 0, _A0 + 256)),
    ('A', 'f', _r(_A0 + 512, _A0 + 1024)),
    ('B', 'all', _r(_B0, _B0 + 1152)),
    ('D', 'qk', _r(_D0, _D0 + 512)),
    ('D', 'gates', _r(_D0 + 768, _D0 + 784)),
]
TM_BLOCKS = [
    ('A', 'ig', _r(_A0 + 256, _A0 + 512) + _r(_A0 + 1024, _A0 + 1280)),
    ('B', 'v', _r(_B0 + 512, _B0 + 768)),
    ('C', 'qkv0', _r(_C0, _C0 + 512)),
    ('C', 'qkv1', _r(_C0 + 512, _C0 + 768) + _r(_C0 + 784, _C0 + 1040)),
    ('C', 'ba', _r(_C0 + 768, _C0 + 784)),
    ('D', 'vo', _r(_D0 + 512, _D0 + 768) + _r(_D0 + 784, _D0 + 1040)),
    ('D', 'gates', _r(_D0 + 768, _D0 + 784)),
]
FM_COLS = sum((b[2] for b in FM_BLOCKS), [])
TM_COLS = sum((b[2] for b in TM_BLOCKS), [])
NFM = len(FM_COLS)
NTM = len(TM_COLS)


def fm_layout():
    out = []
    off = 0
    for g, nm, cols in FM_BLOCKS:
        out.append((g, f'fm_{g}_{nm}', off, len(cols)))
        off += len(cols)
    return out


def tm_layout():
    out = []
    off = 0
    for g, nm, cols in TM_BLOCKS:
        out.append((g, f'tm_{g}_{nm}', off, len(cols)))
        off += len(cols)
    return out


def scan_rows(order, s0, n):
    if order == 'r':
        return [(0, n, NCTX + s0, 1)]
    segs = []
    assert s0 % 64 == 0 and n % 64 == 0
    for i in range(n // 64):
        c = s0 // 64 + i
        segs.append((i * 64, 64, NCTX + c, 64))
    return segs


def rows_ap(fwk, dr, row_start, row_step, nrows, c0, c1):
    a = dr.ap()
    ncols_total = a.shape[1]
    ap = bass.AP(a.tensor, a.offset + row_start * ncols_total + c0, [[row_step * ncols_total, nrows], [1, c1 - c0]])
    return V(dr, ap)


def bcast_last(v, n):
    a = v.ap
    return V(v.tl, bass.AP(a.tensor, a.offset, [list(x) for x in a.ap] + [[0, n]]))


def dram_bcast_rows(dr, off, n):
    a = dr.ap()
    return V(dr, bass.AP(a.tensor, a.offset + off, [[0, 128], [1, n]]))


def layer_norm(fw, t, out, gb, bb, sc):
    fw.act(out, t, AF.Identity, accum=sc['s1'])
    fw.act(out, t, AF.Square, accum=sc['s2'])
    fw.ts(sc['mean'], sc['s1'], 1.0 / 1024, ALU.mult)
    fw.tt(sc['m2'], sc['mean'], sc['mean'], ALU.mult)
    fw.stt(sc['var'], sc['s2'], 1.0 / 1024, sc['m2'], ALU.mult, ALU.subtract)
    fw.ts(sc['var'], sc['var'], LN_EPS, ALU.add)
    fw.act(sc['std'], sc['var'], AF.Sqrt)
    fw.recip(sc['rstd'], sc['std'])
    fw.stt(sc['nmr'], sc['mean'], -1.0, sc['rstd'], ALU.mult, ALU.mult)
    fw.act(out, t, AF.Identity, bias=sc['nmr'], scale=sc['rstd'])
    fw.tt(out, out, gb, ALU.mult, e='pool')
    fw.tt(out, out, bb, ALU.add)


def ln_scratch(fw, st, pfx):
    sc = {k: fw.sb([128, 1], name=f'{pfx}_{k}', stack=st) for k in ('s1', 's2', 'mean', 'm2', 'var', 'std', 'rstd', 'nmr')}
    return sc


def load_cast_weight(fw, st, wbf3, wdram, K, ncols, pfx, chunk=256):
    with ExitStack() as st2:
        stg = [fw.sb([128, K * chunk], name=f'{pfx}_stg{i}', stack=st2) for i in range(2)]
        engs = ['pool', 'dve', 'act']
        i = 0
        for c0 in range(0, ncols, chunk):
            w = min(chunk, ncols - c0)
            s = stg[i % 2]
            s3 = s[:, 0:K * w].re('p (k c) -> p k c', k=K)
            fw.dma('sp', s3, wdram[:, c0:c0 + w].re('(k p) c -> p k c', p=128))
            fw.copy(wbf3[:, :, c0:c0 + w], s3, e=engs[i % 3])
            i += 1
        fw.barrier()


def phase0(fw, L, lst):
    nc = fw.nc
    psb = fw.psb
    modc = fw.sb([128, 64], name='modc', stack=lst)
    modc4 = modc[:, :].re('p (i f w) -> p i f w', i=4, f=8)
    gbc = [[fw.sb([128, 1024], name=f'gbc{w}{g}', stack=lst) for g in range(2)] for w in range(2)]
    with ExitStack() as ps:
        c2 = fw.sb([128, 16], name='c2', stack=ps)
        sc = fw.sb([128, 16], name='sc', stack=ps)
        sc3 = sc[:, :].re('p (k w) -> p k w', w=2)
        scb = fw.sb([128, 16 * 128], name='scb', stack=ps)
        scb4 = scb[:, :].re('p (k w m) -> p k w m', k=8, w=2)
        adab = fw.sb([128, 48], name='adab', stack=ps)
        adabb = fw.sb([128, 2048], name='adabb', stack=ps)
        wst = [fw.sb([128, 8192], name=f'adaw{i}', stack=ps) for i in range(2)]
        fw.dma('sp', c2, L['c2T'])
        fw.dma('sp', adab, L['adab_col'])
        fw.dma('sp', adabb[:, 0:1024], dram_bcast_rows(L['ada_b'], 2048, 1024))
        fw.dma('sp', adabb[:, 1024:2048], dram_bcast_rows(L['ada_b'], 5120, 1024))
        fw.act(sc, c2, AF.Silu)
        for k in range(8):
            for w in range(2):
                fw.copy(scb4[:, k, w, :], sc3[:, k, w:w + 1].bc([128, 128]))
        for blk in range(6):
            ws = wst[blk % 2]
            ws3 = ws[:, :].re('p (k c) -> p k c', k=8)
            fw.dma('sp', ws3, L['ada_w'][:, blk * 1024:(blk + 1) * 1024].re('(k p) c -> p k c', p=128))
            if blk in (0, 1, 3, 4):
                idx = {0: 0, 1: 1, 3: 2, 4: 3}[blk]
                p = psb[0]
                for fc in range(8):
                    for k in range(8):
                        fw.mm(p[:, fc * 2:fc * 2 + 2], ws3[:, k, fc * 128:(fc + 1) * 128], sc3[:, k, :],
                              start=(k == 0), stop=(k == 7), inc=(k == 7))
                pv = p[:, 0:16].re('p (f w) -> p f w', w=2)
                fw.tt(modc4[:, idx], pv, bcast_last(adab[:, blk * 8:(blk + 1) * 8], 2), ALU.add)
                if blk in (1, 4):
                    fw.ts(modc4[:, idx], modc4[:, idx], 1.0, ALU.add)
            else:
                g = 0 if blk == 2 else 1
                for w in range(2):
                    for half in range(2):
                        p = psb[1 + (w * 2 + half) % 2]
                        for k in range(8):
                            fw.mm(p[:, :], scb4[:, k, w, :], ws3[:, k, half * 512:(half + 1) * 512],
                                  start=(k == 0), stop=(k == 7), inc=(k == 7))
                        fw.tt(gbc[w][g][:, half * 512:(half + 1) * 512], p[:, :],
                              adabb[:, g * 1024 + half * 512: g * 1024 + (half + 1) * 512], ALU.add)
        fw.barrier()
    return {'modc4': modc4, 'gbc': gbc}


def transpose_mod(fw, xt3, nsub, hT3, modc4, idx, w, ident, cnt0=0):
    psb = fw.psb
    n = nsub * 128
    for k in range(8):
        p = psb[k % 2]
        for j in range(nsub):
            fw.tr(p[:, j * 128:(j + 1) * 128], xt3[:, j, k * 128:(k + 1) * 128], ident)
        if idx is None:
            fw.copy(hT3[:, k, 0:n], p[:, 0:n], e=('act' if k % 2 else 'dve'))
        elif k % 2 == 0:
            fw.ts(hT3[:, k, 0:n], p[:, 0:n], modc4[:, idx + 1, k, w:w + 1], ALU.mult,
                  s2=modc4[:, idx, k, w:w + 1], op1=ALU.add)
        else:
            fw.act(hT3[:, k, 0:n], p[:, 0:n], AF.Identity, bias=modc4[:, idx, k, w:w + 1],
                   scale=modc4[:, idx + 1, k, w:w + 1])


def phase1(fw, L, P0, S, cm_groups, x_in):
    nc = fw.nc
    psb = fw.psb
    with ExitStack() as st:
        wfm = fw.sb([128, 8 * NFM], BF16, name='wfm', stack=st)
        wfm3 = wfm[:, :].re('p (k c) -> p k c', k=8)
        wtm = fw.sb([128, 8 * NTM], BF16, name='wtm', stack=st)
        wtm3 = wtm[:, :].re('p (k c) -> p k c', k=8)
        load_cast_weight(fw, st, wfm3, L['w_fm'], 8, NFM, 'wfm')
        load_cast_weight(fw, st, wtm3, L['w_tm'], 8, NTM, 'wtm')
        xb = [fw.sb([128, 4096], name=f'p1x{i}', stack=st) for i in range(2)]
        hb = [fw.sb([128, 8 * 512], BF16, name=f'p1h{i}', stack=st) for i in range(2)]
        stg = [fw.sb([128, 512], name=f'p1s{i}', stack=st) for i in range(4)]
        ident = fw.ident
        tiles = [('r', 0, 0, 256, set('ABCD'))]
        rg = set('ABCD') - set(cm_groups)
        for m in range(8):
            if rg:
                tiles.append(('r', 1, m * 512, 512, rg))
        for m in range(8):
            if cm_groups:
                tiles.append(('c', 1, m * 512, 512, set(cm_groups)))
        ti = 0
        rot = 0
        srot = 0
        for order, w, s0, n, groups in tiles:
            nsub = n // 128
            xt = xb[ti % 2]
            xt3 = xt[:, :].re('p (j c) -> p j c', c=1024)
            hT3 = hb[ti % 2][:, :].re('p (k t) -> p k t', k=8)
            ti += 1
            for j in range(nsub):
                if w == 0:
                    segs = [(0, 128, j * 128, 1)]
                else:
                    segs = scan_rows(order, s0 + j * 128, 128)
                for poff, nr, r0, rstep in segs:
                    fw.dma('sp', xt3[poff:poff + nr, j, :], rows_ap(fw, x_in, r0, rstep, nr, 0, 1024))
            transpose_mod(fw, xt3, nsub, hT3, P0['modc4'], 0, w, ident)
            tok0 = s0 if w == 0 else NCTX + s0
            for g, nm, off, ncols in fm_layout():
                if g not in groups:
                    continue
                for c0 in range(0, ncols, 128):
                    cw = min(128, ncols - c0)
                    p = psb[2 + rot % 6]
                    rot += 1
                    for k in range(8):
                        fw.mm(p[0:cw, 0:n], wfm3[:, k, off + c0: off + c0 + cw], hT3[:, k, 0:n],
                              start=(k == 0), stop=(k == 7), inc=(k == 7))
                    sg = stg[srot % 4]
                    srot += 1
                    fw.copy(sg[0:cw, 0:n], p[0:cw, 0:n], e=('act' if srot % 2 else 'dve'))
                    fw.dma('pool', S[nm][c0:c0 + cw, tok0:tok0 + n], sg[0:cw, 0:n])
            for j in range(nsub):
                for g, nm, off, ncols in tm_layout():
                    if g not in groups:
                        continue
                    p = psb[2 + rot % 6]
                    rot += 1
                    for k in range(8):
                        fw.mm(p[:, 0:ncols], hT3[:, k, j * 128:(j + 1) * 128], wtm3[:, k, off:off + ncols],
                              start=(k == 0), stop=(k == 7), inc=(k == 7))
                    sg = stg[srot % 4]
                    srot += 1
                    fw.copy(sg[:, 0:ncols], p[:, 0:ncols], e=('act' if srot % 2 else 'dve'))
                    fw.dma('pool', S[nm][tok0 + j * 128: tok0 + (j + 1) * 128, 0:ncols], sg[:, 0:ncols])
        fw.barrier()


def phase3(fw, L, P0, Y, x_in, X1, x_out, last, out_lat=None):
    psb = fw.psb
    ident = fw.ident
    modc4, gbc = P0['modc4'], P0['gbc']
    t0 = 2 if last else 0
    with ExitStack() as st:
        wo = fw.sb([128, 8 * 1024], BF16, name='wo', stack=st)
        wo3 = wo[:, :].re('p (k c) -> p k c', k=8)
        load_cast_weight(fw, st, wo3, L['w_out'], 8, 1024, 'wo')
        yb = [fw.sb([128, 1024], name=f'p3y{i}', stack=st) for i in range(2)]
        xb = [fw.sb([128, 1024], name=f'p3x{i}', stack=st) for i in range(2)]
        yTb = [fw.sb([128, 8 * 128], BF16, name=f'p3yT{i}', stack=st) for i in range(2)]
        tb = [fw.sb([128, 1024], name=f'p3t{i}', stack=st) for i in range(2)]
        ob = [fw.sb([128, 1024], name=f'p3o{i}', stack=st) for i in range(2)]
        lnbc = fw.sb([128, 2048], name='lnbc1', stack=st)
        fw.dma('sp', lnbc, dram_bcast_rows(L['ln'], 0, 2048))
        sc = ln_scratch(fw, st, 'ln1')
        for ti in range(t0, T // 128):
            w = 0 if ti < 2 else 1
            yt, xt, yT, tt_, ot = yb[ti % 2], xb[ti % 2], yTb[ti % 2], tb[ti % 2], ob[ti % 2]
            fw.dma('sp', yt, Y[ti * 128:(ti + 1) * 128, :])
            fw.dma('sp', xt, x_in[ti * 128:(ti + 1) * 128, :])
            yt3 = yt[:, :].re('p (j c) -> p j c', j=1)
            yT3 = yT[:, :].re('p (k t) -> p k t', k=8)
            for hh in range(2):
                p = psb[hh]
                for q in range(4):
                    k = hh * 4 + q
                    fw.tr(p[:, q * 128:(q + 1) * 128], yt[:, k * 128:(k + 1) * 128], ident)
                fw.copy(yT3[:, hh * 4:(hh + 1) * 4, :], p[:, :].re('p (q t) -> p q t', q=4), e=('act' if hh else 'dve'))
            for half in range(2):
                p = psb[2 + half]
                for k in range(8):
                    fw.mm(p[:, :], yT3[:, k, :], wo3[:, k, half * 512:(half + 1) * 512],
                          start=(k == 0), stop=(k == 7), inc=(k == 7))
                fw.tt(tt_[:, half * 512:(half + 1) * 512], p[:, :], gbc[w][0][:, half * 512:(half + 1) * 512], ALU.mult)
                fw.stt(tt_[:, half * 512:(half + 1) * 512], xt[:, half * 512:(half + 1) * 512], ALPHA,
                       tt_[:, half * 512:(half + 1) * 512], ALU.mult, ALU.add)
            layer_norm(fw, tt_, ot, lnbc[:, 0:1024], lnbc[:, 1024:2048], sc)
            fw.dma('pool', X1[ti * 128:(ti + 1) * 128, :], ot)
        fw.barrier()
    with ExitStack() as st:
        w1 = fw.sb([128, 8 * 4096], BF16, name='w1', stack=st)
        w13 = w1[:, :].re('p (k c) -> p k c', k=8)
        w2 = fw.sb([128, 32 * 1024], BF16, name='w2', stack=st)
        w23 = w2[:, :].re('p (k c) -> p k c', k=32)
        load_cast_weight(fw, st, w13, L['mlp_w1'], 8, 4096, 'w1')
        load_cast_weight(fw, st, w23, L['mlp_w2'], 32, 1024, 'w2', chunk=64)
        xb = [fw.sb([128, 2048], name=f'p4x{i}', stack=st) for i in range(2)]
        hT = fw.sb([128, 8 * 256], BF16, name='p4hT', stack=st)
        hT3 = hT[:, :].re('p (k t) -> p k t', k=8)
        h1 = fw.sb([128, 32 * 256], BF16, name='p4h1', stack=st)
        h13 = h1[:, :].re('p (k t) -> p k t', k=32)
        sq = [fw.sb([128, 256], name=f'p4sq{i}', stack=st) for i in range(2)]
        tb = [fw.sb([128, 1024], name=f'p4t{i}', stack=st) for i in range(1)]
        ob = [fw.sb([128, 1024], name=f'p4o{i}', stack=st) for i in range(2)]
        lnbc = fw.sb([128, 2048], name='lnbc2', stack=st)
        fw.dma('sp', lnbc, dram_bcast_rows(L['ln'], 2048, 2048))
        sc = ln_scratch(fw, st, 'ln2')
        oi = 0
        for mi in range(t0 // 2, T // 256):
            w = 0 if mi < 1 else 1
            xt = xb[mi % 2]
            xt3 = xt[:, :].re('p (j c) -> p j c', j=2)
            fw.dma('sp', xt3, X1[mi * 256:(mi + 1) * 256, :].re('(j p) c -> p j c', p=128))
            transpose_mod(fw, xt3, 2, hT3, modc4, 2, w, ident)
            for fc in range(32):
                p = psb[2 + fc % 3]
                for k in range(8):
                    fw.mm(p[:, 0:256], w13[:, k, fc * 128:(fc + 1) * 128], hT3[:, k, :],
                          start=(k == 0), stop=(k == 7), inc=(k == 7))
                s = sq[fc % 2]
                fw.act(s, p[:, 0:256], AF.Square)
                fw.stt(h13[:, fc, :], p[:, 0:256], 0.0, s, ALU.is_gt, ALU.mult)
            for j in range(2):
                tt_, ot = tb[0], ob[oi % 2]
                oi += 1
                for half in range(2):
                    p = psb[5 + half]
                    for kk in range(32):
                        fw.mm(p[:, :], h13[:, kk, j * 128:(j + 1) * 128], w23[:, kk, half * 512:(half + 1) * 512],
                              start=(kk == 0), stop=(kk == 31), inc=(kk == 31))
                    fw.tt(tt_[:, half * 512:(half + 1) * 512], p[:, :], gbc[w][1][:, half * 512:(half + 1) * 512], ALU.mult)
                    fw.stt(tt_[:, half * 512:(half + 1) * 512], xt3[:, j, half * 512:(half + 1) * 512], ALPHA,
                           tt_[:, half * 512:(half + 1) * 512], ALU.mult, ALU.add)
                layer_norm(fw, tt_, ot, lnbc[:, 0:1024], lnbc[:, 1024:2048], sc)
                r0 = mi * 256 + j * 128
                if last:
                    fw.dma('pool', out_lat[r0 - NCTX: r0 - NCTX + 128, :], ot)
                else:
                    fw.dma('pool', x_out[r0:r0 + 128, :], ot)
        fw.barrier()


NORM_EPS = 1e-6
GS = 516


def fv(tl, off, dims, p0=0, np_=128):
    a = tl.t[p0:p0 + np_, :]
    return V(tl, bass.AP(a.tensor, a.offset + off, [list(a.ap[0])] + [list(d) for d in dims]))


def build_consts():
    c = {}
    off = 0
    arrs = []

    def add(name, arr):
        nonlocal off
        a = np.zeros((128, arr.shape[1]), np.float32)
        a[:arr.shape[0]] = arr
        c[name] = (off, arr.shape[1])
        arrs.append(a)
        off += arr.shape[1]
    for C in (32, 64):
        j = np.arange(C)[:, None]
        i = np.arange(C)[None, :]
        add(f'mi{C}_0', (j <= i).astype(np.float32))
        add(f'mi{C}_1', (j >= i).astype(np.float32))
        add(f'ms{C}_0', (j < i).astype(np.float32))
        add(f'ms{C}_1', (j > i).astype(np.float32))
    sel = np.zeros((16, 2, 2, 128), np.float32)
    for d in range(2):
        for ct in range(2):
            for m in range(128):
                sel[8 + 4 * d + 2 * ct + m // 64, d, ct, m] = 1.0
    add('sel_lf', sel.reshape(16, 512))
    p = np.arange(128)
    add('blk64', (p[:, None] // 64 == p[None, :] // 64).astype(np.float32))
    add('hsel', (p[:, None] // 64 == np.arange(2)[None, :]).astype(np.float32))
    add('ones', np.ones((128, 128), np.float32))
    return c, np.concatenate(arrs, 1)


CST_MAP, CST_ARR = build_consts()
NCST = CST_ARR.shape[1]


def cv(fw, name, np_=128, c0=0, c1=None):
    off, n = CST_MAP[name]
    c1 = n if c1 is None else c1
    return fw.cst[0:np_, off + c0: off + c1]


def blocks_for(d):
    lat = [(NCTX + m * 512, 512) for m in range(8)]
    if d == 1:
        lat = lat[::-1]
    return [(0, 256)] + lat


def gla_core(fw, name, QT, KTd, LFd, Vsrc, Od, dvx, C, kscale):
    psb = fw.psb
    ident = fw.ident
    W = 4 * dvx
    with ExitStack() as st:
        qb = [fw.sb([64, 2048], name=f'g_q{i}', stack=st) for i in range(1)]
        kb = [fw.sb([64, 2048], name=f'g_k{i}', stack=st) for i in range(1)]
        lb = [fw.sb([64, 2048], name=f'g_l{i}', stack=st) for i in range(1)]
        vb = [fw.sb([C, 16 * W], name=f'g_v{i}', stack=st) for i in range(2)]
        ob = [fw.sb([C, 16 * W], name=f'g_o{i}', stack=st) for i in range(2)]
        Gz = fw.sb([64, 4 * GS], name='g_Gz', stack=st)
        cum = fw.sb([64, 2048], name='g_cum', stack=st)
        d3 = fw.sb([64, 2048], name='g_d3', stack=st)
        Ep = fw.sb([64, 2048], name='g_Ep', stack=st)
        Em = fw.sb([64, 2048], name='g_Em', stack=st)
        E3 = fw.sb([64, 2048], name='g_E3', stack=st)
        Qt = fw.sb([64, 2048], name='g_Qt', stack=st)
        Kbar = fw.sb([64, 2048], name='g_Kbar', stack=st)
        Khat = fw.sb([64, 2048], name='g_Khat', stack=st)
        glast = fw.sb([64, 64], name='g_glast', stack=st)
        Am = [fw.sb([C, 4 * C], name=f'g_Am{i}', stack=st) for i in range(2)]
        Kt = [fw.sb([C, 256], name=f'g_Kt{i}', stack=st) for i in range(2)]
        S = [fw.sb([64, dvx], name=f'g_S{i}', stack=st) for i in range(4)]
        ones = fw.sb([64, 512], name='g_ones', stack=st)
        fw.memset(ones, 1.0)
        fw.memset(Gz, 0.0)
        bi = 0
        ci = 0
        for d in range(2):
            for h in range(4):
                fw.memset(S[h], 0.0)
            moff = CST_MAP[f'mi{C}_{d}'][0]
            for (t0, n) in blocks_for(d):
                nn = n // C
                q, k, l, v, o = qb[0], kb[0], lb[0], vb[bi % 2], ob[bi % 2]
                bi += 1

                def v3(tl):
                    return fv(tl, 0, [[512, 4], [1, n]], 0, 64)
                fw.dma('sp', v3(q), QT[:, t0:t0 + n].re('(h p) t -> p h t', p=64))
                fw.dma('sp', v3(k), KTd[d][:, t0:t0 + n].re('(h p) t -> p h t', p=64))
                fw.dma('sp', v3(l), LFd[d][:, t0:t0 + n].re('(h p) t -> p h t', p=64))
                vd, vc0, vw = Vsrc[d]
                fw.dma('sp', fv(v, 0, [[W, nn], [1, W]], 0, C),
                       vd[t0:t0 + n, vc0:vc0 + W].re('(n c) d -> c n d', c=C))
                for h in range(4):
                    fw.scan(Gz[:, h * GS + 1: h * GS + 1 + n], ones[:, 0:n], l[:, h * 512: h * 512 + n],
                            0.0, ALU.mult, ALU.add)
                lastc = C - 1 if d == 0 else 0
                for h in range(4):
                    outv = fv(cum, h * 512, [[C, nn], [1, C]], 0, 64)
                    if d == 0:
                        fw.tt(outv, fv(Gz, h * GS + 1, [[C, nn], [1, C]], 0, 64),
                              fv(Gz, h * GS, [[C, nn], [0, C]], 0, 64), ALU.subtract)
                    else:
                        fw.tt(outv, fv(Gz, h * GS + C, [[C, nn], [0, C]], 0, 64),
                              fv(Gz, h * GS, [[C, nn], [1, C]], 0, 64), ALU.subtract)
                    fw.tt(fv(d3, h * 512, [[C, nn], [1, C]], 0, 64), outv,
                          fv(cum, h * 512 + lastc, [[C, nn], [0, C]], 0, 64), ALU.subtract, e='pool')
                fw.act(v3(Ep), v3(cum), AF.Exp)
                fw.act(v3(Em), v3(cum), AF.Exp, scale=-1.0)
                fw.act(v3(E3), v3(d3), AF.Exp, scale=-1.0)
                fw.tt(v3(Qt), v3(q), v3(Ep), ALU.mult)
                fw.stt(v3(Kbar), v3(k), kscale, v3(Em), ALU.mult, ALU.mult)
                fw.stt(v3(Khat), v3(k), kscale, v3(E3), ALU.mult, ALU.mult)
                fw.copy(fv(glast, 0, [[16, 4], [1, nn]], 0, 64), fv(Ep, lastc, [[512, 4], [C, nn]], 0, 64))
                order = range(nn) if d == 0 else range(nn - 1, -1, -1)
                for n_ in order:
                    a = n_ * C
                    psA, psT, psO, psS = psb[ci % 2], psb[2 + ci % 2], psb[4 + ci % 2], psb[6 + ci % 2]
                    am, kt = Am[ci % 2], Kt[ci % 2]
                    ci += 1
                    for h in range(4):
                        cs = slice(h * 512 + a, h * 512 + a + C)
                        fw.mm(psA[0:C, h * C:(h + 1) * C], Kbar[0:64, cs], Qt[0:64, cs], inc=(h == 3))
                    fw.tt(fv(am, 0, [[C, 4], [1, C]], 0, C), fv(psA, 0, [[C, 4], [1, C]], 0, C),
                          fv(fw.cst, moff, [[0, 4], [1, C]], 0, C), ALU.mult)
                    for h in range(4):
                        fw.tr(psT[0:C, h * 64:(h + 1) * 64], Khat[0:64, h * 512 + a: h * 512 + a + C], ident[0:64, 0:64])
                    fw.copy(kt[0:C, :], psT[0:C, 0:256], e='act')
                    for h in range(4):
                        cs = slice(h * 512 + a, h * 512 + a + C)
                        fw.mm(psO[0:C, h * dvx:(h + 1) * dvx], Qt[0:64, cs], S[h][0:64, :],
                              start=True, stop=False, inc=False)
                        fw.mm(psO[0:C, h * dvx:(h + 1) * dvx], am[0:C, h * C:(h + 1) * C],
                              v[0:C, n_ * W + h * dvx: n_ * W + (h + 1) * dvx], start=False, stop=True)
                    fw.copy(o[0:C, n_ * W:(n_ + 1) * W], psO[0:C, 0:W], e=('act' if ci % 2 else 'dve'))
                    for h in range(4):
                        fw.mm(psS[0:64, h * dvx:(h + 1) * dvx], kt[0:C, h * 64:(h + 1) * 64],
                              v[0:C, n_ * W + h * dvx: n_ * W + (h + 1) * dvx])
                        fw.stt(S[h][0:64, :], S[h][0:64, :], glast[0:64, h * 16 + n_: h * 16 + n_ + 1],
                               psS[0:64, h * dvx:(h + 1) * dvx], ALU.mult, ALU.add)
                fw.dma('pool', Od[d][t0:t0 + n, :].re('(n c) d -> c n d', c=C), fv(o, 0, [[W, nn], [1, W]], 0, C))
        fw.barrier()


def rms_head_norm(fw, h, out, gbc, sc):
    fw.tt(sc['sq'], h, h, ALU.mult)
    fw.reduce(sc['ss'], fv(sc['sq'], 0, [[64, 4], [1, 64]]), ALU.add)
    fw.ts(sc['ss'], sc['ss'], 1.0 / 64, ALU.mult, s2=NORM_EPS, op1=ALU.add)
    fw.act(sc['r'], sc['ss'], AF.Sqrt)
    fw.recip(sc['r'], sc['r'])
    fw.tt(fv(out, 0, [[64, 4], [1, 64]]), fv(h, 0, [[64, 4], [1, 64]]), fv(sc['r'], 0, [[1, 4], [0, 64]]), ALU.mult)
    fw.tt(out, out, gbc, ALU.mult, e='pool')


def store_rows_scan(fw, Y, c0, cw, tile_v, ti, colmajor):
    if ti < 2 or not colmajor:
        fw.dma('pool', Y[ti * 128:(ti + 1) * 128, c0:c0 + cw], tile_v)
    else:
        s0 = ti * 128 - NCTX
        for poff, nr, r0, rstep in scan_rows('c', s0, 128):
            fw.dma('pool', rows_ap(fw, Y, r0, rstep, nr, c0, c0 + cw), tile_v[poff:poff + nr, :])


def hgrn2(fw, L, S, Y, layer, colmajor, stop=9):
    psb = fw.psb
    KT = [S['A_KT0'], S['A_KT1']]
    LF = [S['A_LF0'], S['A_LF1']]
    with ExitStack() as st:
        gam = fw.sb([128, 16], name='h_gam', stack=st)
        eg = fw.sb([128, 16], name='h_eg', stack=st)
        tot = fw.sb([128, 4], name='h_tot', stack=st)
        part = fw.sb([128, 4], name='h_part', stack=st)
        lbv = fw.sb([128, 4], name='h_lb', stack=st)
        oml = fw.sb([128, 4], name='h_oml', stack=st)
        fw.dma('sp', gam, L['hgrn_gamma_t'])
        fw.act(eg, gam, AF.Exp)
        fw.reduce(tot, fv(eg, 0, [[4, 4], [1, 4]]), ALU.add)
        if layer >= 1:
            fw.reduce(part, fv(eg, 1, [[4, 4], [1, layer]]), ALU.add)
        else:
            fw.memset(part, 0.0)
        fw.recip(tot, tot)
        fw.tt(lbv, part, tot, ALU.mult)
        fw.ts(oml, lbv, -1.0, ALU.mult, s2=1.0, op1=ALU.add)
        HB = T // 2
        fr = [fw.sb([128, HB], name=f'h_fr{i}', stack=st) for i in range(2)]
        e1 = [fw.sb([128, HB], name=f'h_e{i}', stack=st) for i in range(2)]
        lf = [fw.sb([128, HB], name=f'h_lf{i}', stack=st) for i in range(2)]
        kk = [fw.sb([128, HB], name=f'h_k{i}', stack=st) for i in range(2)]
        it = 0
        for d in range(2):
            for ct in range(2):
                col = d * 2 + ct
                for hb in range(2):
                    f_, e_, l_, k_ = fr[it % 2], e1[it % 2], lf[it % 2], kk[it % 2]
                    it += 1
                    ts_ = slice(hb * HB, (hb + 1) * HB)
                    rows = slice(d * 256 + ct * 128, d * 256 + (ct + 1) * 128)
                    fw.dma('sp', f_, S['fm_A_f'][rows, ts_])
                    fw.act(e_, f_, AF.Exp, scale=-1.0)
                    fw.ts(e_, e_, 1.0, ALU.add)
                    fw.recip(e_, e_)
                    fw.ts(e_, e_, oml[:, col:col + 1], ALU.mult, s2=lbv[:, col:col + 1], op1=ALU.add)
                    fw.act(l_, e_, AF.Ln)
                    fw.ts(k_, e_, -1.0, ALU.mult, s2=1.0, op1=ALU.add, e='pool')
                    fw.dma('pool', LF[d][ct * 128:(ct + 1) * 128, ts_], l_)
                    fw.dma('pool', KT[d][ct * 128:(ct + 1) * 128, ts_], k_)
        fw.barrier()
    if stop <= 1:
        return
    Od = [S['A_O0'], S['A_O1']]
    Vsrc = [(S['tm_A_ig'], 0, 512), (S['tm_A_ig'], 0, 512)]
    gla_core(fw, 'hgrn', S['fm_A_q'], KT, LF, Vsrc, Od, 64, 32, 1.0)
    if stop <= 2:
        return
    with ExitStack() as st:
        gbc = fw.sb([128, 256], name='h_gbc', stack=st)
        fw.dma('sp', gbc, dram_bcast_rows(L['hgrn_norm_g'], 0, 256))
        o0 = [fw.sb([128, 256], name=f'h_o0{i}', stack=st) for i in range(2)]
        o1 = [fw.sb([128, 256], name=f'h_o1{i}', stack=st) for i in range(2)]
        gg = [fw.sb([128, 256], name=f'h_g{i}', stack=st) for i in range(2)]
        yy = [fw.sb([128, 256], name=f'h_y{i}', stack=st) for i in range(2)]
        sc = {'sq': fw.sb([128, 256], name='h_sq', stack=st), 'ss': fw.sb([128, 4], name='h_ss', stack=st),
              'r': fw.sb([128, 4], name='h_r', stack=st)}
        for ti in range(T // 128):
            a0, a1, g_, y_ = o0[ti % 2], o1[ti % 2], gg[ti % 2], yy[ti % 2]
            rs = slice(ti * 128, (ti + 1) * 128)
            fw.dma('sp', a0, Od[0][rs, :])
            fw.dma('sp', a1, Od[1][rs, :])
            fw.dma('sp', g_, S['tm_A_ig'][rs, 256:512])
            fw.tt(a0, a0, a1, ALU.add)
            fw.act(g_, g_, AF.Silu)
            rms_head_norm(fw, a0, y_, gbc, sc)
            fw.tt(y_, y_, g_, ALU.mult)
            store_rows_scan(fw, Y, 0, 256, y_[:, :], ti, colmajor)
        fw.barrier()


def mlstm(fw, L, S, Y, layer, colmajor):
    psb = fw.psb
    LF = [S['D_LF0'], S['D_LF1']]
    VX = [S['D_VX0'], S['D_VX1']]
    with ExitStack() as st:
        gt = fw.sb([16, T], name='m_gt', stack=st)
        ee = fw.sb([16, T], name='m_e', stack=st)
        fb = fw.sb([16, 1], name='m_fb', stack=st)
        fw.dma('sp', gt, S['fm_D_gates'])
        fw.dma('sp', fb, L['mlstm_fb_col'])
        fw.ts(fb, fb, -1.0, ALU.mult)
        fw.act(ee, gt, AF.Exp, bias=fb, scale=-1.0)
        fw.ts(ee, ee, 1.0, ALU.add)
        fw.act(ee, ee, AF.Ln)
        fw.ts(ee, ee, -1.0, ALU.mult)
        stg = [fw.sb([128, 512], name=f'm_stg{i}', stack=st) for i in range(2)]
        it = 0
        soff = CST_MAP['sel_lf'][0]
        for d in range(2):
            for ct in range(2):
                for t0 in range(0, T, 512):
                    n = min(512, T - t0)
                    p = psb[it % 4]
                    sg = stg[it % 2]
                    it += 1
                    fw.mm(p[:, 0:n], fw.cst[0:16, soff + (d * 2 + ct) * 128: soff + (d * 2 + ct + 1) * 128], ee[0:16, t0:t0 + n])
                    fw.copy(sg[:, 0:n], p[:, 0:n], e=('act' if it % 2 else 'dve'))
                    fw.dma('pool', LF[d][ct * 128:(ct + 1) * 128, t0:t0 + n], sg[:, 0:n])
        ibc = fw.sb([128, 8], name='m_ibc', stack=st)
        fw.dma('sp', ibc, dram_bcast_rows(L['mlstm_i_bias'], 0, 8))
        vo = [fw.sb([128, 256], name=f'm_vo{i}', stack=st) for i in range(2)]
        gts = [fw.sb([128, 16], name=f'm_gts{i}', stack=st) for i in range(2)]
        sv = [fw.sb([128, 8], name=f'm_sv{i}', stack=st) for i in range(2)]
        vx = [[fw.sb([128, 260], name=f'm_vx{d}{i}', stack=st) for i in range(2)] for d in range(2)]
        for ti in range(T // 128):
            rs = slice(ti * 128, (ti + 1) * 128)
            v_, g_, s_ = vo[ti % 2], gts[ti % 2], sv[ti % 2]
            fw.dma('sp', v_, S['tm_D_vo'][rs, 0:256])
            fw.dma('sp', g_, S['tm_D_gates'][rs, :])
            fw.tt(s_, g_[:, 0:8], ibc, ALU.add)
            fw.act(s_, s_, AF.Exp)
            for d in range(2):
                x_ = vx[d][ti % 2]
                fw.tt(fv(x_, 0, [[65, 4], [1, 64]]), fv(v_, 0, [[64, 4], [1, 64]]), fv(s_, d * 4, [[1, 4], [0, 64]]),
                      ALU.mult, e=('pool' if d else 'dve'))
                fw.copy(fv(x_, 64, [[65, 4], [1, 1]]), fv(s_, d * 4, [[1, 4], [1, 1]]))
                fw.dma('pool', VX[d][rs, :], x_)
        fw.barrier()
    Od = [S['D_O0'], S['D_O1']]
    Vsrc = [(VX[0], 0, 260), (VX[1], 0, 260)]
    KTv = S['fm_D_qk'][256:512, :]
    gla_core(fw, 'mlstm', S['fm_D_qk'][0:256, :], [KTv, KTv], LF, Vsrc, Od, 65, 64, 0.125)
    with ExitStack() as st:
        gbc = fw.sb([128, 256], name='m_gbc', stack=st)
        fw.dma('sp', gbc, dram_bcast_rows(L['mlstm_norm_g'], 0, 256))
        oo = [[fw.sb([128, 260], name=f'm_o{d}{i}', stack=st) for i in range(2)] for d in range(2)]
        og = [fw.sb([128, 256], name=f'm_og{i}', stack=st) for i in range(2)]
        hh = [fw.sb([128, 256], name=f'm_h{i}', stack=st) for i in range(2)]
        h2 = fw.sb([128, 256], name='m_h2', stack=st)
        yy = [fw.sb([128, 256], name=f'm_y{i}', stack=st) for i in range(2)]
        den = [fw.sb([128, 4], name=f'm_den{d}', stack=st) for d in range(2)]
        sc = {'sq': fw.sb([128, 256], name='m_sq', stack=st), 'ss': fw.sb([128, 4], name='m_ss', stack=st),
              'r': fw.sb([128, 4], name='m_r', stack=st)}
        for ti in range(T // 128):
            rs = slice(ti * 128, (ti + 1) * 128)
            g_, h_, y_ = og[ti % 2], hh[ti % 2], yy[ti % 2]
            fw.dma('sp', g_, S['tm_D_vo'][rs, 256:512])
            for d in range(2):
                o_ = oo[d][ti % 2]
                fw.dma('sp', o_, Od[d][rs, :])
                fw.act(den[d], fv(o_, 64, [[65, 4]]), AF.Abs)
                fw.ts(den[d], den[d], 1.0, ALU.max)
                fw.recip(den[d], den[d])
                fw.tt(fv(h_ if d == 0 else h2, 0, [[64, 4], [1, 64]]), fv(o_, 0, [[65, 4], [1, 64]]),
                      fv(den[d], 0, [[1, 4], [0, 64]]), ALU.mult)
            fw.tt(h_, h_, h2, ALU.add)
            fw.act(g_, g_, AF.Sigmoid)
            rms_head_norm(fw, h_, y_, gbc, sc)
            fw.tt(y_, y_, g_, ALU.mult)
            store_rows_scan(fw, Y, 768, 256, y_[:, :], ti, colmajor)
        fw.barrier()


def set_psum(fw, mode):
    if getattr(fw, '_ps_stack', None) is not None:
        fw._ps_stack.close()
    fw._ps_stack = ExitStack()
    fw._ps_gen = getattr(fw, '_ps_gen', 0) + 1
    if mode == 'full':
        fw.psb = [fw.ps([128, 512], name=f'psb{i}_{fw._ps_gen}', stack=fw._ps_stack) for i in range(8)]
    else:
        banks = [fw.ps([128, 512], name=f'psk{i}_{fw._ps_gen}', stack=fw._ps_stack) for i in range(8)]
        fw.psh = []
        for i, b in enumerate(banks):
            for hf in range(2):
                tl = Tl(fw, b.t[:, hf * 256:(hf + 1) * 256], 'ps')
                tl.name = f'psh{i}_{hf}_{fw._ps_gen}'
                fw.psh.append(tl)


def finish_rms_gate(fw, L, Od, gsrc, gain, ycol, func, Y, colmajor, pfx):
    with ExitStack() as st:
        gbc = fw.sb([128, 256], name=f'{pfx}_gbc', stack=st)
        fw.dma('sp', gbc, dram_bcast_rows(L[gain], 0, 256))
        o0 = [fw.sb([128, 256], name=f'{pfx}_o0{i}', stack=st) for i in range(2)]
        o1 = [fw.sb([128, 256], name=f'{pfx}_o1{i}', stack=st) for i in range(2)]
        gg = [fw.sb([128, 256], name=f'{pfx}_g{i}', stack=st) for i in range(2)]
        yy = [fw.sb([128, 256], name=f'{pfx}_y{i}', stack=st) for i in range(2)]
        sc = {'sq': fw.sb([128, 256], name=f'{pfx}_sq', stack=st), 'ss': fw.sb([128, 4], name=f'{pfx}_ss', stack=st),
              'r': fw.sb([128, 4], name=f'{pfx}_r', stack=st)}
        for ti in range(T // 128):
            a0, a1, g_, y_ = o0[ti % 2], o1[ti % 2], gg[ti % 2], yy[ti % 2]
            rs = slice(ti * 128, (ti + 1) * 128)
            fw.dma('sp', a0, Od[0][rs, :])
            fw.dma('sp', a1, Od[1][rs, :])
            fw.dma('sp', g_, gsrc[rs, :])
            fw.tt(a0, a0, a1, ALU.add)
            fw.act(g_, g_, func)
            rms_head_norm(fw, a0, y_, gbc, sc)
            fw.tt(y_, y_, g_, ALU.mult)
            store_rows_scan(fw, Y, ycol, 256, y_[:, :], ti, colmajor)
        fw.barrier()


def neumann_solve(fw, U0, L0, Ytl, UL, psi, C=64, w=64):
    psY, psU, psL = psi
    Uc, Lc = U0, L0
    for lvl in range(6):
        for h in range(4):
            fw.mm(psY[0:C, h * w:(h + 1) * w], Uc[0:C, h * C:(h + 1) * C], Ytl[0:C, h * w:(h + 1) * w], inc=(h == 3))
        fw.tt(Ytl[0:C, 0:4 * w], Ytl[0:C, 0:4 * w], psY[0:C, 0:4 * w], ALU.subtract if lvl == 0 else ALU.add)
        if lvl < 5:
            Un, Ln = UL[lvl % 2]
            for h in range(4):
                hs = slice(h * C, (h + 1) * C)
                fw.mm(psU[0:C, hs], Lc[0:C, hs], Uc[0:C, hs], inc=(h == 3))
            fw.copy(Un[0:C, :], psU[0:C, 0:4 * C], e='act')
            if lvl < 4:
                for h in range(4):
                    hs = slice(h * C, (h + 1) * C)
                    fw.mm(psL[0:C, hs], Uc[0:C, hs], Lc[0:C, hs], inc=(h == 3))
                fw.copy(Ln[0:C, :], psL[0:C, 0:4 * C], e='act')
            Uc, Lc = Un, Ln


def bc3(tl, off, C, inner, hstep=1, p0=0):
    return fv(tl, off, [[hstep, 4], [0, inner]], p0, C)


def gdn(fw, L, S, Y, layer, colmajor, stop=9):
    C = 64
    PC = S['tm_C']
    psb = fw.psb
    ident = fw.ident
    with ExitStack() as st:
        wbc = fw.sb([128, 5 * 768], name='c_wbc', stack=st)
        fw.dma('sp', wbc, dram_bcast_rows(L['gdn_conv'], 0, 5 * 768))
        dtb = fw.sb([128, 8], name='c_dtb', stack=st)
        nA = fw.sb([128, 8], name='c_nA', stack=st)
        fw.dma('sp', dtb, dram_bcast_rows(L['gdn_dt_bias'], 0, 8))
        fw.dma('sp', nA, dram_bcast_rows(L['gdn_a_log'], 0, 8))
        fw.act(nA, nA, AF.Exp)
        fw.ts(nA, nA, -1.0, ALU.mult)
        xs = [[fw.sb([128, 768], name=f'c_x{j}{i}', stack=st) for j in range(5)] for i in range(2)]
        acc = [fw.sb([128, 768], name=f'c_acc{i}', stack=st) for i in range(2)]
        sq = fw.sb([128, 512], name='c_sq', stack=st)
        ss = fw.sb([128, 8], name='c_ss', stack=st)
        qkn = [fw.sb([128, 512], name=f'c_qkn{i}', stack=st) for i in range(2)]
        stq = [fw.sb([64, 512], name=f'c_stq{i}', stack=st) for i in range(2)]
        stk = [fw.sb([64, 512], name=f'c_stk{i}', stack=st) for i in range(2)]
        ba = [fw.sb([128, 16], name=f'c_ba{i}', stack=st) for i in range(2)]
        gl = [fw.sb([128, 16], name=f'c_gl{i}', stack=st) for i in range(2)]
        for ti in range(T // 128):
            t0 = ti * 128
            seg0, seg1 = (0, NCTX) if ti < 2 else (NCTX, T)
            X = xs[ti % 2]
            ac, qk, sq_, sk_, ba_, gl_ = acc[ti % 2], qkn[ti % 2], stq[ti % 2], stk[ti % 2], ba[ti % 2], gl[ti % 2]
            for j in range(5):
                s = j - 2
                lo, hi = max(seg0, t0 + s), min(seg1, t0 + 128 + s)
                if lo != t0 + s or hi != t0 + 128 + s:
                    fw.memset(X[j], 0.0, e='pool')
                fw.dma('sp', X[j][lo - (t0 + s): hi - (t0 + s), :], PC[lo:hi, 0:768])
                fw.tt(X[j], X[j], wbc[:, j * 768:(j + 1) * 768], ALU.mult, e=('pool' if j % 2 else 'dve'))
            fw.tt(ac, X[0], X[1], ALU.add)
            fw.tt(ac, ac, X[2], ALU.add, e='pool')
            fw.tt(ac, ac, X[3], ALU.add)
            fw.tt(ac, ac, X[4], ALU.add, e='pool')
            fw.act(ac, ac, AF.Silu)
            fw.tt(sq, ac[:, 0:512], ac[:, 0:512], ALU.mult)
            fw.reduce(ss, fv(sq, 0, [[64, 8], [1, 64]]), ALU.add)
            fw.ts(ss, ss, 1e-12, ALU.add)
            fw.act(ss, ss, AF.Sqrt)
            fw.recip(ss, ss)
            fw.ts(ss[:, 0:4], ss[:, 0:4], 0.125, ALU.mult)
            fw.tt(fv(qk, 0, [[64, 8], [1, 64]]), fv(ac, 0, [[64, 8], [1, 64]]), fv(ss, 0, [[1, 8], [0, 64]]), ALU.mult)
            for g in range(8):
                p = psb[g // 4 + 2 * (ti % 2)]
                fw.tr(p[0:64, (g % 4) * 128:(g % 4 + 1) * 128], qk[:, g * 64:(g + 1) * 64], ident)
            fw.copy(sq_[0:64, :], psb[0 + 2 * (ti % 2)][0:64, :], e='act')
            fw.copy(sk_[0:64, :], psb[1 + 2 * (ti % 2)][0:64, :])
            fw.dma('pool', S['C_QT'][:, t0:t0 + 128].re('(h p) t -> p h t', p=64), fv(sq_, 0, [[128, 4], [1, 128]], 0, 64))
            fw.dma('pool', S['C_KT'][:, t0:t0 + 128].re('(h p) t -> p h t', p=64), fv(sk_, 0, [[128, 4], [1, 128]], 0, 64))
            fw.dma('pool', S['C_KV'][t0:t0 + 128, 0:256], qk[:, 256:512])
            fw.dma('pool', S['C_KV'][t0:t0 + 128, 256:512], ac[:, 512:768])
            fw.dma('sp', ba_, S['tm_C_ba'][t0:t0 + 128, :])
            fw.act(gl_[:, 0:8], ba_[:, 0:8], AF.Sigmoid)
            fw.tt(ba_[:, 8:16], ba_[:, 8:16], dtb, ALU.add)
            fw.act(ba_[:, 8:16], ba_[:, 8:16], AF.Exp)
            fw.ts(ba_[:, 8:16], ba_[:, 8:16], 1.0, ALU.add)
            fw.act(ba_[:, 8:16], ba_[:, 8:16], AF.Ln)
            fw.tt(gl_[:, 8:16], ba_[:, 8:16], nA, ALU.mult)
            fw.dma('pool', S['C_G'][t0:t0 + 128, :], gl_)
        fw.barrier()
    if stop <= 1:
        return
    psb = fw.psb
    Od = [S['C_O0'], S['C_O1']]
    with ExitStack() as st:
        qb = [fw.sb([64, 2048], name=f'c_q{i}', stack=st) for i in range(2)]
        kb = [fw.sb([64, 2048], name=f'c_k{i}', stack=st) for i in range(2)]
        kvb = [fw.sb([C, 8 * 512], name=f'c_kv{i}', stack=st) for i in range(2)]
        glb = [fw.sb([C, 8 * 16], name=f'c_glb{i}', stack=st) for i in range(2)]
        ob = [fw.sb([C, 8 * 256], name=f'c_ob{i}', stack=st) for i in range(2)]
        lab = fw.sb([C, 32], name='c_lab', stack=st)
        bb = fw.sb([C, 32], name='c_bb', stack=st)
        ecum = fw.sb([C, 32], name='c_ecum', stack=st)
        necum = fw.sb([C, 32], name='c_necum', stack=st)
        elast = fw.sb([C, 32], name='c_elast', stack=st)
        etot = fw.sb([C, 32], name='c_etot', stack=st)
        dd = fw.sb([C, 32], name='c_dd', stack=st)
        LA = [fw.sb([C, 256], name=f'c_LA{i}', stack=st) for i in range(2)]
        Gt = [fw.sb([C, 256], name=f'c_Gt{i}', stack=st) for i in range(2)]
        Gi = [fw.sb([C, 256], name=f'c_Gi{i}', stack=st) for i in range(2)]
        Gs = [fw.sb([C, 256], name=f'c_Gs{i}', stack=st) for i in range(2)]
        U0 = [fw.sb([C, 256], name=f'c_U0{i}', stack=st) for i in range(2)]
        L0 = [fw.sb([C, 256], name=f'c_L0{i}', stack=st) for i in range(2)]
        At = [fw.sb([C, 256], name=f'c_At{i}', stack=st) for i in range(2)]
        UL = [(fw.sb([C, 256], name=f'c_Un{i}', stack=st), fw.sb([C, 256], name=f'c_Ln{i}', stack=st)) for i in range(2)]
        Yt = [fw.sb([C, 256], name=f'c_Y{i}', stack=st) for i in range(2)]
        vn = [fw.sb([C, 256], name=f'c_vn{i}', stack=st) for i in range(2)]
        Kh = [fw.sb([C, 256], name=f'c_Kh{i}', stack=st) for i in range(2)]
        tmpo = [fw.sb([C, 256], name=f'c_to{i}', stack=st) for i in range(2)]
        Sst = [fw.sb([64, 64], name=f'c_S{i}', stack=st) for i in range(4)]
        ones_off = CST_MAP['ones'][0]
        bi = 0
        ci = 0
        for d in range(2):
            for h in range(4):
                fw.memset(Sst[h], 0.0)
            mi_off = CST_MAP[f'mi64_{d}'][0]
            ms_off = CST_MAP[f'ms64_{d}'][0]
            msT_off = CST_MAP[f'ms64_{1 - d}'][0]
            mi = fw.cst[0:C, mi_off:mi_off + C]
            for (t0, n) in blocks_for(d):
                nn = n // C
                q, k, kv, glt, o = qb[bi % 2], kb[bi % 2], kvb[bi % 2], glb[bi % 2], ob[bi % 2]
                bi += 1
                fw.dma('sp', fv(q, 0, [[512, 4], [1, n]], 0, 64), S['C_QT'][:, t0:t0 + n].re('(h p) t -> p h t', p=64))
                fw.dma('sp', fv(k, 0, [[512, 4], [1, n]], 0, 64), S['C_KT'][:, t0:t0 + n].re('(h p) t -> p h t', p=64))
                fw.dma('sp', fv(kv, 0, [[512, nn], [1, 512]], 0, C), S['C_KV'][t0:t0 + n, :].re('(n c) d -> c n d', c=C))
                fw.dma('sp', fv(glt, 0, [[16, nn], [1, 16]], 0, C), S['C_G'][t0:t0 + n, :].re('(n c) d -> c n d', c=C))
                fw.copy(fv(bb, 0, [[4, nn], [1, 4]], 0, C), fv(glt, 4 * d, [[16, nn], [1, 4]], 0, C))
                fw.copy(fv(lab, 0, [[4, nn], [1, 4]], 0, C), fv(glt, 8 + 4 * d, [[16, nn], [1, 4]], 0, C))
                pc = psb[7]
                fw.mm(pc[0:C, 0:nn * 4], mi, lab[0:C, 0:nn * 4])
                fw.mm(pc[0:C, 32:32 + nn * 4], fw.cst[0:C, ones_off:ones_off + C], lab[0:C, 0:nn * 4])
                fw.act(ecum[:, 0:nn * 4], pc[0:C, 0:nn * 4], AF.Exp)
                fw.ts(necum[:, 0:nn * 4], ecum[:, 0:nn * 4], -1.0, ALU.mult)
                fw.act(etot[:, 0:nn * 4], pc[0:C, 32:32 + nn * 4], AF.Exp)
                fw.copy(dd[:, 0:nn * 4], pc[0:C, 0:nn * 4])
                fw.tt(dd[:, 0:nn * 4], pc[0:C, 32:32 + nn * 4], dd[:, 0:nn * 4], ALU.subtract)
                fw.act(elast[:, 0:nn * 4], dd[:, 0:nn * 4], AF.Exp)
                order = range(nn) if d == 0 else range(nn - 1, -1, -1)
                for n_ in order:
                    a = n_ * C
                    psD, psK, psQ, psL, psR, psO = psb[0], psb[1], psb[2], psb[3], psb[4], psb[2]
                    psY, psU, psL2 = psb[5], psb[6], psb[7]
                    i2 = ci % 2
                    ci += 1
                    ktok = kv[0:C, n_ * 512: n_ * 512 + 256]
                    vtok = kv[0:C, n_ * 512 + 256: n_ * 512 + 512]
                    fw.tt(fv(LA[i2], 0, [[C, 4], [1, C]], 0, C), fv(fw.cst, msT_off, [[0, 4], [1, C]], 0, C),
                          bc3(lab, n_ * 4, C, C), ALU.mult, e='pool')
                    for h in range(4):
                        fw.mm(psD[0:C, h * C:(h + 1) * C], LA[i2][0:C, h * C:(h + 1) * C], mi, inc=(h == 3))
                    fw.act(Gt[i2][0:C, :], psD[0:C, 0:256], AF.Exp)
                    fw.tt(fv(Gi[i2], 0, [[C, 4], [1, C]], 0, C), fv(Gt[i2], 0, [[C, 4], [1, C]], 0, C),
                          fv(fw.cst, mi_off, [[0, 4], [1, C]], 0, C), ALU.mult, e='pool')
                    fw.tt(fv(Gs[i2], 0, [[C, 4], [1, C]], 0, C), fv(Gt[i2], 0, [[C, 4], [1, C]], 0, C),
                          fv(fw.cst, ms_off, [[0, 4], [1, C]], 0, C), ALU.mult, e='pool')
                    fw.tt(fv(Gs[i2], 0, [[C, 4], [1, C]], 0, C), fv(Gs[i2], 0, [[C, 4], [1, C]], 0, C),
                          bc3(bb, n_ * 4, C, C), ALU.mult, e='pool')
                    for h in range(4):
                        cs = slice(h * 512 + a, h * 512 + a + C)
                        fw.mm(psK[0:C, h * C:(h + 1) * C], k[0:64, cs], k[0:64, cs], inc=(h == 3))
                    fw.tt(U0[i2][0:C, :], psK[0:C, 0:256], Gs[i2][0:C, :], ALU.mult)
                    for h in range(4):
                        cs = slice(h * 512 + a, h * 512 + a + C)
                        fw.mm(psQ[0:C, h * C:(h + 1) * C], k[0:64, cs], q[0:64, cs], inc=(h == 3))
                    fw.tt(At[i2][0:C, :], psQ[0:C, 0:256], Gi[i2][0:C, :], ALU.mult)
                    for h in range(4):
                        fw.tr(psL[0:C, h * C:(h + 1) * C], U0[i2][0:C, h * C:(h + 1) * C], ident[0:64, 0:64])
                    fw.copy(L0[i2][0:C, :], psL[0:C, 0:256], e='act')
                    for h in range(4):
                        cs = slice(h * 512 + a, h * 512 + a + C)
                        fw.mm(psR[0:C, h * 64:(h + 1) * 64], k[0:64, cs], Sst[h][0:64, :])
                    fw.tt(fv(Yt[i2], 0, [[64, 4], [1, 64]], 0, C), fv(psR, 0, [[64, 4], [1, 64]], 0, C),
                          bc3(necum, n_ * 4, C, 64), ALU.mult)
                    fw.tt(Yt[i2][0:C, :], Yt[i2][0:C, :], vtok, ALU.add)
                    neumann_solve(fw, U0[i2], L0[i2], Yt[i2], UL, (psY, psU, psL2))
                    fw.tt(fv(vn[i2], 0, [[64, 4], [1, 64]], 0, C), fv(Yt[i2], 0, [[64, 4], [1, 64]], 0, C),
                          bc3(bb, n_ * 4, C, 64), ALU.mult)
                    for h in range(4):
                        cs = slice(h * 512 + a, h * 512 + a + C)
                        fw.mm(psO[0:C, h * 64:(h + 1) * 64], q[0:64, cs], Sst[h][0:64, :])
                    fw.tt(fv(tmpo[i2], 0, [[64, 4], [1, 64]], 0, C), fv(psO, 0, [[64, 4], [1, 64]], 0, C),
                          bc3(ecum, n_ * 4, C, 64), ALU.mult)
                    for h in range(4):
                        fw.mm(psD[0:C, h * 64:(h + 1) * 64], At[i2][0:C, h * C:(h + 1) * C], vn[i2][0:C, h * 64:(h + 1) * 64])
                    fw.tt(o[0:C, n_ * 256:(n_ + 1) * 256], tmpo[i2][0:C, :], psD[0:C, 0:256], ALU.add)
                    fw.tt(fv(Kh[i2], 0, [[64, 4], [1, 64]], 0, C), fv(kv, n_ * 512, [[64, 4], [1, 64]], 0, C),
                          bc3(elast, n_ * 4, C, 64), ALU.mult, e='pool')
                    for h in range(4):
                        fw.mm(psK[0:64, h * 64:(h + 1) * 64], Kh[i2][0:C, h * 64:(h + 1) * 64], vn[i2][0:C, h * 64:(h + 1) * 64])
                        fw.stt(Sst[h][0:64, :], Sst[h][0:64, :], etot[0:64, n_ * 4 + h: n_ * 4 + h + 1],
                               psK[0:64, h * 64:(h + 1) * 64], ALU.mult, ALU.add)
                fw.dma('pool', Od[d][t0:t0 + n, :].re('(n c) d -> c n d', c=C), fv(o, 0, [[256, nn], [1, 256]], 0, C))
        fw.barrier()
    if stop <= 2:
        return
    finish_rms_gate(fw, L, Od, PC[:, 768:1024], 'gdn_norm_g', 512, AF.Silu, Y, colmajor, 'cf')


RWKV_GN_EPS = 64e-5
TP = T + 4
SEGS = [(1, 0, NCTX), (NCTX + 3, NCTX, NLAT)]


def shift_tile(fw, xp, nb, out, hm, omm, P):
    for ps0, t0, n in SEGS:
        fw.tt(nb[0:P, t0:t0 + n], xp[0:P, ps0 - 1: ps0 - 1 + n], xp[0:P, ps0 + 1: ps0 + 1 + n], ALU.add, e='pool')
        fw.ts(nb[0:P, t0:t0 + n], nb[0:P, t0:t0 + n], hm, ALU.mult)
        fw.stt(out[0:P, t0:t0 + n], xp[0:P, ps0: ps0 + n], omm, nb[0:P, t0:t0 + n], ALU.mult, ALU.add)


def load_padded(fw, xp, src, P):
    for ps0, t0, n in SEGS:
        fw.dma('sp', xp[0:P, ps0:ps0 + n], src[:, t0:t0 + n])


def rwkv7(fw, L, S, Y, layer, colmajor, stop=9):
    C = 64
    psb = fw.psb
    ident = fw.ident
    PT = S['fm_B_all']
    LW = [S['B_LW0'], S['B_LW1']]
    AA = [S['B_AA0'], S['B_AA1']]
    KD = [S['B_KD0'], S['B_KD1']]
    BB = [S['B_BB0'], S['B_BB1']]
    blocks9 = [(t0, min(512, T - t0)) for t0 in range(0, T, 512)]
    with ExitStack() as st:
        pc = fw.sb([128, 14], name='r_pc', stack=st)
        fw.dma('sp', pc, L['rwkv_pcol'])
        mu128 = fw.sb([128, 9], name='r_mu128', stack=st)
        mu64 = fw.sb([64, 18], name='r_mu64', stack=st)
        fw.dma('sp', mu128, L['rwkv_mu128'])
        fw.dma('sp', mu64, L['rwkv_mu64'])
        hm128 = fw.sb([128, 9], name='r_hm128', stack=st)
        om128 = fw.sb([128, 9], name='r_om128', stack=st)
        hm64 = fw.sb([64, 18], name='r_hm64', stack=st)
        om64 = fw.sb([64, 18], name='r_om64', stack=st)
        fw.ts(hm128, mu128, 0.5, ALU.mult)
        fw.ts(om128, mu128, -1.0, ALU.mult, s2=1.0, op1=ALU.add)
        fw.ts(hm64, mu64, 0.5, ALU.mult)
        fw.ts(om64, mu64, -1.0, ALU.mult, s2=1.0, op1=ALU.add)
        npc = fw.sb([128, 14], name='r_npc', stack=st)
        fw.ts(npc, pc, -1.0, ALU.mult)
        omka = fw.sb([128, 2], name='r_omka', stack=st)
        fw.ts(omka, pc[:, 2:4], -1.0, ALU.mult, s2=1.0, op1=ALU.add)
        w2 = fw.sb([64, 512], name='r_w2', stack=st)
        a2 = fw.sb([64, 512], name='r_a2', stack=st)
        g2 = fw.sb([128, 256], name='r_g2', stack=st)
        fw.dma('sp', fv(w2, 0, [[256, 2], [1, 256]], 0, 64), V(L['rwkv_w2'], L['rwkv_w2'].ap().rearrange('d r c -> r d c')))
        fw.dma('sp', fv(a2, 0, [[256, 2], [1, 256]], 0, 64), V(L['rwkv_a2'], L['rwkv_a2'].ap().rearrange('d r c -> r d c')))
        fw.dma('sp', g2, L['rwkv_g2'])
        xp = [fw.sb([128, TP], name=f'r_xp{i}', stack=st) for i in range(2)]
        fw.memset(xp[0], 0.0)
        fw.memset(xp[1], 0.0)
        nb = fw.sb([128, T], name='r_nb', stack=st)
        stg = [fw.sb([128, 512], name=f'r_stg{i}', stack=st) for i in range(3)]
        si = 0
        with ExitStack() as st1:
            lo = fw.sb([64, T], name='r_lo', stack=st1)
            for kind in range(2):
                for d in range(2):
                    row0 = 768 + kind * 128 + d * 64
                    col = row0 // 64
                    x_ = xp[(kind * 2 + d) % 2]
                    load_padded(fw, x_, PT[row0:row0 + 64, :], 64)
                    shift_tile(fw, x_, nb, lo, hm64[:, col:col + 1], om64[:, col:col + 1], 64)
                    if kind == 0:
                        fw.act(lo, lo, AF.Tanh)
                    wsrc = w2 if kind == 0 else a2
                    for ct in range(2):
                        bcol = (6 if kind == 0 else 10) + d * 2 + ct
                        for (t0, n) in blocks9:
                            p = psb[si % 4]
                            sg = stg[si % 3]
                            si += 1
                            fw.mm(p[:, 0:n], wsrc[0:64, d * 256 + ct * 128: d * 256 + (ct + 1) * 128], lo[0:64, t0:t0 + n])
                            fw.act(sg[:, 0:n], p[:, 0:n], AF.Exp, bias=npc[:, bcol:bcol + 1], scale=-1.0)
                            fw.ts(sg[:, 0:n], sg[:, 0:n], 1.0, ALU.add)
                            fw.recip(sg[:, 0:n], sg[:, 0:n])
                            if kind == 0:
                                fw.ts(sg[:, 0:n], sg[:, 0:n], -0.6065306597126334, ALU.mult, e='pool')
                                fw.dma('pool', LW[d][ct * 128:(ct + 1) * 128, t0:t0 + n], sg[:, 0:n])
                            else:
                                fw.dma('pool', AA[d][ct * 128:(ct + 1) * 128, t0:t0 + n], sg[:, 0:n])
            fw.barrier()
        with ExitStack() as st2:
            rs_ = fw.sb([128, T], name='r_rs', stack=st2)
            ks_ = fw.sb([128, T], name='r_ks', stack=st2)
            kk_ = fw.sb([128, T], name='r_kk', stack=st2)
            aa_ = [fw.sb([128, T], name=f'r_aa{i}', stack=st2) for i in range(2)]
            t1_ = fw.sb([128, T], name='r_t1', stack=st2)
            sbs = fw.sb([128, 34 * 4], name='r_sbs', stack=st2)
            hs_off = CST_MAP['hsel'][0]
            b64_off = CST_MAP['blk64'][0]
            for ct in range(2):
                load_padded(fw, xp[0], PT[ct * 128:(ct + 1) * 128, :], 128)
                shift_tile(fw, xp[0], nb, rs_, hm128[:, ct:ct + 1], om128[:, ct:ct + 1], 128)
                fw.dma('pool', S['B_RT'][ct * 128:(ct + 1) * 128, :], rs_)
                load_padded(fw, xp[1], PT[256 + ct * 128: 256 + (ct + 1) * 128, :], 128)
                shift_tile(fw, xp[1], nb, ks_, hm128[:, 2 + ct:3 + ct], om128[:, 2 + ct:3 + ct], 128)
                fw.ts(kk_, ks_, pc[:, ct:ct + 1], ALU.mult)
                fw.tt(t1_, kk_, kk_, ALU.mult, e='pool')
                for (t0, n) in blocks9:
                    p = psb[si % 4]
                    si += 1
                    fw.mm(p[:, 0:n], fw.cst[:, b64_off:b64_off + 128], t1_[:, t0:t0 + n])
                    fw.ts(nb[:, t0:t0 + n], p[:, 0:n], 1e-12, ALU.add)
                fw.act(nb, nb, AF.Sqrt)
                fw.recip(nb, nb)
                fw.tt(kk_, kk_, nb, ALU.mult)
                fw.dma('pool', S['B_KK'][ct * 128:(ct + 1) * 128, :], kk_)
                pS = psb[4 + ct]
                for d in range(2):
                    a_ = aa_[d]
                    fw.dma('sp', a_, AA[d][ct * 128:(ct + 1) * 128, :])
                    fw.tt(t1_, kk_, a_, ALU.mult, e='pool')
                    fw.dma('pool', BB[d][ct * 128:(ct + 1) * 128, :], t1_)
                    fw.ts(a_, a_, pc[:, 2 + ct:3 + ct], ALU.mult, s2=omka[:, ct:ct + 1], op1=ALU.add)
                    fw.tt(a_, a_, ks_, ALU.mult)
                    fw.dma('pool', KD[d][ct * 128:(ct + 1) * 128, :], a_)
                    fw.stt(a_, a_, pc[:, 4 + ct:5 + ct], rs_, ALU.mult, ALU.mult)
                    for ti in range(34):
                        fw.mm(pS[:, d * 68 + ti * 2: d * 68 + ti * 2 + 2], a_[:, ti * 128:(ti + 1) * 128], fw.cst[:, hs_off:hs_off + 2],
                              inc=(ti == 33))
                fw.copy(fv(sbs, ct * 2, [[4, 34], [1, 2]]), fv(pS, 0, [[2, 34], [1, 2]]))
                fw.tt(fv(sbs, ct * 2, [[4, 34], [1, 2]]), fv(sbs, ct * 2, [[4, 34], [1, 2]]), fv(pS, 68, [[2, 34], [1, 2]]), ALU.add)
            fw.dma('pool', S['B_SB'][:, :].re('(n p) c -> p n c', p=128), fv(sbs, 0, [[4, 34], [1, 4]]))
            fw.barrier()
        with ExitStack() as st3:
            gs_ = fw.sb([128, T], name='r_gs', stack=st3)
            load_padded(fw, xp[0], PT[1024:1152, :], 128)
            shift_tile(fw, xp[0], nb, gs_, hm128[:, 8:9], om128[:, 8:9], 128)
            fw.act(gs_, gs_, AF.Sigmoid)
            for ti in range(34):
                p = psb[ti % 4]
                sg = stg[ti % 3]
                fw.mm(p[:, 0:256], gs_[:, ti * 128:(ti + 1) * 128], g2[:, :])
                fw.copy(sg[:, 0:256], p[:, 0:256], e=('act' if ti % 2 else 'dve'))
                fw.dma('pool', S['B_GATE'][ti * 128:(ti + 1) * 128, :], sg[:, 0:256])
            fw.barrier()
        with ExitStack() as st4:
            mvb = fw.sb([128, 256], name='r_mvb', stack=st4)
            fw.dma('sp', mvb, dram_bcast_rows(L['rwkv_mu_row'], 512, 256))
            vv = [[fw.sb([128, 256], name=f'r_v{j}{i}', stack=st4) for j in range(3)] for i in range(2)]
            for ti in range(34):
                t0 = ti * 128
                seg0, seg1 = (0, NCTX) if ti < 2 else (NCTX, T)
                X = vv[ti % 2]
                for j in range(3):
                    s = j - 1
                    lo_, hi_ = max(seg0, t0 + s), min(seg1, t0 + 128 + s)
                    if lo_ != t0 + s or hi_ != t0 + 128 + s:
                        fw.memset(X[j], 0.0, e='pool')
                    fw.dma('sp', X[j][lo_ - (t0 + s): hi_ - (t0 + s), :], S['tm_B_v'][lo_:hi_, :])
                fw.tt(X[0], X[0], X[2], ALU.add, e='pool')
                fw.stt(X[0], X[0], 0.5, X[1], ALU.mult, ALU.subtract)
                fw.tt(X[0], X[0], mvb, ALU.mult)
                fw.tt(X[0], X[0], X[1], ALU.add)
                fw.dma('pool', S['B_VS'][t0:t0 + 128, :], X[0])
            fw.barrier()
    if stop <= 1:
        return
    Od = [S['B_O0'], S['B_O1']]
    with ExitStack() as st:
        names = ['r', 'kd', 'kk', 'bb', 'lw']
        inb = {nm: [fw.sb([64, 2048], name=f'rc_{nm}{i}', stack=st) for i in range(1)] for nm in names}
        vb = [fw.sb([C, 8 * 256], name=f'rc_v{i}', stack=st) for i in range(2)]
        ob = [fw.sb([C, 8 * 256], name=f'rc_o{i}', stack=st) for i in range(2)]
        Gz = fw.sb([64, 4 * GS], name='rc_Gz', stack=st)
        cum = fw.sb([64, 2048], name='rc_cum', stack=st)
        cumx = fw.sb([64, 2048], name='rc_cumx', stack=st)
        d3 = fw.sb([64, 2048], name='rc_d3', stack=st)
        Ea = fw.sb([64, 2048], name='rc_Ea', stack=st)
        Eb = fw.sb([64, 2048], name='rc_Eb', stack=st)
        Rt = fw.sb([64, 2048], name='rc_Rt', stack=st)
        Kb = fw.sb([64, 2048], name='rc_Kb', stack=st)
        Bb = fw.sb([64, 2048], name='rc_Bb', stack=st)
        Ah = fw.sb([64, 2048], name='rc_Ah', stack=st)
        Kh = fw.sb([64, 2048], name='rc_Kh', stack=st)
        Bh = fw.sb([64, 2048], name='rc_Bh', stack=st)
        glast = fw.sb([64, 64], name='rc_glast', stack=st)
        U0 = fw.sb([C, 256], name='rc_U0', stack=st)
        L0 = fw.sb([C, 256], name='rc_L0', stack=st)
        Gm = fw.sb([C, 256], name='rc_Gm', stack=st)
        Pm = fw.sb([C, 256], name='rc_Pm', stack=st)
        Qm = fw.sb([C, 256], name='rc_Qm', stack=st)
        UL = [(fw.sb([C, 256], name=f'rc_Un{i}', stack=st), fw.sb([C, 256], name=f'rc_Ln{i}', stack=st)) for i in range(2)]
        Yt = fw.sb([C, 256], name='rc_Y', stack=st)
        KBt = fw.sb([C, 512], name='rc_KBt', stack=st)
        Sst = [fw.sb([64, 64], name=f'rc_S{i}', stack=st) for i in range(4)]
        ones = fw.sb([64, 512], name='rc_ones', stack=st)
        negm = fw.sb([C, 128], name='rc_negm', stack=st)
        fw.memset(ones, 1.0)
        fw.memset(Gz, 0.0)
        for d in range(2):
            o_, n_ = CST_MAP[f'mi64_{d}']
            fw.ts(negm[:, d * 64:(d + 1) * 64], fw.cst[0:C, o_:o_ + 64], -1.0, ALU.mult)
        bi = 0
        for d in range(2):
            for h in range(4):
                fw.memset(Sst[h], 0.0)
            mi_off = CST_MAP[f'mi64_{d}'][0]
            ms_off = CST_MAP[f'ms64_{d}'][0]
            msT_off = CST_MAP[f'ms64_{1 - d}'][0]
            srcs = {'r': S['B_RT'], 'kd': KD[d], 'kk': S['B_KK'], 'bb': BB[d], 'lw': LW[d]}
            for (t0, n) in blocks_for(d):
                nn = n // C
                cur = {nm: inb[nm][0] for nm in names}
                v, o = vb[bi % 2], ob[bi % 2]
                bi += 1

                def v3(tl):
                    return fv(tl, 0, [[512, 4], [1, n]], 0, 64)
                for nm in names:
                    fw.dma('sp', v3(cur[nm]), srcs[nm][:, t0:t0 + n].re('(h p) t -> p h t', p=64))
                fw.dma('sp', fv(v, 0, [[256, nn], [1, 256]], 0, C), S['B_VS'][t0:t0 + n, :].re('(n c) d -> c n d', c=C))
                l = cur['lw']
                for h in range(4):
                    fw.scan(Gz[:, h * GS + 1: h * GS + 1 + n], ones[:, 0:n], l[:, h * 512: h * 512 + n], 0.0, ALU.mult, ALU.add)
                lastc = C - 1 if d == 0 else 0
                for h in range(4):
                    oc = fv(cum, h * 512, [[C, nn], [1, C]], 0, 64)
                    ox = fv(cumx, h * 512, [[C, nn], [1, C]], 0, 64)
                    if d == 0:
                        g0 = fv(Gz, h * GS, [[C, nn], [0, C]], 0, 64)
                        fw.tt(oc, fv(Gz, h * GS + 1, [[C, nn], [1, C]], 0, 64), g0, ALU.subtract)
                        fw.tt(ox, fv(Gz, h * GS, [[C, nn], [1, C]], 0, 64), g0, ALU.subtract, e='pool')
                    else:
                        g1 = fv(Gz, h * GS + C, [[C, nn], [0, C]], 0, 64)
                        fw.tt(oc, g1, fv(Gz, h * GS, [[C, nn], [1, C]], 0, 64), ALU.subtract)
                        fw.tt(ox, g1, fv(Gz, h * GS + 1, [[C, nn], [1, C]], 0, 64), ALU.subtract, e='pool')
                    fw.tt(fv(d3, h * 512, [[C, nn], [1, C]], 0, 64), oc,
                          fv(cum, h * 512 + lastc, [[C, nn], [0, C]], 0, 64), ALU.subtract)
                fw.act(v3(Ea), v3(cum), AF.Exp)
                fw.tt(v3(Rt), v3(cur['r']), v3(Ea), ALU.mult)
                fw.copy(fv(glast, 0, [[16, 4], [1, nn]], 0, 64), fv(Ea, lastc, [[512, 4], [C, nn]], 0, 64))
                fw.act(v3(Eb), v3(cum), AF.Exp, scale=-1.0)
                fw.tt(v3(Kb), v3(cur['kd']), v3(Eb), ALU.mult)
                fw.tt(v3(Bb), v3(cur['bb']), v3(Eb), ALU.mult, e='pool')
                fw.act(v3(Ea), v3(cumx), AF.Exp)
                fw.tt(v3(Ah), v3(cur['kk']), v3(Ea), ALU.mult)
                fw.act(v3(Eb), v3(d3), AF.Exp, scale=-1.0)
                fw.tt(v3(Kh), v3(cur['kd']), v3(Eb), ALU.mult)
                fw.tt(v3(Bh), v3(cur['bb']), v3(Eb), ALU.mult, e='pool')
                order = range(nn) if d == 0 else range(nn - 1, -1, -1)
                for n_ in order:
                    a = n_ * C

                    def hc(tl, h):
                        return tl[0:64, h * 512 + a: h * 512 + a + C]

                    def msk(off, neg=False):
                        if neg:
                            return fv(negm, d * 64, [[0, 4], [1, C]], 0, C)
                        return fv(fw.cst, off, [[0, 4], [1, C]], 0, C)

                    def m4(tl):
                        return fv(tl, 0, [[C, 4], [1, C]], 0, C)
                    for h in range(4):
                        fw.mm(psb[0][0:C, h * C:(h + 1) * C], hc(Bb, h), hc(Ah, h), inc=(h == 3))
                    fw.tt(m4(U0), m4(psb[0]), msk(ms_off), ALU.mult)
                    for h in range(4):
                        fw.mm(psb[1][0:C, h * C:(h + 1) * C], hc(Ah, h), hc(Bb, h), inc=(h == 3))
                    fw.tt(m4(L0), m4(psb[1]), msk(msT_off), ALU.mult)
                    for h in range(4):
                        fw.mm(psb[2][0:C, h * C:(h + 1) * C], hc(Kb, h), hc(Ah, h), inc=(h == 3))
                    fw.tt(m4(Gm), m4(psb[2]), msk(ms_off), ALU.mult)
                    for h in range(4):
                        fw.mm(psb[3][0:C, h * C:(h + 1) * C], hc(Kb, h), hc(Rt, h), inc=(h == 3))
                    fw.tt(m4(Pm), m4(psb[3]), msk(mi_off), ALU.mult)
                    for h in range(4):
                        fw.mm(psb[4][0:C, h * C:(h + 1) * C], hc(Bb, h), hc(Rt, h), inc=(h == 3))
                    fw.tt(m4(Qm), m4(psb[4]), msk(0, neg=True), ALU.mult)
                    for h in range(4):
                        fw.mm(psb[2][0:C, h * 64:(h + 1) * 64], hc(Ah, h), Sst[h][0:64, :], start=True, stop=False, inc=False)
                        fw.mm(psb[2][0:C, h * 64:(h + 1) * 64], Gm[0:C, h * C:(h + 1) * C],
                              v[0:C, n_ * 256 + h * 64: n_ * 256 + (h + 1) * 64], start=False, stop=True, inc=(h == 3))
                    fw.copy(Yt[0:C, :], psb[2][0:C, 0:256], e='act')
                    neumann_solve(fw, U0, L0, Yt, UL, (psb[5], psb[6], psb[7]))
                    for h in range(4):
                        fw.mm(psb[3][0:C, h * 64:(h + 1) * 64], hc(Rt, h), Sst[h][0:64, :], start=True, stop=False, inc=False)
                        fw.mm(psb[3][0:C, h * 64:(h + 1) * 64], Pm[0:C, h * C:(h + 1) * C],
                              v[0:C, n_ * 256 + h * 64: n_ * 256 + (h + 1) * 64], start=False, stop=False, inc=False)
                        fw.mm(psb[3][0:C, h * 64:(h + 1) * 64], Qm[0:C, h * C:(h + 1) * C],
                              Yt[0:C, h * 64:(h + 1) * 64], start=False, stop=True, inc=(h == 3))
                    fw.copy(o[0:C, n_ * 256:(n_ + 1) * 256], psb[3][0:C, 0:256], e='act')
                    for h in range(4):
                        fw.tr(psb[4][0:C, h * 64:(h + 1) * 64], hc(Kh, h), ident[0:64, 0:64])
                        fw.tr(psb[4][0:C, 256 + h * 64: 256 + (h + 1) * 64], hc(Bh, h), ident[0:64, 0:64])
                    fw.copy(KBt[0:C, 0:256], psb[4][0:C, 0:256], e='act')
                    fw.ts(KBt[0:C, 256:512], psb[4][0:C, 256:512], -1.0, ALU.mult)
                    for h in range(4):
                        fw.mm(psb[0][0:64, h * 64:(h + 1) * 64], KBt[0:C, h * 64:(h + 1) * 64],
                              v[0:C, n_ * 256 + h * 64: n_ * 256 + (h + 1) * 64], start=True, stop=False, inc=False)
                        fw.mm(psb[0][0:64, h * 64:(h + 1) * 64], KBt[0:C, 256 + h * 64: 256 + (h + 1) * 64],
                              Yt[0:C, h * 64:(h + 1) * 64], start=False, stop=True)
                        fw.stt(Sst[h][0:64, :], Sst[h][0:64, :], glast[0:64, h * 16 + n_: h * 16 + n_ + 1],
                               psb[0][0:64, h * 64:(h + 1) * 64], ALU.mult, ALU.add)
                fw.dma('pool', Od[d][t0:t0 + n, :].re('(n c) d -> c n d', c=C), fv(o, 0, [[256, nn], [1, 256]], 0, C))
        fw.barrier()
    if stop <= 2:
        return
    with ExitStack() as st:
        gb = fw.sb([128, 256], name='rf_gb', stack=st)
        bbt = fw.sb([128, 256], name='rf_bb', stack=st)
        fw.dma('sp', gb, dram_bcast_rows(L['rwkv_ln_g'], 0, 256))
        fw.dma('sp', bbt, dram_bcast_rows(L['rwkv_ln_b'], 0, 256))
        o0 = [fw.sb([128, 256], name=f'rf_o0{i}', stack=st) for i in range(2)]
        o1 = [fw.sb([128, 256], name=f'rf_o1{i}', stack=st) for i in range(2)]
        gt = [fw.sb([128, 256], name=f'rf_g{i}', stack=st) for i in range(2)]
        vs = [fw.sb([128, 256], name=f'rf_v{i}', stack=st) for i in range(2)]
        sb_ = [fw.sb([128, 4], name=f'rf_sb{i}', stack=st) for i in range(2)]
        yy = [fw.sb([128, 256], name=f'rf_y{i}', stack=st) for i in range(2)]
        sq = fw.sb([128, 256], name='rf_sq', stack=st)
        mu_ = fw.sb([128, 4], name='rf_mu', stack=st)
        var = fw.sb([128, 4], name='rf_var', stack=st)
        for ti in range(34):
            rs = slice(ti * 128, (ti + 1) * 128)
            a0, a1, g_, v_, s_, y_ = o0[ti % 2], o1[ti % 2], gt[ti % 2], vs[ti % 2], sb_[ti % 2], yy[ti % 2]
            fw.dma('sp', a0, Od[0][rs, :])
            fw.dma('sp', a1, Od[1][rs, :])
            fw.dma('sp', g_, S['B_GATE'][rs, :])
            fw.dma('sp', v_, S['B_VS'][rs, :])
            fw.dma('sp', s_, S['B_SB'][rs, :])
            fw.tt(a0, a0, a1, ALU.add)
            fw.reduce(mu_, fv(a0, 0, [[64, 4], [1, 64]]), ALU.add)
            fw.ts(mu_, mu_, 1.0 / 64, ALU.mult)
            fw.tt(fv(a0, 0, [[64, 4], [1, 64]]), fv(a0, 0, [[64, 4], [1, 64]]), fv(mu_, 0, [[1, 4], [0, 64]]), ALU.subtract)
            fw.tt(sq, a0, a0, ALU.mult, e='pool')
            fw.reduce(var, fv(sq, 0, [[64, 4], [1, 64]]), ALU.add)
            fw.ts(var, var, 1.0 / 64, ALU.mult, s2=RWKV_GN_EPS, op1=ALU.add)
            fw.act(var, var, AF.Sqrt)
            fw.recip(var, var)
            fw.tt(fv(y_, 0, [[64, 4], [1, 64]]), fv(a0, 0, [[64, 4], [1, 64]]), fv(var, 0, [[1, 4], [0, 64]]), ALU.mult)
            fw.tt(y_, y_, gb, ALU.mult, e='pool')
            fw.tt(y_, y_, bbt, ALU.add)
            fw.tt(fv(v_, 0, [[64, 4], [1, 64]]), fv(v_, 0, [[64, 4], [1, 64]]), fv(s_, 0, [[1, 4], [0, 64]]), ALU.mult, e='pool')
            fw.tt(y_, y_, v_, ALU.add)
            fw.tt(y_, y_, g_, ALU.mult)
            store_rows_scan(fw, Y, 256, 256, y_[:, :], ti, colmajor)
        fw.barrier()


SCRATCH = {
    'fm_A_q': [256, T], 'fm_A_f': [512, T], 'fm_B_all': [1152, T], 'fm_D_qk': [512, T], 'fm_D_gates': [16, T],
    'tm_A_ig': [T, 512], 'tm_B_v': [T, 256], 'tm_C': [T, 1024], 'tm_C_ba': [T, 16], 'tm_D_vo': [T, 512], 'tm_D_gates': [T, 16],
    'A_KT0': [256, T], 'A_KT1': [256, T], 'A_LF0': [256, T], 'A_LF1': [256, T], 'A_O0': [T, 256], 'A_O1': [T, 256],
    'B_LW0': [256, T], 'B_LW1': [256, T], 'B_AA0': [256, T], 'B_AA1': [256, T], 'B_KD0': [256, T], 'B_KD1': [256, T],
    'B_BB0': [256, T], 'B_BB1': [256, T], 'B_RT': [256, T], 'B_KK': [256, T], 'B_VS': [T, 256], 'B_GATE': [T, 256], 'B_SB': [T, 4],
    'B_O0': [T, 256], 'B_O1': [T, 256],
    'C_QT': [256, T], 'C_KT': [256, T], 'C_KV': [T, 512], 'C_G': [T, 16], 'C_O0': [T, 256], 'C_O1': [T, 256],
    'D_LF0': [256, T], 'D_LF1': [256, T], 'D_VX0': [T, 260], 'D_VX1': [T, 260], 'D_O0': [T, 260], 'D_O1': [T, 260],
    'Y': [T, 1024], 'X1': [T, 1024],
}


def host_inputs_for_layer(inp, l, b):
    cc = np.stack([inp['c_ctx'], inp['c'][b]], 0)
    d = {}
    d['c2T'] = np.ascontiguousarray(cc.T.reshape(8, 128, 2).transpose(1, 0, 2).reshape(128, 16))
    d['ada_w'] = inp['ada_w'][l]
    ab = inp['ada_b'][l]
    d['ada_b'] = ab.reshape(1, 6144)
    d['adab_col'] = np.ascontiguousarray(ab.reshape(48, 128).T)
    win = inp['w_in'][l]
    d['w_fm'] = np.ascontiguousarray(win[:, FM_COLS])
    d['w_tm'] = np.ascontiguousarray(win[:, TM_COLS])
    d['ln'] = np.stack([inp[k][l] for k in ('ln1_g', 'ln1_b', 'ln2_g', 'ln2_b')], 0)
    d['w_out'] = inp['w_out'][l]
    d['mlp_w1'] = inp['mlp_w1'][l]
    d['mlp_w2'] = inp['mlp_w2'][l]
    gam = inp['hgrn_gamma']
    d['hgrn_gamma_t'] = np.ascontiguousarray(gam.reshape(4, 2, 2, 128).transpose(3, 1, 2, 0).reshape(128, 16))
    d['hgrn_norm_g'] = inp['hgrn_norm_g'][l].reshape(1, 256)
    fb = np.zeros((16, 1), np.float32)
    fb[8:, 0] = inp['mlstm_f_bias'][l].reshape(8)
    d['mlstm_fb_col'] = fb
    d['mlstm_i_bias'] = inp['mlstm_i_bias'][l].reshape(1, 8)
    d['mlstm_norm_g'] = inp['mlstm_norm_g'][l].reshape(1, 256)
    d['gdn_conv'] = inp['gdn_conv'][l].reshape(1, 5 * 768)
    d['gdn_dt_bias'] = inp['gdn_dt_bias'][l].reshape(1, 8)
    d['gdn_a_log'] = inp['gdn_a_log'][l].reshape(1, 8)
    d['gdn_norm_g'] = inp['gdn_norm_g'][l].reshape(1, 256)
    pc = np.zeros((128, 14), np.float32)

    def c2(v):
        return v.reshape(2, 128).T
    pc[:, 0:2] = c2(inp['rwkv_k_k'][l])
    pc[:, 2:4] = c2(inp['rwkv_k_a'][l])
    pc[:, 4:6] = c2(inp['rwkv_r_k'][l].reshape(256))
    pc[:, 6:8] = c2(inp['rwkv_w0'][l][0])
    pc[:, 8:10] = c2(inp['rwkv_w0'][l][1])
    pc[:, 10:12] = c2(inp['rwkv_a0'][l][0])
    pc[:, 12:14] = c2(inp['rwkv_a0'][l][1])
    d['rwkv_pcol'] = pc
    mu = inp['rwkv_mu'][l]
    d['rwkv_mu128'] = np.ascontiguousarray(mu.reshape(9, 128).T)
    d['rwkv_mu64'] = np.ascontiguousarray(mu.reshape(18, 64).T)
    d['rwkv_mu_row'] = mu.reshape(1, 1152)
    d['rwkv_w2'] = inp['rwkv_w2'][l]
    d['rwkv_a2'] = inp['rwkv_a2'][l]
    d['rwkv_g2'] = inp['rwkv_g2'][l]
    d['rwkv_ln_g'] = inp['rwkv_ln_g'][l].reshape(1, 256)
    d['rwkv_ln_b'] = inp['rwkv_ln_b'][l].reshape(1, 256)
    return {k: np.ascontiguousarray(v, dtype=np.float32) for k, v in d.items()}


LAYER_INPUT_SHAPES = None


def layer_input_shapes():
    global LAYER_INPUT_SHAPES
    if LAYER_INPUT_SHAPES is None:
        fake = {
            'c_ctx': np.zeros(1024, np.float32), 'c': np.zeros((1, 1024), np.float32),
            'ada_w': np.zeros((1, 1024, 6144), np.float32), 'ada_b': np.zeros((1, 6144), np.float32),
            'w_in': np.zeros((1, 1024, 4512), np.float32), 'w_out': np.zeros((1, 1024, 1024), np.float32),
            'ln1_g': np.zeros((1, 1024), np.float32), 'ln1_b': np.zeros((1, 1024), np.float32),
            'ln2_g': np.zeros((1, 1024), np.float32), 'ln2_b': np.zeros((1, 1024), np.float32),
            'mlp_w1': np.zeros((1, 1024, 4096), np.float32), 'mlp_w2': np.zeros((1, 4096, 1024), np.float32),
            'hgrn_gamma': np.zeros((4, 2, 256), np.float32), 'hgrn_norm_g': np.zeros((1, 256), np.float32),
            'rwkv_mu': np.zeros((1, 1152), np.float32), 'rwkv_w0': np.zeros((1, 2, 256), np.float32),
            'rwkv_w2': np.zeros((1, 2, 64, 256), np.float32), 'rwkv_a0': np.zeros((1, 2, 256), np.float32),
            'rwkv_a2': np.zeros((1, 2, 64, 256), np.float32), 'rwkv_g2': np.zeros((1, 128, 256), np.float32),
            'rwkv_k_k': np.zeros((1, 256), np.float32), 'rwkv_k_a': np.zeros((1, 256), np.float32),
            'rwkv_r_k': np.zeros((1, 4, 64), np.float32), 'rwkv_ln_g': np.zeros((1, 256), np.float32),
            'rwkv_ln_b': np.zeros((1, 256), np.float32), 'gdn_conv': np.zeros((1, 5, 768), np.float32),
            'gdn_a_log': np.zeros((1, 2, 4), np.float32), 'gdn_dt_bias': np.zeros((1, 2, 4), np.float32),
            'gdn_norm_g': np.zeros((1, 256), np.float32), 'mlstm_i_bias': np.zeros((1, 2, 4), np.float32),
            'mlstm_f_bias': np.zeros((1, 2, 4), np.float32), 'mlstm_norm_g': np.zeros((1, 256), np.float32),
        }
        LAYER_INPUT_SHAPES = {k: list(v.shape) for k, v in host_inputs_for_layer(fake, 0, 0).items()}
    return LAYER_INPUT_SHAPES


def build_program(layers, final_last):
    nc = bass.Bass("TRN2", target_bir_lowering=False)
    with ExitStack() as st, nc.allow_low_precision(reason="bf16 matmul operands, fp32 accumulate"):
        fw = FW(nc, st)
        fw._ps_stack = None
        set_psum(fw, 'full')
        fw.ident = fw.sb([128, 128], name='ident', stack=st)
        fw.memset(fw.ident, 1.0)
        fw.op('pool', lambda: nc.gpsimd.affine_select(out=fw.ident.ap(), in_=fw.ident.ap(), pattern=[[-1, 128]],
              compare_op=ALU.is_equal, fill=0.0, base=0, channel_multiplier=1), [fw.ident], [fw.ident])
        consts = fw.dram('consts', [128, NCST], F32, kind='ExternalInput')
        fw.cst = fw.sb([128, NCST], name='cst', stack=st)
        fw.dma('sp', fw.cst, consts)
        x_ext = fw.dram('x_in', [T, 1024], F32, kind='ExternalInput')
        if final_last:
            out_t = fw.dram('out', [NLAT, 1024], F32, kind='ExternalOutput')
        else:
            out_t = fw.dram('out', [T, 1024], F32, kind='ExternalOutput')
        scr = {k: fw.dram('s_' + k, shp, F32) for k, shp in SCRATCH.items()}
        S = dict(scr)
        S['tm_C_qkv0'] = scr['tm_C'][:, 0:512]
        S['tm_C_qkv1'] = scr['tm_C'][:, 512:1024]
        xpp = [fw.dram(f'xpp{i}', [T, 1024], F32) for i in range(2)] if len(layers) > 1 else []
        shapes = layer_input_shapes()
        x_cur = x_ext
        for li, layer in enumerate(layers):
            is_last_in_prog = (li == len(layers) - 1)
            last = final_last and is_last_in_prog
            L = {k: fw.dram(f'{k}_l{layer}', shp, F32, kind='ExternalInput') for k, shp in shapes.items()}
            cm = set('AB') if layer % 2 == 1 else set('CD')
            with ExitStack() as lst:
                P0 = phase0(fw, L, lst)
                phase1(fw, L, P0, S, cm, x_cur)
                hgrn2(fw, L, S, scr['Y'], layer, 'A' in cm)
                rwkv7(fw, L, S, scr['Y'], layer, 'B' in cm)
                gdn(fw, L, S, scr['Y'], layer, 'C' in cm)
                mlstm(fw, L, S, scr['Y'], layer, 'D' in cm)
                if is_last_in_prog:
                    x_next = out_t
                else:
                    x_next = xpp[li % 2]
                phase3(fw, L, P0, scr['Y'], x_cur, scr['X1'], x_next, last, out_lat=(out_t if last else None))
                fw.barrier()
            x_cur = x_next
        fw.finish()
    return nc


_PROG_CACHE = {}


def get_program(layers, final_last):
    key = (tuple(layers), final_last)
    if key not in _PROG_CACHE:
        _PROG_CACHE[key] = build_program(layers, final_last)
    return _PROG_CACHE[key]


LAUNCH_GROUPS = [[0], [1], [2], [3]]


def kernel(**inputs):
    inp = {k: np.asarray(v, dtype=np.float32) for k, v in inputs.items()}
    B = inp['x'].shape[0]
    ncores = 8
    xs = [np.ascontiguousarray(np.concatenate([inp['ctx'][b], inp['x'][b]], 0)) for b in range(B)]
    out = None
    for gi, layers in enumerate(LAUNCH_GROUPS):
        final_last = layers[-1] == 3
        nc = get_program(layers, final_last)
        in_maps = []
        for core in range(ncores):
            b = core % B
            m = {'consts': CST_ARR, 'x_in': xs[b]}
            for layer in layers:
                for k, v in host_inputs_for_layer(inp, layer, b).items():
                    m[f'{k}_l{layer}'] = v
            in_maps.append(m)
        res = run_bass_kernel_spmd(nc, in_maps, core_ids=list(range(ncores)))
        outs = [np.asarray(res.results[b]['out']) for b in range(B)]
        if final_last:
            out = np.stack(outs, 0).astype(np.float32)
        else:
            xs = outs
    return out
```

```python
from contextlib import ExitStack
import math
import numpy as np
import concourse.bass as bass
import concourse.mybir as mybir
from concourse.bass_utils import run_bass_kernel_spmd

F32 = mybir.dt.float32
BF16 = mybir.dt.bfloat16
AF = mybir.ActivationFunctionType
ALU = mybir.AluOpType
AX = mybir.AxisListType


class Tl:
    def __init__(self, fw, t, kind):
        self.fw = fw
        self.t = t
        self.kind = kind
        self.w = None
        self.r = {}
        self.dsem = None
        self.dcnt = 0
        self.dw = 0
        self.dr = 0
        self.dram_w = {}
        self.dram_r = {}
        self.name = None
        self.dent = None
        self.dkey = None
        self.dbase = 0
        self.is_output = False

    def ap(self):
        return self.t.ap() if self.kind == 'dr' else self.t[:]

    def __getitem__(self, k):
        a = self.t.ap()[k] if self.kind == 'dr' else self.t[k]
        return V(self, a)


class V:
    def __init__(self, tl, ap):
        self.tl = tl
        self.ap = ap

    def __getitem__(self, k):
        return V(self.tl, self.ap[k])

    def re(self, s, **kw):
        return V(self.tl, self.ap.rearrange(s, **kw))

    def bc(self, shape):
        return V(self.tl, self.ap.to_broadcast(shape))


def _v(x):
    if isinstance(x, Tl):
        return V(x, x.ap())
    return x


class Eng:
    def __init__(self, name, e, sem):
        self.name = name
        self.e = e
        self.sem = sem
        self.cnt = 0
        self.seen = {}
        self.seen_d = {}


class FW:
    def __init__(self, nc, stack):
        self.nc = nc
        self.stack = stack
        self.E = {}
        for name, e in (('pe', nc.tensor), ('act', nc.scalar), ('dve', nc.vector),
                        ('pool', nc.gpsimd), ('sp', nc.sync)):
            sem = stack.enter_context(nc.semaphore('s_' + name))
            self.E[name] = Eng(name, e, sem)
        self.ntile = 0
        self.out_dmas = []
        self.dactive = {}
        self.dfree = []
        self.dall = []
        self.dwaited = {}

    def _attach_dsem(self, tl):
        nm = tl.name
        if nm not in self.dactive:
            if self.dfree:
                ent = self.dfree.pop()
            else:
                sem = self.stack.enter_context(self.nc.semaphore('d_%d' % len(self.dall)))
                ent = [sem, 0, len(self.dall)]
                self.dall.append(ent)
            self.dactive[nm] = ent
        ent = self.dactive[nm]
        tl.dent = ent
        tl.dsem = ent[0]
        tl.dcnt = ent[1]
        tl.dbase = ent[1]
        tl.dkey = ent[2]

    def barrier(self):
        sp = self.E['sp']
        for n, e in self.E.items():
            if n != 'sp' and e.cnt and sp.seen.get(n, 0) < e.cnt:
                sp.e.wait_ge(e.sem, e.cnt)
                sp.seen[n] = e.cnt
        for ent in self.dall:
            sem, cnt, key = ent
            if cnt and self.dwaited.get(key, 0) < cnt:
                sp.e.wait_ge(sem, 16 * cnt)
                self.dwaited[key] = cnt
        sp.e.sem_inc(sp.sem, 1)
        sp.cnt += 1
        for n, e in self.E.items():
            if n != 'sp':
                e.e.wait_ge(sp.sem, sp.cnt)
                e.seen['sp'] = sp.cnt
                for m, o in self.E.items():
                    if m != 'sp':
                        e.seen[m] = max(e.seen.get(m, 0), o.cnt)
        for nm, ent in self.dactive.items():
            self.dfree.append(ent)
        self.dactive = {}

    def sb(self, shape, dt=F32, name=None, stack=None):
        self.ntile += 1
        name = (name or 't') + f'_{self.ntile}'
        t = (stack or self.stack).enter_context(self.nc.sbuf_tensor(name, list(shape), dt))
        tl = Tl(self, t, 'sb')
        tl.name = name
        return tl

    def ps(self, shape, dt=F32, name=None, stack=None):
        self.ntile += 1
        name = (name or 'p') + f'_{self.ntile}'
        t = (stack or self.stack).enter_context(self.nc.psum_tensor(name, list(shape), dt))
        tl = Tl(self, t, 'ps')
        tl.name = name
        return tl

    def dram(self, name, shape, dt=F32, kind='Internal'):
        t = self.nc.dram_tensor(name, list(shape), dt, kind=kind)
        tl = Tl(self, t, 'dr')
        tl.is_output = (kind == 'ExternalOutput')
        return tl

    def _need(self, eng, reads, writes):
        waits = {}
        dwaits = {}

        def add(w):
            if w is None:
                return
            n, c = w
            if n == eng.name and n == 'pe':
                return
            if waits.get(n, 0) < c:
                waits[n] = c

        for v in reads:
            tl = v.tl
            add(tl.w)
            if tl.dw:
                dwaits[tl] = max(dwaits.get(tl, 0), tl.dw)
        for v in writes:
            tl = v.tl
            add(tl.w)
            for n, c in tl.r.items():
                add((n, c))
            if tl.dsem is not None and tl.dcnt > tl.dbase:
                dwaits[tl] = max(dwaits.get(tl, 0), tl.dcnt)
        return waits, dwaits

    def _emit_waits(self, eng, waits, dwaits):
        for n, c in waits.items():
            if eng.seen.get(n, 0) >= c:
                continue
            eng.e.wait_ge(self.E[n].sem, c)
            eng.seen[n] = c
        for tl, c in dwaits.items():
            k = tl.dkey
            if eng.seen_d.get(k, 0) >= c:
                continue
            eng.e.wait_ge(tl.dsem, 16 * c)
            eng.seen_d[k] = c

    def op(self, engname, fn, reads, writes, inc=True):
        eng = self.E[engname]
        reads = [_v(x) for x in reads if x is not None and not isinstance(x, (int, float))]
        writes = [_v(x) for x in writes]
        waits, dwaits = self._need(eng, reads, writes)
        self._emit_waits(eng, waits, dwaits)
        ins = fn()
        c = eng.cnt + 1
        if inc:
            ins.then_inc(eng.sem, 1)
            eng.cnt = c
        for v in writes:
            v.tl.w = (engname, c)
            v.tl.r = {}
        for v in reads:
            if v.tl.r.get(engname, 0) < c:
                v.tl.r[engname] = c
        return ins

    def dma(self, qname, out, in_):
        q = self.E[qname]
        out = _v(out)
        in_ = _v(in_)
        o_tl, i_tl = out.tl, in_.tl
        sb_tl = o_tl if o_tl.kind != 'dr' else i_tl
        assert sb_tl.kind != 'dr'
        if sb_tl.dsem is None or self.dactive.get(sb_tl.name) is not sb_tl.dent:
            self._attach_dsem(sb_tl)
            sb_tl.dw = 0
            sb_tl.dr = 0
        waits = {}
        dwaits = {}
        semwaits = {}

        def add(w):
            if w is None:
                return
            n, c = w
            if waits.get(n, 0) < c:
                waits[n] = c

        if o_tl.kind != 'dr':
            add(o_tl.w)
            for n, c in o_tl.r.items():
                add((n, c))
            if o_tl.dr:
                dwaits[o_tl] = o_tl.dr
        else:
            for k, (s, val) in list(o_tl.dram_r.items()) + list(o_tl.dram_w.items()):
                if semwaits.get(k, (None, 0))[1] < val:
                    semwaits[k] = (s, val)
        if i_tl.kind != 'dr':
            add(i_tl.w)
            if i_tl.dw:
                dwaits[i_tl] = max(dwaits.get(i_tl, 0), i_tl.dw)
        else:
            for k, (s, val) in i_tl.dram_w.items():
                if semwaits.get(k, (None, 0))[1] < val:
                    semwaits[k] = (s, val)
        self._emit_waits(q, waits, dwaits)
        for k, (s, val) in semwaits.items():
            if q.seen_d.get(k, 0) >= val:
                continue
            q.e.wait_ge(s, 16 * val)
            q.seen_d[k] = val
        ins = q.e.dma_start(out=out.ap, in_=in_.ap)
        sb_tl.dcnt += 1
        sb_tl.dent[1] = sb_tl.dcnt
        ins.then_inc(sb_tl.dsem, 16)
        key = sb_tl.dkey
        if o_tl.kind != 'dr':
            o_tl.dw = o_tl.dcnt
            o_tl.w = None
            o_tl.r = {}
            if i_tl.kind == 'dr':
                i_tl.dram_r[key] = (sb_tl.dsem, sb_tl.dcnt)
        else:
            i_tl.dr = i_tl.dcnt
            o_tl.dram_w[key] = (sb_tl.dsem, sb_tl.dcnt)
            o_tl.dram_r = {}
            if o_tl.is_output:
                self.out_dmas.append((sb_tl.dsem, sb_tl.dcnt))
        return ins

    def finish(self):
        self.barrier()
        sp = self.E['sp']
        for n, e in self.E.items():
            if n != 'sp' and e.cnt:
                if sp.seen.get(n, 0) < e.cnt:
                    sp.e.wait_ge(e.sem, e.cnt)
        for s, val in self.out_dmas:
            sp.e.wait_ge(s, 16 * val)
        for n in ('pool',):
            for s, val in self.out_dmas:
                self.E[n].e.wait_ge(s, 16 * val)

    def mm(self, out, lhsT, rhs, start=True, stop=True, inc=True):
        out, lhsT, rhs = _v(out), _v(lhsT), _v(rhs)
        return self.op('pe', lambda: self.nc.tensor.matmul(out.ap, lhsT.ap, rhs.ap, start=start, stop=stop),
                       [lhsT, rhs], [out], inc=inc)

    def tr(self, out, in_, ident):
        out, in_, ident = _v(out), _v(in_), _v(ident)
        return self.op('pe', lambda: self.nc.tensor.transpose(out.ap, in_.ap, ident.ap), [in_, ident], [out])

    def act(self, out, in_, func, bias=None, scale=1.0, accum=None):
        out, in_ = _v(out), _v(in_)
        rd = [in_]
        kw = {}
        if bias is not None:
            if isinstance(bias, (int, float)):
                kw['bias'] = float(bias)
            else:
                bias = _v(bias)
                rd.append(bias)
                kw['bias'] = bias.ap
        if isinstance(scale, (int, float)):
            kw['scale'] = float(scale)
        else:
            scale = _v(scale)
            rd.append(scale)
            kw['scale'] = scale.ap
        wr = [out]
        if accum is not None:
            accum = _v(accum)
            wr.append(accum)
            kw['accum_out'] = accum.ap
        return self.op('act', lambda: self.nc.scalar.activation(out=out.ap, in_=in_.ap, func=func, **kw), rd, wr)

    def _veng(self, e):
        return self.nc.vector if e == 'dve' else self.nc.gpsimd

    def tt(self, out, a, b, op, e='dve'):
        out, a, b = _v(out), _v(a), _v(b)
        return self.op(e, lambda: self._veng(e).tensor_tensor(out=out.ap, in0=a.ap, in1=b.ap, op=op), [a, b], [out])

    def ts(self, out, a, s1, op0, s2=None, op1=None, e='dve', accum=None):
        out, a = _v(out), _v(a)
        rd = [a]

        def cv(s):
            if s is None or isinstance(s, (int, float)):
                return None if s is None else float(s)
            s = _v(s)
            rd.append(s)
            return s.ap
        s1a = cv(s1)
        s2a = cv(s2)
        kw = {}
        wr = [out]
        if op1 is not None:
            kw['op1'] = op1
        if accum is not None:
            accum = _v(accum)
            wr.append(accum)
            kw['accum_out'] = accum.ap
        return self.op(e, lambda: self._veng(e).tensor_scalar(out=out.ap, in0=a.ap, scalar1=s1a, scalar2=s2a, op0=op0, **kw), rd, wr)

    def stt(self, out, a, s, b, op0, op1, accum=None):
        out, a, b = _v(out), _v(a), _v(b)
        rd = [a, b]
        if isinstance(s, (int, float)):
            sa = float(s)
        else:
            s = _v(s)
            rd.append(s)
            sa = s.ap
        kw = {}
        wr = [out]
        if accum is not None:
            accum = _v(accum)
            wr.append(accum)
            kw['accum_out'] = accum.ap
        return self.op('dve', lambda: self.nc.vector.scalar_tensor_tensor(out=out.ap, in0=a.ap, scalar=sa, in1=b.ap, op0=op0, op1=op1, **kw), rd, wr)

    def copy(self, out, in_, e='dve'):
        out, in_ = _v(out), _v(in_)
        if e == 'act':
            return self.act(out, in_, AF.Copy)
        return self.op(e, lambda: self._veng(e).tensor_copy(out=out.ap, in_=in_.ap), [in_], [out])

    def memset(self, out, val, e='dve'):
        out = _v(out)
        return self.op(e, lambda: self._veng(e).memset(out.ap, val), [], [out])

    def reduce(self, out, in_, op, axis=AX.X, e='dve'):
        out, in_ = _v(out), _v(in_)
        return self.op(e, lambda: self._veng(e).tensor_reduce(out=out.ap, in_=in_.ap, axis=axis, op=op), [in_], [out])

    def recip(self, out, in_):
        out, in_ = _v(out), _v(in_)
        return self.op('dve', lambda: self.nc.vector.reciprocal(out=out.ap, in_=in_.ap), [in_], [out])

    def scan(self, out, d0, d1, init, op0, op1):
        out, d0, d1 = _v(out), _v(d0), _v(d1)
        rd = [d0, d1]
        if isinstance(init, (int, float)):
            ia = float(init)
        else:
            init = _v(init)
            rd.append(init)
            ia = init.ap
        return self.op('dve', lambda: self.nc.vector.tensor_tensor_scan(out=out.ap, data0=d0.ap, data1=d1.ap, initial=ia, op0=op0, op1=op1), rd, [out])

from contextlib import ExitStack
import math

T = 4352
NCTX = 256
NLAT = 4096
D = 1024
DFF = 4096
ALPHA = 8.0 ** 0.25
LN_EPS = 1e-5

_A0, _B0, _C0, _D0 = 0, 1280, 2432, 3472


def _r(a, b):
    return list(range(a, b))


FM_BLOCKS = [
    ('A', 'q', _r(_A0 + 0, _A0 + 256)),
    ('A', 'f', _r(_A0 + 512, _A0 + 1024)),
    ('B', 'all', _r(_B0, _B0 + 1152)),
    ('D', 'qk', _r(_D0, _D0 + 512)),
    ('D', 'gates', _r(_D0 + 768, _D0 + 784)),
]
TM_BLOCKS = [
    ('A', 'ig', _r(_A0 + 256, _A0 + 512) + _r(_A0 + 1024, _A0 + 1280)),
    ('B', 'v', _r(_B0 + 512, _B0 + 768)),
    ('C', 'qkv0', _r(_C0, _C0 + 512)),
    ('C', 'qkv1', _r(_C0 + 512, _C0 + 768) + _r(_C0 + 784, _C0 + 1040)),
    ('C', 'ba', _r(_C0 + 768, _C0 + 784)),
    ('D', 'vo', _r(_D0 + 512, _D0 + 768) + _r(_D0 + 784, _D0 + 1040)),
    ('D', 'gates', _r(_D0 + 768, _D0 + 784)),
]
FM_COLS = sum((b[2] for b in FM_BLOCKS), [])
TM_COLS = sum((b[2] for b in TM_BLOCKS), [])
NFM = len(FM_COLS)
NTM = len(TM_COLS)


def fm_layout():
    out = []
    off = 0
    for g, nm, cols in FM_BLOCKS:
        out.append((g, f'fm_{g}_{nm}', off, len(cols)))
        off += len(cols)
    return out


def tm_layout():
    out = []
    off = 0
    for g, nm, cols in TM_BLOCKS:
        out.append((g, f'tm_{g}_{nm}', off, len(cols)))
        off += len(cols)
    return out


def scan_rows(order, s0, n):
    if order == 'r':
        return [(0, n, NCTX + s0, 1)]
    segs = []
    assert s0 % 64 == 0 and n % 64 == 0
    for i in range(n // 64):
        c = s0 // 64 + i
        segs.append((i * 64, 64, NCTX + c, 64))
    return segs


def rows_ap(fwk, dr, row_start, row_step, nrows, c0, c1):
    a = dr.ap()
    ncols_total = a.shape[1]
    ap = bass.AP(a.tensor, a.offset + row_start * ncols_total + c0, [[row_step * ncols_total, nrows], [1, c1 - c0]])
    return V(dr, ap)


def bcast_last(v, n):
    a = v.ap
    return V(v.tl, bass.AP(a.tensor, a.offset, [list(x) for x in a.ap] + [[0, n]]))


def dram_bcast_rows(dr, off, n):
    a = dr.ap()
    return V(dr, bass.AP(a.tensor, a.offset + off, [[0, 128], [1, n]]))


def layer_norm(fw, t, out, gb, bb, sc):
    fw.act(out, t, AF.Identity, accum=sc['s1'])
    fw.act(out, t, AF.Square, accum=sc['s2'])
    fw.ts(sc['mean'], sc['s1'], 1.0 / 1024, ALU.mult)
    fw.tt(sc['m2'], sc['mean'], sc['mean'], ALU.mult)
    fw.stt(sc['var'], sc['s2'], 1.0 / 1024, sc['m2'], ALU.mult, ALU.subtract)
    fw.ts(sc['var'], sc['var'], LN_EPS, ALU.add)
    fw.act(sc['std'], sc['var'], AF.Sqrt)
    fw.recip(sc['rstd'], sc['std'])
    fw.stt(sc['nmr'], sc['mean'], -1.0, sc['rstd'], ALU.mult, ALU.mult)
    fw.act(out, t, AF.Identity, bias=sc['nmr'], scale=sc['rstd'])
    fw.tt(out, out, gb, ALU.mult, e='pool')
    fw.tt(out, out, bb, ALU.add)


def ln_scratch(fw, st, pfx):
    sc = {k: fw.sb([128, 1], name=f'{pfx}_{k}', stack=st) for k in ('s1', 's2', 'mean', 'm2', 'var', 'std', 'rstd', 'nmr')}
    return sc


def load_cast_weight(fw, st, wbf3, wdram, K, ncols, pfx, chunk=256):
    with ExitStack() as st2:
        stg = [fw.sb([128, K * chunk], name=f'{pfx}_stg{i}', stack=st2) for i in range(2)]
        engs = ['pool', 'dve', 'act']
        i = 0
        for c0 in range(0, ncols, chunk):
            w = min(chunk, ncols - c0)
            s = stg[i % 2]
            s3 = s[:, 0:K * w].re('p (k c) -> p k c', k=K)
            fw.dma('sp', s3, wdram[:, c0:c0 + w].re('(k p) c -> p k c', p=128))
            fw.copy(wbf3[:, :, c0:c0 + w], s3, e=engs[i % 3])
            i += 1
        fw.barrier()


def phase0(fw, L, lst):
    nc = fw.nc
    psb = fw.psb
    modc = fw.sb([128, 64], name='modc', stack=lst)
    modc4 = modc[:, :].re('p (i f w) -> p i f w', i=4, f=8)
    gbc = [[fw.sb([128, 1024], name=f'gbc{w}{g}', stack=lst) for g in range(2)] for w in range(2)]
    with ExitStack() as ps:
        c2 = fw.sb([128, 16], name='c2', stack=ps)
        sc = fw.sb([128, 16], name='sc', stack=ps)
        sc3 = sc[:, :].re('p (k w) -> p k w', w=2)
        scb = fw.sb([128, 16 * 128], name='scb', stack=ps)
        scb4 = scb[:, :].re('p (k w m) -> p k w m', k=8, w=2)
        adab = fw.sb([128, 48], name='adab', stack=ps)
        adabb = fw.sb([128, 2048], name='adabb', stack=ps)
        wst = [fw.sb([128, 8192], name=f'adaw{i}', stack=ps) for i in range(2)]
        fw.dma('sp', c2, L['c2T'])
        fw.dma('sp', adab, L['adab_col'])
        fw.dma('sp', adabb[:, 0:1024], dram_bcast_rows(L['ada_b'], 2048, 1024))
        fw.dma('sp', adabb[:, 1024:2048], dram_bcast_rows(L['ada_b'], 5120, 1024))
        fw.act(sc, c2, AF.Silu)
        for k in range(8):
            for w in range(2):
                fw.copy(scb4[:, k, w, :], sc3[:, k, w:w + 1].bc([128, 128]))
        for blk in range(6):
            ws = wst[blk % 2]
            ws3 = ws[:, :].re('p (k c) -> p k c', k=8)
            fw.dma('sp', ws3, L['ada_w'][:, blk * 1024:(blk + 1) * 1024].re('(k p) c -> p k c', p=128))
            if blk in (0, 1, 3, 4):
                idx = {0: 0, 1: 1, 3: 2, 4: 3}[blk]
                p = psb[0]
                for fc in range(8):
                    for k in range(8):
                        fw.mm(p[:, fc * 2:fc * 2 + 2], ws3[:, k, fc * 128:(fc + 1) * 128], sc3[:, k, :],
                              start=(k == 0), stop=(k == 7), inc=(k == 7))
                pv = p[:, 0:16].re('p (f w) -> p f w', w=2)
                fw.tt(modc4[:, idx], pv, bcast_last(adab[:, blk * 8:(blk + 1) * 8], 2), ALU.add)
                if blk in (1, 4):
                    fw.ts(modc4[:, idx], modc4[:, idx], 1.0, ALU.add)
            else:
                g = 0 if blk == 2 else 1
                for w in range(2):
                    for half in range(2):
                        p = psb[1 + (w * 2 + half) % 2]
                        for k in range(8):
                            fw.mm(p[:, :], scb4[:, k, w, :], ws3[:, k, half * 512:(half + 1) * 512],
                                  start=(k == 0), stop=(k == 7), inc=(k == 7))
                        fw.tt(gbc[w][g][:, half * 512:(half + 1) * 512], p[:, :],
                              adabb[:, g * 1024 + half * 512: g * 1024 + (half + 1) * 512], ALU.add)
        fw.barrier()
    return {'modc4': modc4, 'gbc': gbc}


def transpose_mod(fw, xt3, nsub, hT3, modc4, idx, w, ident, cnt0=0):
    psb = fw.psb
    n = nsub * 128
    for k in range(8):
        p = psb[k % 2]
        for j in range(nsub):
            fw.tr(p[:, j * 128:(j + 1) * 128], xt3[:, j, k * 128:(k + 1) * 128], ident)
        if idx is None:
            fw.copy(hT3[:, k, 0:n], p[:, 0:n], e=('act' if k % 2 else 'dve'))
        elif k % 2 == 0:
            fw.ts(hT3[:, k, 0:n], p[:, 0:n], modc4[:, idx + 1, k, w:w + 1], ALU.mult,
                  s2=modc4[:, idx, k, w:w + 1], op1=ALU.add)
        else:
            fw.act(hT3[:, k, 0:n], p[:, 0:n], AF.Identity, bias=modc4[:, idx, k, w:w + 1],
                   scale=modc4[:, idx + 1, k, w:w + 1])


def phase1(fw, L, P0, S, cm_groups, x_in):
    nc = fw.nc
    psb = fw.psb
    with ExitStack() as st:
        wfm = fw.sb([128, 8 * NFM], BF16, name='wfm', stack=st)
        wfm3 = wfm[:, :].re('p (k c) -> p k c', k=8)
        wtm = fw.sb([128, 8 * NTM], BF16, name='wtm', stack=st)
        wtm3 = wtm[:, :].re('p (k c) -> p k c', k=8)
        load_cast_weight(fw, st, wfm3, L['w_fm'], 8, NFM, 'wfm')
        load_cast_weight(fw, st, wtm3, L['w_tm'], 8, NTM, 'wtm')
        xb = [fw.sb([128, 4096], name=f'p1x{i}', stack=st) for i in range(2)]
        hb = [fw.sb([128, 8 * 512], BF16, name=f'p1h{i}', stack=st) for i in range(2)]
        stg = [fw.sb([128, 512], name=f'p1s{i}', stack=st) for i in range(4)]
        ident = fw.ident
        tiles = [('r', 0, 0, 256, set('ABCD'))]
        rg = set('ABCD') - set(cm_groups)
        for m in range(8):
            if rg:
                tiles.append(('r', 1, m * 512, 512, rg))
        for m in range(8):
            if cm_groups:
                tiles.append(('c', 1, m * 512, 512, set(cm_groups)))
        ti = 0
        rot = 0
        srot = 0
        for order, w, s0, n, groups in tiles:
            nsub = n // 128
            xt = xb[ti % 2]
            xt3 = xt[:, :].re('p (j c) -> p j c', c=1024)
            hT3 = hb[ti % 2][:, :].re('p (k t) -> p k t', k=8)
            ti += 1
            for j in range(nsub):
                if w == 0:
                    segs = [(0, 128, j * 128, 1)]
                else:
                    segs = scan_rows(order, s0 + j * 128, 128)
                for poff, nr, r0, rstep in segs:
                    fw.dma('sp', xt3[poff:poff + nr, j, :], rows_ap(fw, x_in, r0, rstep, nr, 0, 1024))
            transpose_mod(fw, xt3, nsub, hT3, P0['modc4'], 0, w, ident)
            tok0 = s0 if w == 0 else NCTX + s0
            for g, nm, off, ncols in fm_layout():
                if g not in groups:
                    continue
                for c0 in range(0, ncols, 128):
                    cw = min(128, ncols - c0)
                    p = psb[2 + rot % 6]
                    rot += 1
                    for k in range(8):
                        fw.mm(p[0:cw, 0:n], wfm3[:, k, off + c0: off + c0 + cw], hT3[:, k, 0:n],
                              start=(k == 0), stop=(k == 7), inc=(k == 7))
                    sg = stg[srot % 4]
                    srot += 1
                    fw.copy(sg[0:cw, 0:n], p[0:cw, 0:n], e=('act' if srot % 2 else 'dve'))
                    fw.dma('pool', S[nm][c0:c0 + cw, tok0:tok0 + n], sg[0:cw, 0:n])
            for j in range(nsub):
                for g, nm, off, ncols in tm_layout():
                    if g not in groups:
                        continue
                    p = psb[2 + rot % 6]
                    rot += 1
                    for k in range(8):
                        fw.mm(p[:, 0:ncols], hT3[:, k, j * 128:(j + 1) * 128], wtm3[:, k, off:off + ncols],
                              start=(k == 0), stop=(k == 7), inc=(k == 7))
                    sg = stg[srot % 4]
                    srot += 1
                    fw.copy(sg[:, 0:ncols], p[:, 0:ncols], e=('act' if srot % 2 else 'dve'))
                    fw.dma('pool', S[nm][tok0 + j * 128: tok0 + (j + 1) * 128, 0:ncols], sg[:, 0:ncols])
        fw.barrier()


def phase3(fw, L, P0, Y, x_in, X1, x_out, last, out_lat=None):
    psb = fw.psb
    ident = fw.ident
    modc4, gbc = P0['modc4'], P0['gbc']
    t0 = 2 if last else 0
    with ExitStack() as st:
        wo = fw.sb([128, 8 * 1024], BF16, name='wo', stack=st)
        wo3 = wo[:, :].re('p (k c) -> p k c', k=8)
        load_cast_weight(fw, st, wo3, L['w_out'], 8, 1024, 'wo')
        yb = [fw.sb([128, 1024], name=f'p3y{i}', stack=st) for i in range(2)]
        xb = [fw.sb([128, 1024], name=f'p3x{i}', stack=st) for i in range(2)]
        yTb = [fw.sb([128, 8 * 128], BF16, name=f'p3yT{i}', stack=st) for i in range(2)]
        tb = [fw.sb([128, 1024], name=f'p3t{i}', stack=st) for i in range(2)]
        ob = [fw.sb([128, 1024], name=f'p3o{i}', stack=st) for i in range(2)]
        lnbc = fw.sb([128, 2048], name='lnbc1', stack=st)
        fw.dma('sp', lnbc, dram_bcast_rows(L['ln'], 0, 2048))
        sc = ln_scratch(fw, st, 'ln1')
        for ti in range(t0, T // 128):
            w = 0 if ti < 2 else 1
            yt, xt, yT, tt_, ot = yb[ti % 2], xb[ti % 2], yTb[ti % 2], tb[ti % 2], ob[ti % 2]
            fw.dma('sp', yt, Y[ti * 128:(ti + 1) * 128, :])
            fw.dma('sp', xt, x_in[ti * 128:(ti + 1) * 128, :])
            yt3 = yt[:, :].re('p (j c) -> p j c', j=1)
            yT3 = yT[:, :].re('p (k t) -> p k t', k=8)
            for hh in range(2):
                p = psb[hh]
                for q in range(4):
                    k = hh * 4 + q
                    fw.tr(p[:, q * 128:(q + 1) * 128], yt[:, k * 128:(k + 1) * 128], ident)
                fw.copy(yT3[:, hh * 4:(hh + 1) * 4, :], p[:, :].re('p (q t) -> p q t', q=4), e=('act' if hh else 'dve'))
            for half in range(2):
                p = psb[2 + half]
                for k in range(8):
                    fw.mm(p[:, :], yT3[:, k, :], wo3[:, k, half * 512:(half + 1) * 512],
                          start=(k == 0), stop=(k == 7), inc=(k == 7))
                fw.tt(tt_[:, half * 512:(half + 1) * 512], p[:, :], gbc[w][0][:, half * 512:(half + 1) * 512], ALU.mult)
                fw.stt(tt_[:, half * 512:(half + 1) * 512], xt[:, half * 512:(half + 1) * 512], ALPHA,
                       tt_[:, half * 512:(half + 1) * 512], ALU.mult, ALU.add)
            layer_norm(fw, tt_, ot, lnbc[:, 0:1024], lnbc[:, 1024:2048], sc)
            fw.dma('pool', X1[ti * 128:(ti + 1) * 128, :], ot)
        fw.barrier()
    with ExitStack() as st:
        w1 = fw.sb([128, 8 * 4096], BF16, name='w1', stack=st)
        w13 = w1[:, :].re('p (k c) -> p k c', k=8)
        w2 = fw.sb([128, 32 * 1024], BF16, name='w2', stack=st)
        w23 = w2[:, :].re('p (k c) -> p k c', k=32)
        load_cast_weight(fw, st, w13, L['mlp_w1'], 8, 4096, 'w1')
        load_cast_weight(fw, st, w23, L['mlp_w2'], 32, 1024, 'w2', chunk=64)
        xb = [fw.sb([128, 2048], name=f'p4x{i}', stack=st) for i in range(2)]
        hT = fw.sb([128, 8 * 256], BF16, name='p4hT', stack=st)
        hT3 = hT[:, :].re('p (k t) -> p k t', k=8)
        h1 = fw.sb([128, 32 * 256], BF16, name='p4h1', stack=st)
        h13 = h1[:, :].re('p (k t) -> p k t', k=32)
        sq = [fw.sb([128, 256], name=f'p4sq{i}', stack=st) for i in range(2)]
        tb = [fw.sb([128, 1024], name=f'p4t{i}', stack=st) for i in range(1)]
        ob = [fw.sb([128, 1024], name=f'p4o{i}', stack=st) for i in range(2)]
        lnbc = fw.sb([128, 2048], name='lnbc2', stack=st)
        fw.dma('sp', lnbc, dram_bcast_rows(L['ln'], 2048, 2048))
        sc = ln_scratch(fw, st, 'ln2')
        oi = 0
        for mi in range(t0 // 2, T // 256):
            w = 0 if mi < 1 else 1
            xt = xb[mi % 2]
            xt3 = xt[:, :].re('p (j c) -> p j c', j=2)
            fw.dma('sp', xt3, X1[mi * 256:(mi + 1) * 256, :].re('(j p) c -> p j c', p=128))
            transpose_mod(fw, xt3, 2, hT3, modc4, 2, w, ident)
            for fc in range(32):
                p = psb[2 + fc % 3]
                for k in range(8):
                    fw.mm(p[:, 0:256], w13[:, k, fc * 128:(fc + 1) * 128], hT3[:, k, :],
                          start=(k == 0), stop=(k == 7), inc=(k == 7))
                s = sq[fc % 2]
                fw.act(s, p[:, 0:256], AF.Square)
                fw.stt(h13[:, fc, :], p[:, 0:256], 0.0, s, ALU.is_gt, ALU.mult)
            for j in range(2):
                tt_, ot = tb[0], ob[oi % 2]
                oi += 1
                for half in range(2):
                    p = psb[5 + half]
                    for kk in range(32):
                        fw.mm(p[:, :], h13[:, kk, j * 128:(j + 1) * 128], w23[:, kk, half * 512:(half + 1) * 512],
                              start=(kk == 0), stop=(kk == 31), inc=(kk == 31))
                    fw.tt(tt_[:, half * 512:(half + 1) * 512], p[:, :], gbc[w][1][:, half * 512:(half + 1) * 512], ALU.mult)
                    fw.stt(tt_[:, half * 512:(half + 1) * 512], xt3[:, j, half * 512:(half + 1) * 512], ALPHA,
                           tt_[:, half * 512:(half + 1) * 512], ALU.mult, ALU.add)
                layer_norm(fw, tt_, ot, lnbc[:, 0:1024], lnbc[:, 1024:2048], sc)
                r0 = mi * 256 + j * 128
                if last:
                    fw.dma('pool', out_lat[r0 - NCTX: r0 - NCTX + 128, :], ot)
                else:
                    fw.dma('pool', x_out[r0:r0 + 128, :], ot)
        fw.barrier()


NORM_EPS = 1e-6
GS = 516


def fv(tl, off, dims, p0=0, np_=128):
    a = tl.t[p0:p0 + np_, :]
    return V(tl, bass.AP(a.tensor, a.offset + off, [list(a.ap[0])] + [list(d) for d in dims]))


def build_consts():
    c = {}
    off = 0
    arrs = []

    def add(name, arr):
        nonlocal off
        a = np.zeros((128, arr.shape[1]), np.float32)
        a[:arr.shape[0]] = arr
        c[name] = (off, arr.shape[1])
        arrs.append(a)
        off += arr.shape[1]
    for C in (32, 64):
        j = np.arange(C)[:, None]
        i = np.arange(C)[None, :]
        add(f'mi{C}_0', (j <= i).astype(np.float32))
        add(f'mi{C}_1', (j >= i).astype(np.float32))
        add(f'ms{C}_0', (j < i).astype(np.float32))
        add(f'ms{C}_1', (j > i).astype(np.float32))
    sel = np.zeros((16, 2, 2, 128), np.float32)
    for d in range(2):
        for ct in range(2):
            for m in range(128):
                sel[8 + 4 * d + 2 * ct + m // 64, d, ct, m] = 1.0
    add('sel_lf', sel.reshape(16, 512))
    p = np.arange(128)
    add('blk64', (p[:, None] // 64 == p[None, :] // 64).astype(np.float32))
    add('hsel', (p[:, None] // 64 == np.arange(2)[None, :]).astype(np.float32))
    add('ones', np.ones((128, 128), np.float32))
    return c, np.concatenate(arrs, 1)


CST_MAP, CST_ARR = build_consts()
NCST = CST_ARR.shape[1]


def cv(fw, name, np_=128, c0=0, c1=None):
    off, n = CST_MAP[name]
    c1 = n if c1 is None else c1
    return fw.cst[0:np_, off + c0: off + c1]


def blocks_for(d):
    lat = [(NCTX + m * 512, 512) for m in range(8)]
    if d == 1:
        lat = lat[::-1]
    return [(0, 256)] + lat


def gla_core(fw, name, QT, KTd, LFd, Vsrc, Od, dvx, C, kscale):
    psb = fw.psb
    ident = fw.ident
    W = 4 * dvx
    with ExitStack() as st:
        qb = [fw.sb([64, 2048], name=f'g_q{i}', stack=st) for i in range(1)]
        kb = [fw.sb([64, 2048], name=f'g_k{i}', stack=st) for i in range(1)]
        lb = [fw.sb([64, 2048], name=f'g_l{i}', stack=st) for i in range(1)]
        vb = [fw.sb([C, 16 * W], name=f'g_v{i}', stack=st) for i in range(2)]
        ob = [fw.sb([C, 16 * W], name=f'g_o{i}', stack=st) for i in range(2)]
        Gz = fw.sb([64, 4 * GS], name='g_Gz', stack=st)
        cum = fw.sb([64, 2048], name='g_cum', stack=st)
        d3 = fw.sb([64, 2048], name='g_d3', stack=st)
        Ep = fw.sb([64, 2048], name='g_Ep', stack=st)
        Em = fw.sb([64, 2048], name='g_Em', stack=st)
        E3 = fw.sb([64, 2048], name='g_E3', stack=st)
        Qt = fw.sb([64, 2048], name='g_Qt', stack=st)
        Kbar = fw.sb([64, 2048], name='g_Kbar', stack=st)
        Khat = fw.sb([64, 2048], name='g_Khat', stack=st)
        glast = fw.sb([64, 64], name='g_glast', stack=st)
        Am = [fw.sb([C, 4 * C], name=f'g_Am{i}', stack=st) for i in range(2)]
        Kt = [fw.sb([C, 256], name=f'g_Kt{i}', stack=st) for i in range(2)]
        S = [fw.sb([64, dvx], name=f'g_S{i}', stack=st) for i in range(4)]
        ones = fw.sb([64, 512], name='g_ones', stack=st)
        fw.memset(ones, 1.0)
        fw.memset(Gz, 0.0)
        bi = 0
        ci = 0
        for d in range(2):
            for h in range(4):
                fw.memset(S[h], 0.0)
            moff = CST_MAP[f'mi{C}_{d}'][0]
            for (t0, n) in blocks_for(d):
                nn = n // C
                q, k, l, v, o = qb[0], kb[0], lb[0], vb[bi % 2], ob[bi % 2]
                bi += 1

                def v3(tl):
                    return fv(tl, 0, [[512, 4], [1, n]], 0, 64)
                fw.dma('sp', v3(q), QT[:, t0:t0 + n].re('(h p) t -> p h t', p=64))
                fw.dma('sp', v3(k), KTd[d][:, t0:t0 + n].re('(h p) t -> p h t', p=64))
                fw.dma('sp', v3(l), LFd[d][:, t0:t0 + n].re('(h p) t -> p h t', p=64))
                vd, vc0, vw = Vsrc[d]
                fw.dma('sp', fv(v, 0, [[W, nn], [1, W]], 0, C),
                       vd[t0:t0 + n, vc0:vc0 + W].re('(n c) d -> c n d', c=C))
                for h in range(4):
                    fw.scan(Gz[:, h * GS + 1: h * GS + 1 + n], ones[:, 0:n], l[:, h * 512: h * 512 + n],
                            0.0, ALU.mult, ALU.add)
                lastc = C - 1 if d == 0 else 0
                for h in range(4):
                    outv = fv(cum, h * 512, [[C, nn], [1, C]], 0, 64)
                    if d == 0:
                        fw.tt(outv, fv(Gz, h * GS + 1, [[C, nn], [1, C]], 0, 64),
                              fv(Gz, h * GS, [[C, nn], [0, C]], 0, 64), ALU.subtract)
                    else:
                        fw.tt(outv, fv(Gz, h * GS + C, [[C, nn], [0, C]], 0, 64),
                              fv(Gz, h * GS, [[C, nn], [1, C]], 0, 64), ALU.subtract)
                    fw.tt(fv(d3, h * 512, [[C, nn], [1, C]], 0, 64), outv,
                          fv(cum, h * 512 + lastc, [[C, nn], [0, C]], 0, 64), ALU.subtract, e='pool')
                fw.act(v3(Ep), v3(cum), AF.Exp)
                fw.act(v3(Em), v3(cum), AF.Exp, scale=-1.0)
                fw.act(v3(E3), v3(d3), AF.Exp, scale=-1.0)
                fw.tt(v3(Qt), v3(q), v3(Ep), ALU.mult)
                fw.stt(v3(Kbar), v3(k), kscale, v3(Em), ALU.mult, ALU.mult)
                fw.stt(v3(Khat), v3(k), kscale, v3(E3), ALU.mult, ALU.mult)
                fw.copy(fv(glast, 0, [[16, 4], [1, nn]], 0, 64), fv(Ep, lastc, [[512, 4], [C, nn]], 0, 64))
                order = range(nn) if d == 0 else range(nn - 1, -1, -1)
                for n_ in order:
                    a = n_ * C
                    psA, psT, psO, psS = psb[ci % 2], psb[2 + ci % 2], psb[4 + ci % 2], psb[6 + ci % 2]
                    am, kt = Am[ci % 2], Kt[ci % 2]
                    ci += 1
                    for h in range(4):
                        cs = slice(h * 512 + a, h * 512 + a + C)
                        fw.mm(psA[0:C, h * C:(h + 1) * C], Kbar[0:64, cs], Qt[0:64, cs], inc=(h == 3))
                    fw.tt(fv(am, 0, [[C, 4], [1, C]], 0, C), fv(psA, 0, [[C, 4], [1, C]], 0, C),
                          fv(fw.cst, moff, [[0, 4], [1, C]], 0, C), ALU.mult)
                    for h in range(4):
                        fw.tr(psT[0:C, h * 64:(h + 1) * 64], Khat[0:64, h * 512 + a: h * 512 + a + C], ident[0:64, 0:64])
                    fw.copy(kt[0:C, :], psT[0:C, 0:256], e='act')
                    for h in range(4):
                        cs = slice(h * 512 + a, h * 512 + a + C)
                        fw.mm(psO[0:C, h * dvx:(h + 1) * dvx], Qt[0:64, cs], S[h][0:64, :],
                              start=True, stop=False, inc=False)
                        fw.mm(psO[0:C, h * dvx:(h + 1) * dvx], am[0:C, h * C:(h + 1) * C],
                              v[0:C, n_ * W + h * dvx: n_ * W + (h + 1) * dvx], start=False, stop=True)
                    fw.copy(o[0:C, n_ * W:(n_ + 1) * W], psO[0:C, 0:W], e=('act' if ci % 2 else 'dve'))
                    for h in range(4):
                        fw.mm(psS[0:64, h * dvx:(h + 1) * dvx], kt[0:C, h * 64:(h + 1) * 64],
                              v[0:C, n_ * W + h * dvx: n_ * W + (h + 1) * dvx])
                        fw.stt(S[h][0:64, :], S[h][0:64, :], glast[0:64, h * 16 + n_: h * 16 + n_ + 1],
                               psS[0:64, h * dvx:(h + 1) * dvx], ALU.mult, ALU.add)
                fw.dma('pool', Od[d][t0:t0 + n, :].re('(n c) d -> c n d', c=C), fv(o, 0, [[W, nn], [1, W]], 0, C))
        fw.barrier()


def rms_head_norm(fw, h, out, gbc, sc):
    fw.tt(sc['sq'], h, h, ALU.mult)
    fw.reduce(sc['ss'], fv(sc['sq'], 0, [[64, 4], [1, 64]]), ALU.add)
    fw.ts(sc['ss'], sc['ss'], 1.0 / 64, ALU.mult, s2=NORM_EPS, op1=ALU.add)
    fw.act(sc['r'], sc['ss'], AF.Sqrt)
    fw.recip(sc['r'], sc['r'])
    fw.tt(fv(out, 0, [[64, 4], [1, 64]]), fv(h, 0, [[64, 4], [1, 64]]), fv(sc['r'], 0, [[1, 4], [0, 64]]), ALU.mult)
    fw.tt(out, out, gbc, ALU.mult, e='pool')


def store_rows_scan(fw, Y, c0, cw, tile_v, ti, colmajor):
    if ti < 2 or not colmajor:
        fw.dma('pool', Y[ti * 128:(ti + 1) * 128, c0:c0 + cw], tile_v)
    else:
        s0 = ti * 128 - NCTX
        for poff, nr, r0, rstep in scan_rows('c', s0, 128):
            fw.dma('pool', rows_ap(fw, Y, r0, rstep, nr, c0, c0 + cw), tile_v[poff:poff + nr, :])


def hgrn2(fw, L, S, Y, layer, colmajor, stop=9):
    psb = fw.psb
    KT = [S['A_KT0'], S['A_KT1']]
    LF = [S['A_LF0'], S['A_LF1']]
    with ExitStack() as st:
        gam = fw.sb([128, 16], name='h_gam', stack=st)
        eg = fw.sb([128, 16], name='h_eg', stack=st)
        tot = fw.sb([128, 4], name='h_tot', stack=st)
        part = fw.sb([128, 4], name='h_part', stack=st)
        lbv = fw.sb([128, 4], name='h_lb', stack=st)
        oml = fw.sb([128, 4], name='h_oml', stack=st)
        fw.dma('sp', gam, L['hgrn_gamma_t'])
        fw.act(eg, gam, AF.Exp)
        fw.reduce(tot, fv(eg, 0, [[4, 4], [1, 4]]), ALU.add)
        if layer >= 1:
            fw.reduce(part, fv(eg, 1, [[4, 4], [1, layer]]), ALU.add)
        else:
            fw.memset(part, 0.0)
        fw.recip(tot, tot)
        fw.tt(lbv, part, tot, ALU.mult)
        fw.ts(oml, lbv, -1.0, ALU.mult, s2=1.0, op1=ALU.add)
        HB = T // 2
        fr = [fw.sb([128, HB], name=f'h_fr{i}', stack=st) for i in range(2)]
        e1 = [fw.sb([128, HB], name=f'h_e{i}', stack=st) for i in range(2)]
        lf = [fw.sb([128, HB], name=f'h_lf{i}', stack=st) for i in range(2)]
        kk = [fw.sb([128, HB], name=f'h_k{i}', stack=st) for i in range(2)]
        it = 0
        for d in range(2):
            for ct in range(2):
                col = d * 2 + ct
                for hb in range(2):
                    f_, e_, l_, k_ = fr[it % 2], e1[it % 2], lf[it % 2], kk[it % 2]
                    it += 1
                    ts_ = slice(hb * HB, (hb + 1) * HB)
                    rows = slice(d * 256 + ct * 128, d * 256 + (ct + 1) * 128)
                    fw.dma('sp', f_, S['fm_A_f'][rows, ts_])
                    fw.act(e_, f_, AF.Exp, scale=-1.0)
                    fw.ts(e_, e_, 1.0, ALU.add)
                    fw.recip(e_, e_)
                    fw.ts(e_, e_, oml[:, col:col + 1], ALU.mult, s2=lbv[:, col:col + 1], op1=ALU.add)
                    fw.act(l_, e_, AF.Ln)
                    fw.ts(k_, e_, -1.0, ALU.mult, s2=1.0, op1=ALU.add, e='pool')
                    fw.dma('pool', LF[d][ct * 128:(ct + 1) * 128, ts_], l_)
                    fw.dma('pool', KT[d][ct * 128:(ct + 1) * 128, ts_], k_)
        fw.barrier()
    if stop <= 1:
        return
    Od = [S['A_O0'], S['A_O1']]
    Vsrc = [(S['tm_A_ig'], 0, 512), (S['tm_A_ig'], 0, 512)]
    gla_core(fw, 'hgrn', S['fm_A_q'], KT, LF, Vsrc, Od, 64, 32, 1.0)
    if stop <= 2:
        return
    with ExitStack() as st:
        gbc = fw.sb([128, 256], name='h_gbc', stack=st)
        fw.dma('sp', gbc, dram_bcast_rows(L['hgrn_norm_g'], 0, 256))
        o0 = [fw.sb([128, 256], name=f'h_o0{i}', stack=st) for i in range(2)]
        o1 = [fw.sb([128, 256], name=f'h_o1{i}', stack=st) for i in range(2)]
        gg = [fw.sb([128, 256], name=f'h_g{i}', stack=st) for i in range(2)]
        yy = [fw.sb([128, 256], name=f'h_y{i}', stack=st) for i in range(2)]
        sc = {'sq': fw.sb([128, 256], name='h_sq', stack=st), 'ss': fw.sb([128, 4], name='h_ss', stack=st),
              'r': fw.sb([128, 4], name='h_r', stack=st)}
        for ti in range(T // 128):
            a0, a1, g_, y_ = o0[ti % 2], o1[ti % 2], gg[ti % 2], yy[ti % 2]
            rs = slice(ti * 128, (ti + 1) * 128)
            fw.dma('sp', a0, Od[0][rs, :])
            fw.dma('sp', a1, Od[1][rs, :])
            fw.dma('sp', g_, S['tm_A_ig'][rs, 256:512])
            fw.tt(a0, a0, a1, ALU.add)
            fw.act(g_, g_, AF.Silu)
            rms_head_norm(fw, a0, y_, gbc, sc)
            fw.tt(y_, y_, g_, ALU.mult)
            store_rows_scan(fw, Y, 0, 256, y_[:, :], ti, colmajor)
        fw.barrier()


def mlstm(fw, L, S, Y, layer, colmajor):
    psb = fw.psb
    LF = [S['D_LF0'], S['D_LF1']]
    VX = [S['D_VX0'], S['D_VX1']]
    with ExitStack() as st:
        gt = fw.sb([16, T], name='m_gt', stack=st)
        ee = fw.sb([16, T], name='m_e', stack=st)
        fb = fw.sb([16, 1], name='m_fb', stack=st)
        fw.dma('sp', gt, S['fm_D_gates'])
        fw.dma('sp', fb, L['mlstm_fb_col'])
        fw.ts(fb, fb, -1.0, ALU.mult)
        fw.act(ee, gt, AF.Exp, bias=fb, scale=-1.0)
        fw.ts(ee, ee, 1.0, ALU.add)
        fw.act(ee, ee, AF.Ln)
        fw.ts(ee, ee, -1.0, ALU.mult)
        stg = [fw.sb([128, 512], name=f'm_stg{i}', stack=st) for i in range(2)]
        it = 0
        soff = CST_MAP['sel_lf'][0]
        for d in range(2):
            for ct in range(2):
                for t0 in range(0, T, 512):
                    n = min(512, T - t0)
                    p = psb[it % 4]
                    sg = stg[it % 2]
                    it += 1
                    fw.mm(p[:, 0:n], fw.cst[0:16, soff + (d * 2 + ct) * 128: soff + (d * 2 + ct + 1) * 128], ee[0:16, t0:t0 + n])
                    fw.copy(sg[:, 0:n], p[:, 0:n], e=('act' if it % 2 else 'dve'))
                    fw.dma('pool', LF[d][ct * 128:(ct + 1) * 128, t0:t0 + n], sg[:, 0:n])
        ibc = fw.sb([128, 8], name='m_ibc', stack=st)
        fw.dma('sp', ibc, dram_bcast_rows(L['mlstm_i_bias'], 0, 8))
        vo = [fw.sb([128, 256], name=f'm_vo{i}', stack=st) for i in range(2)]
        gts = [fw.sb([128, 16], name=f'm_gts{i}', stack=st) for i in range(2)]
        sv = [fw.sb([128, 8], name=f'm_sv{i}', stack=st) for i in range(2)]
        vx = [[fw.sb([128, 260], name=f'm_vx{d}{i}', stack=st) for i in range(2)] for d in range(2)]
        for ti in range(T // 128):
            rs = slice(ti * 128, (ti + 1) * 128)
            v_, g_, s_ = vo[ti % 2], gts[ti % 2], sv[ti % 2]
            fw.dma('sp', v_, S['tm_D_vo'][rs, 0:256])
            fw.dma('sp', g_, S['tm_D_gates'][rs, :])
            fw.tt(s_, g_[:, 0:8], ibc, ALU.add)
            fw.act(s_, s_, AF.Exp)
            for d in range(2):
                x_ = vx[d][ti % 2]
                fw.tt(fv(x_, 0, [[65, 4], [1, 64]]), fv(v_, 0, [[64, 4], [1, 64]]), fv(s_, d * 4, [[1, 4], [0, 64]]),
                      ALU.mult, e=('pool' if d else 'dve'))
                fw.copy(fv(x_, 64, [[65, 4], [1, 1]]), fv(s_, d * 4, [[1, 4], [1, 1]]))
                fw.dma('pool', VX[d][rs, :], x_)
        fw.barrier()
    Od = [S['D_O0'], S['D_O1']]
    Vsrc = [(VX[0], 0, 260), (VX[1], 0, 260)]
    KTv = S['fm_D_qk'][256:512, :]
    gla_core(fw, 'mlstm', S['fm_D_qk'][0:256, :], [KTv, KTv], LF, Vsrc, Od, 65, 64, 0.125)
    with ExitStack() as st:
        gbc = fw.sb([128, 256], name='m_gbc', stack=st)
        fw.dma('sp', gbc, dram_bcast_rows(L['mlstm_norm_g'], 0, 256))
        oo = [[fw.sb([128, 260], name=f'm_o{d}{i}', stack=st) for i in range(2)] for d in range(2)]
        og = [fw.sb([128, 256], name=f'm_og{i}', stack=st) for i in range(2)]
        hh = [fw.sb([128, 256], name=f'm_h{i}', stack=st) for i in range(2)]
        h2 = fw.sb([128, 256], name='m_h2', stack=st)
        yy = [fw.sb([128, 256], name=f'm_y{i}', stack=st) for i in range(2)]
        den = [fw.sb([128, 4], name=f'm_den{d}', stack=st) for d in range(2)]
        sc = {'sq': fw.sb([128, 256], name='m_sq', stack=st), 'ss': fw.sb([128, 4], name='m_ss', stack=st),
              'r': fw.sb([128, 4], name='m_r', stack=st)}
        for ti in range(T // 128):
            rs = slice(ti * 128, (ti + 1) * 128)
            g_, h_, y_ = og[ti % 2], hh[ti % 2], yy[ti % 2]
            fw.dma('sp', g_, S['tm_D_vo'][rs, 256:512])
            for d in range(2):
                o_ = oo[d][ti % 2]
                fw.dma('sp', o_, Od[d][rs, :])
                fw.act(den[d], fv(o_, 64, [[65, 4]]), AF.Abs)
                fw.ts(den[d], den[d], 1.0, ALU.max)
                fw.recip(den[d], den[d])
                fw.tt(fv(h_ if d == 0 else h2, 0, [[64, 4], [1, 64]]), fv(o_, 0, [[65, 4], [1, 64]]),
                      fv(den[d], 0, [[1, 4], [0, 64]]), ALU.mult)
            fw.tt(h_, h_, h2, ALU.add)
            fw.act(g_, g_, AF.Sigmoid)
            rms_head_norm(fw, h_, y_, gbc, sc)
            fw.tt(y_, y_, g_, ALU.mult)
            store_rows_scan(fw, Y, 768, 256, y_[:, :], ti, colmajor)
        fw.barrier()


def set_psum(fw, mode):
    if getattr(fw, '_ps_stack', None) is not None:
        fw._ps_stack.close()
    fw._ps_stack = ExitStack()
    fw._ps_gen = getattr(fw, '_ps_gen', 0) + 1
    if mode == 'full':
        fw.psb = [fw.ps([128, 512], name=f'psb{i}_{fw._ps_gen}', stack=fw._ps_stack) for i in range(8)]
    else:
        banks = [fw.ps([128, 512], name=f'psk{i}_{fw._ps_gen}', stack=fw._ps_stack) for i in range(8)]
        fw.psh = []
        for i, b in enumerate(banks):
            for hf in range(2):
                tl = Tl(fw, b.t[:, hf * 256:(hf + 1) * 256], 'ps')
                tl.name = f'psh{i}_{hf}_{fw._ps_gen}'
                fw.psh.append(tl)


def finish_rms_gate(fw, L, Od, gsrc, gain, ycol, func, Y, colmajor, pfx):
    with ExitStack() as st:
        gbc = fw.sb([128, 256], name=f'{pfx}_gbc', stack=st)
        fw.dma('sp', gbc, dram_bcast_rows(L[gain], 0, 256))
        o0 = [fw.sb([128, 256], name=f'{pfx}_o0{i}', stack=st) for i in range(2)]
        o1 = [fw.sb([128, 256], name=f'{pfx}_o1{i}', stack=st) for i in range(2)]
        gg = [fw.sb([128, 256], name=f'{pfx}_g{i}', stack=st) for i in range(2)]
        yy = [fw.sb([128, 256], name=f'{pfx}_y{i}', stack=st) for i in range(2)]
        sc = {'sq': fw.sb([128, 256], name=f'{pfx}_sq', stack=st), 'ss': fw.sb([128, 4], name=f'{pfx}_ss', stack=st),
              'r': fw.sb([128, 4], name=f'{pfx}_r', stack=st)}
        for ti in range(T // 128):
            a0, a1, g_, y_ = o0[ti % 2], o1[ti % 2], gg[ti % 2], yy[ti % 2]
            rs = slice(ti * 128, (ti + 1) * 128)
            fw.dma('sp', a0, Od[0][rs, :])
            fw.dma('sp', a1, Od[1][rs, :])
            fw.dma('sp', g_, gsrc[rs, :])
            fw.tt(a0, a0, a1, ALU.add)
            fw.act(g_, g_, func)
            rms_head_norm(fw, a0, y_, gbc, sc)
            fw.tt(y_, y_, g_, ALU.mult)
            store_rows_scan(fw, Y, ycol, 256, y_[:, :], ti, colmajor)
        fw.barrier()


def neumann_solve(fw, U0, L0, Ytl, UL, psi, C=64, w=64):
    psY, psU, psL = psi
    Uc, Lc = U0, L0
    for lvl in range(6):
        for h in range(4):
            fw.mm(psY[0:C, h * w:(h + 1) * w], Uc[0:C, h * C:(h + 1) * C], Ytl[0:C, h * w:(h + 1) * w], inc=(h == 3))
        fw.tt(Ytl[0:C, 0:4 * w], Ytl[0:C, 0:4 * w], psY[0:C, 0:4 * w], ALU.subtract if lvl == 0 else ALU.add)
        if lvl < 5:
            Un, Ln = UL[lvl % 2]
            for h in range(4):
                hs = slice(h * C, (h + 1) * C)
                fw.mm(psU[0:C, hs], Lc[0:C, hs], Uc[0:C, hs], inc=(h == 3))
            fw.copy(Un[0:C, :], psU[0:C, 0:4 * C], e='act')
            if lvl < 4:
                for h in range(4):
                    hs = slice(h * C, (h + 1) * C)
                    fw.mm(psL[0:C, hs], Uc[0:C, hs], Lc[0:C, hs], inc=(h == 3))
                fw.copy(Ln[0:C, :], psL[0:C, 0:4 * C], e='act')
            Uc, Lc = Un, Ln


def bc3(tl, off, C, inner, hstep=1, p0=0):
    return fv(tl, off, [[hstep, 4], [0, inner]], p0, C)


def gdn(fw, L, S, Y, layer, colmajor, stop=9):
    C = 64
    PC = S['tm_C']
    psb = fw.psb
    ident = fw.ident
    with ExitStack() as st:
        wbc = fw.sb([128, 5 * 768], name='c_wbc', stack=st)
        fw.dma('sp', wbc, dram_bcast_rows(L['gdn_conv'], 0, 5 * 768))
        dtb = fw.sb([128, 8], name='c_dtb', stack=st)
        nA = fw.sb([128, 8], name='c_nA', stack=st)
        fw.dma('sp', dtb, dram_bcast_rows(L['gdn_dt_bias'], 0, 8))
        fw.dma('sp', nA, dram_bcast_rows(L['gdn_a_log'], 0, 8))
        fw.act(nA, nA, AF.Exp)
        fw.ts(nA, nA, -1.0, ALU.mult)
        xs = [[fw.sb([128, 768], name=f'c_x{j}{i}', stack=st) for j in range(5)] for i in range(2)]
        acc = [fw.sb([128, 768], name=f'c_acc{i}', stack=st) for i in range(2)]
        sq = fw.sb([128, 512], name='c_sq', stack=st)
        ss = fw.sb([128, 8], name='c_ss', stack=st)
        qkn = [fw.sb([128, 512], name=f'c_qkn{i}', stack=st) for i in range(2)]
        stq = [fw.sb([64, 512], name=f'c_stq{i}', stack=st) for i in range(2)]
        stk = [fw.sb([64, 512], name=f'c_stk{i}', stack=st) for i in range(2)]
        ba = [fw.sb([128, 16], name=f'c_ba{i}', stack=st) for i in range(2)]
        gl = [fw.sb([128, 16], name=f'c_gl{i}', stack=st) for i in range(2)]
        for ti in range(T // 128):
            t0 = ti * 128
            seg0, seg1 = (0, NCTX) if ti < 2 else (NCTX, T)
            X = xs[ti % 2]
            ac, qk, sq_, sk_, ba_, gl_ = acc[ti % 2], qkn[ti % 2], stq[ti % 2], stk[ti % 2], ba[ti % 2], gl[ti % 2]
            for j in range(5):
                s = j - 2
                lo, hi = max(seg0, t0 + s), min(seg1, t0 + 128 + s)
                if lo != t0 + s or hi != t0 + 128 + s:
                    fw.memset(X[j], 0.0, e='pool')
                fw.dma('sp', X[j][lo - (t0 + s): hi - (t0 + s), :], PC[lo:hi, 0:768])
                fw.tt(X[j], X[j], wbc[:, j * 768:(j + 1) * 768], ALU.mult, e=('pool' if j % 2 else 'dve'))
            fw.tt(ac, X[0], X[1], ALU.add)
            fw.tt(ac, ac, X[2], ALU.add, e='pool')
            fw.tt(ac, ac, X[3], ALU.add)
            fw.tt(ac, ac, X[4], ALU.add, e='pool')
            fw.act(ac, ac, AF.Silu)
            fw.tt(sq, ac[:, 0:512], ac[:, 0:512], ALU.mult)
            fw.reduce(ss, fv(sq, 0, [[64, 8], [1, 64]]), ALU.add)
            fw.ts(ss, ss, 1e-12, ALU.add)
            fw.act(ss, ss, AF.Sqrt)
            fw.recip(ss, ss)
            fw.ts(ss[:, 0:4], ss[:, 0:4], 0.125, ALU.mult)
            fw.tt(fv(qk, 0, [[64, 8], [1, 64]]), fv(ac, 0, [[64, 8], [1, 64]]), fv(ss, 0, [[1, 8], [0, 64]]), ALU.mult)
            for g in range(8):
                p = psb[g // 4 + 2 * (ti % 2)]
                fw.tr(p[0:64, (g % 4) * 128:(g % 4 + 1) * 128], qk[:, g * 64:(g + 1) * 64], ident)
            fw.copy(sq_[0:64, :], psb[0 + 2 * (ti % 2)][0:64, :], e='act')
            fw.copy(sk_[0:64, :], psb[1 + 2 * (ti % 2)][0:64, :])
            fw.dma('pool', S['C_QT'][:, t0:t0 + 128].re('(h p) t -> p h t', p=64), fv(sq_, 0, [[128, 4], [1, 128]], 0, 64))
            fw.dma('pool', S['C_KT'][:, t0:t0 + 128].re('(h p) t -> p h t', p=64), fv(sk_, 0, [[128, 4], [1, 128]], 0, 64))
            fw.dma('pool', S['C_KV'][t0:t0 + 128, 0:256], qk[:, 256:512])
            fw.dma('pool', S['C_KV'][t0:t0 + 128, 256:512], ac[:, 512:768])
            fw.dma('sp', ba_, S['tm_C_ba'][t0:t0 + 128, :])
            fw.act(gl_[:, 0:8], ba_[:, 0:8], AF.Sigmoid)
            fw.tt(ba_[:, 8:16], ba_[:, 8:16], dtb, ALU.add)
            fw.act(ba_[:, 8:16], ba_[:, 8:16], AF.Exp)
            fw.ts(ba_[:, 8:16], ba_[:, 8:16], 1.0, ALU.add)
            fw.act(ba_[:, 8:16], ba_[:, 8:16], AF.Ln)
            fw.tt(gl_[:, 8:16], ba_[:, 8:16], nA, ALU.mult)
            fw.dma('pool', S['C_G'][t0:t0 + 128, :], gl_)
        fw.barrier()
    if stop <= 1:
        return
    psb = fw.psb
    Od = [S['C_O0'], S['C_O1']]
    with ExitStack() as st:
        qb = [fw.sb([64, 2048], name=f'c_q{i}', stack=st) for i in range(2)]
        kb = [fw.sb([64, 2048], name=f'c_k{i}', stack=st) for i in range(2)]
        kvb = [fw.sb([C, 8 * 512], name=f'c_kv{i}', stack=st) for i in range(2)]
        glb = [fw.sb([C, 8 * 16], name=f'c_glb{i}', stack=st) for i in range(2)]
        ob = [fw.sb([C, 8 * 256], name=f'c_ob{i}', stack=st) for i in range(2)]
        lab = fw.sb([C, 32], name='c_lab', stack=st)
        bb = fw.sb([C, 32], name='c_bb', stack=st)
        ecum = fw.sb([C, 32], name='c_ecum', stack=st)
        necum = fw.sb([C, 32], name='c_necum', stack=st)
        elast = fw.sb([C, 32], name='c_elast', stack=st)
        etot = fw.sb([C, 32], name='c_etot', stack=st)
        dd = fw.sb([C, 32], name='c_dd', stack=st)
        LA = [fw.sb([C, 256], name=f'c_LA{i}', stack=st) for i in range(2)]
        Gt = [fw.sb([C, 256], name=f'c_Gt{i}', stack=st) for i in range(2)]
        Gi = [fw.sb([C, 256], name=f'c_Gi{i}', stack=st) for i in range(2)]
        Gs = [fw.sb([C, 256], name=f'c_Gs{i}', stack=st) for i in range(2)]
        U0 = [fw.sb([C, 256], name=f'c_U0{i}', stack=st) for i in range(2)]
        L0 = [fw.sb([C, 256], name=f'c_L0{i}', stack=st) for i in range(2)]
        At = [fw.sb([C, 256], name=f'c_At{i}', stack=st) for i in range(2)]
        UL = [(fw.sb([C, 256], name=f'c_Un{i}', stack=st), fw.sb([C, 256], name=f'c_Ln{i}', stack=st)) for i in range(2)]
        Yt = [fw.sb([C, 256], name=f'c_Y{i}', stack=st) for i in range(2)]
        vn = [fw.sb([C, 256], name=f'c_vn{i}', stack=st) for i in range(2)]
        Kh = [fw.sb([C, 256], name=f'c_Kh{i}', stack=st) for i in range(2)]
        tmpo = [fw.sb([C, 256], name=f'c_to{i}', stack=st) for i in range(2)]
        Sst = [fw.sb([64, 64], name=f'c_S{i}', stack=st) for i in range(4)]
        ones_off = CST_MAP['ones'][0]
        bi = 0
        ci = 0
        for d in range(2):
            for h in range(4):
                fw.memset(Sst[h], 0.0)
            mi_off = CST_MAP[f'mi64_{d}'][0]
            ms_off = CST_MAP[f'ms64_{d}'][0]
            msT_off = CST_MAP[f'ms64_{1 - d}'][0]
            mi = fw.cst[0:C, mi_off:mi_off + C]
            for (t0, n) in blocks_for(d):
                nn = n // C
                q, k, kv, glt, o = qb[bi % 2], kb[bi % 2], kvb[bi % 2], glb[bi % 2], ob[bi % 2]
                bi += 1
                fw.dma('sp', fv(q, 0, [[512, 4], [1, n]], 0, 64), S['C_QT'][:, t0:t0 + n].re('(h p) t -> p h t', p=64))
                fw.dma('sp', fv(k, 0, [[512, 4], [1, n]], 0, 64), S['C_KT'][:, t0:t0 + n].re('(h p) t -> p h t', p=64))
                fw.dma('sp', fv(kv, 0, [[512, nn], [1, 512]], 0, C), S['C_KV'][t0:t0 + n, :].re('(n c) d -> c n d', c=C))
                fw.dma('sp', fv(glt, 0, [[16, nn], [1, 16]], 0, C), S['C_G'][t0:t0 + n, :].re('(n c) d -> c n d', c=C))
                fw.copy(fv(bb, 0, [[4, nn], [1, 4]], 0, C), fv(glt, 4 * d, [[16, nn], [1, 4]], 0, C))
                fw.copy(fv(lab, 0, [[4, nn], [1, 4]], 0, C), fv(glt, 8 + 4 * d, [[16, nn], [1, 4]], 0, C))
                pc = psb[7]
                fw.mm(pc[0:C, 0:nn * 4], mi, lab[0:C, 0:nn * 4])
                fw.mm(pc[0:C, 32:32 + nn * 4], fw.cst[0:C, ones_off:ones_off + C], lab[0:C, 0:nn * 4])
                fw.act(ecum[:, 0:nn * 4], pc[0:C, 0:nn * 4], AF.Exp)
                fw.ts(necum[:, 0:nn * 4], ecum[:, 0:nn * 4], -1.0, ALU.mult)
                fw.act(etot[:, 0:nn * 4], pc[0:C, 32:32 + nn * 4], AF.Exp)
                fw.copy(dd[:, 0:nn * 4], pc[0:C, 0:nn * 4])
                fw.tt(dd[:, 0:nn * 4], pc[0:C, 32:32 + nn * 4], dd[:, 0:nn * 4], ALU.subtract)
                fw.act(elast[:, 0:nn * 4], dd[:, 0:nn * 4], AF.Exp)
                order = range(nn) if d == 0 else range(nn - 1, -1, -1)
                for n_ in order:
                    a = n_ * C
                    psD, psK, psQ, psL, psR, psO = psb[0], psb[1], psb[2], psb[3], psb[4], psb[2]
                    psY, psU, psL2 = psb[5], psb[6], psb[7]
                    i2 = ci % 2
                    ci += 1
                    ktok = kv[0:C, n_ * 512: n_ * 512 + 256]
                    vtok = kv[0:C, n_ * 512 + 256: n_ * 512 + 512]
                    fw.tt(fv(LA[i2], 0, [[C, 4], [1, C]], 0, C), fv(fw.cst, msT_off, [[0, 4], [1, C]], 0, C),
                          bc3(lab, n_ * 4, C, C), ALU.mult, e='pool')
                    for h in range(4):
                        fw.mm(psD[0:C, h * C:(h + 1) * C], LA[i2][0:C, h * C:(h + 1) * C], mi, inc=(h == 3))
                    fw.act(Gt[i2][0:C, :], psD[0:C, 0:256], AF.Exp)
                    fw.tt(fv(Gi[i2], 0, [[C, 4], [1, C]], 0, C), fv(Gt[i2], 0, [[C, 4], [1, C]], 0, C),
                          fv(fw.cst, mi_off, [[0, 4], [1, C]], 0, C), ALU.mult, e='pool')
                    fw.tt(fv(Gs[i2], 0, [[C, 4], [1, C]], 0, C), fv(Gt[i2], 0, [[C, 4], [1, C]], 0, C),
                          fv(fw.cst, ms_off, [[0, 4], [1, C]], 0, C), ALU.mult, e='pool')
                    fw.tt(fv(Gs[i2], 0, [[C, 4], [1, C]], 0, C), fv(Gs[i2], 0, [[C, 4], [1, C]], 0, C),
                          bc3(bb, n_ * 4, C, C), ALU.mult, e='pool')
                    for h in range(4):
                        cs = slice(h * 512 + a, h * 512 + a + C)
                        fw.mm(psK[0:C, h * C:(h + 1) * C], k[0:64, cs], k[0:64, cs], inc=(h == 3))
                    fw.tt(U0[i2][0:C, :], psK[0:C, 0:256], Gs[i2][0:C, :], ALU.mult)
                    for h in range(4):
                        cs = slice(h * 512 + a, h * 512 + a + C)
                        fw.mm(psQ[0:C, h * C:(h + 1) * C], k[0:64, cs], q[0:64, cs], inc=(h == 3))
                    fw.tt(At[i2][0:C, :], psQ[0:C, 0:256], Gi[i2][0:C, :], ALU.mult)
                    for h in range(4):
                        fw.tr(psL[0:C, h * C:(h + 1) * C], U0[i2][0:C, h * C:(h + 1) * C], ident[0:64, 0:64])
                    fw.copy(L0[i2][0:C, :], psL[0:C, 0:256], e='act')
                    for h in range(4):
                        cs = slice(h * 512 + a, h * 512 + a + C)
                        fw.mm(psR[0:C, h * 64:(h + 1) * 64], k[0:64, cs], Sst[h][0:64, :])
                    fw.tt(fv(Yt[i2], 0, [[64, 4], [1, 64]], 0, C), fv(psR, 0, [[64, 4], [1, 64]], 0, C),
                          bc3(necum, n_ * 4, C, 64), ALU.mult)
                    fw.tt(Yt[i2][0:C, :], Yt[i2][0:C, :], vtok, ALU.add)
                    neumann_solve(fw, U0[i2], L0[i2], Yt[i2], UL, (psY, psU, psL2))
                    fw.tt(fv(vn[i2], 0, [[64, 4], [1, 64]], 0, C), fv(Yt[i2], 0, [[64, 4], [1, 64]], 0, C),
                          bc3(bb, n_ * 4, C, 64), ALU.mult)
                    for h in range(4):
                        cs = slice(h * 512 + a, h * 512 + a + C)
                        fw.mm(psO[0:C, h * 64:(h + 1) * 64], q[0:64, cs], Sst[h][0:64, :])
                    fw.tt(fv(tmpo[i2], 0, [[64, 4], [1, 64]], 0, C), fv(psO, 0, [[64, 4], [1, 64]], 0, C),
                          bc3(ecum, n_ * 4, C, 64), ALU.mult)
                    for h in range(4):
                        fw.mm(psD[0:C, h * 64:(h + 1) * 64], At[i2][0:C, h * C:(h + 1) * C], vn[i2][0:C, h * 64:(h + 1) * 64])
                    fw.tt(o[0:C, n_ * 256:(n_ + 1) * 256], tmpo[i2][0:C, :], psD[0:C, 0:256], ALU.add)
                    fw.tt(fv(Kh[i2], 0, [[64, 4], [1, 64]], 0, C), fv(kv, n_ * 512, [[64, 4], [1, 64]], 0, C),
                          bc3(elast, n_ * 4, C, 64), ALU.mult, e='pool')
                    for h in range(4):
                        fw.mm(psK[0:64, h * 64:(h + 1) * 64], Kh[i2][0:C, h * 64:(h + 1) * 64], vn[i2][0:C, h * 64:(h + 1) * 64])
                        fw.stt(Sst[h][0:64, :], Sst[h][0:64, :], etot[0:64, n_ * 4 + h: n_ * 4 + h + 1],
                               psK[0:64, h * 64:(h + 1) * 64], ALU.mult, ALU.add)
                fw.dma('pool', Od[d][t0:t0 + n, :].re('(n c) d -> c n d', c=C), fv(o, 0, [[256, nn], [1, 256]], 0, C))
        fw.barrier()
    if stop <= 2:
        return
    finish_rms_gate(fw, L, Od, PC[:, 768:1024], 'gdn_norm_g', 512, AF.Silu, Y, colmajor, 'cf')


RWKV_GN_EPS = 64e-5
TP = T + 4
SEGS = [(1, 0, NCTX), (NCTX + 3, NCTX, NLAT)]


def shift_tile(fw, xp, nb, out, hm, omm, P):
    for ps0, t0, n in SEGS:
        fw.tt(nb[0:P, t0:t0 + n], xp[0:P, ps0 - 1: ps0 - 1 + n], xp[0:P, ps0 + 1: ps0 + 1 + n], ALU.add, e='pool')
        fw.ts(nb[0:P, t0:t0 + n], nb[0:P, t0:t0 + n], hm, ALU.mult)
        fw.stt(out[0:P, t0:t0 + n], xp[0:P, ps0: ps0 + n], omm, nb[0:P, t0:t0 + n], ALU.mult, ALU.add)


def load_padded(fw, xp, src, P):
    for ps0, t0, n in SEGS:
        fw.dma('sp', xp[0:P, ps0:ps0 + n], src[:, t0:t0 + n])


def rwkv7(fw, L, S, Y, layer, colmajor, stop=9):
    C = 64
    psb = fw.psb
    ident = fw.ident
    PT = S['fm_B_all']
    LW = [S['B_LW0'], S['B_LW1']]
    AA = [S['B_AA0'], S['B_AA1']]
    KD = [S['B_KD0'], S['B_KD1']]
    BB = [S['B_BB0'], S['B_BB1']]
    blocks9 = [(t0, min(512, T - t0)) for t0 in range(0, T, 512)]
    with ExitStack() as st:
        pc = fw.sb([128, 14], name='r_pc', stack=st)
        fw.dma('sp', pc, L['rwkv_pcol'])
        mu128 = fw.sb([128, 9], name='r_mu128', stack=st)
        mu64 = fw.sb([64, 18], name='r_mu64', stack=st)
        fw.dma('sp', mu128, L['rwkv_mu128'])
        fw.dma('sp', mu64, L['rwkv_mu64'])
        hm128 = fw.sb([128, 9], name='r_hm128', stack=st)
        om128 = fw.sb([128, 9], name='r_om128', stack=st)
        hm64 = fw.sb([64, 18], name='r_hm64', stack=st)
        om64 = fw.sb([64, 18], name='r_om64', stack=st)
        fw.ts(hm128, mu128, 0.5, ALU.mult)
        fw.ts(om128, mu128, -1.0, ALU.mult, s2=1.0, op1=ALU.add)
        fw.ts(hm64, mu64, 0.5, ALU.mult)
        fw.ts(om64, mu64, -1.0, ALU.mult, s2=1.0, op1=ALU.add)
        npc = fw.sb([128, 14], name='r_npc', stack=st)
        fw.ts(npc, pc, -1.0, ALU.mult)
        omka = fw.sb([128, 2], name='r_omka', stack=st)
        fw.ts(omka, pc[:, 2:4], -1.0, ALU.mult, s2=1.0, op1=ALU.add)
        w2 = fw.sb([64, 512], name='r_w2', stack=st)
        a2 = fw.sb([64, 512], name='r_a2', stack=st)
        g2 = fw.sb([128, 256], name='r_g2', stack=st)
        fw.dma('sp', fv(w2, 0, [[256, 2], [1, 256]], 0, 64), V(L['rwkv_w2'], L['rwkv_w2'].ap().rearrange('d r c -> r d c')))
        fw.dma('sp', fv(a2, 0, [[256, 2], [1, 256]], 0, 64), V(L['rwkv_a2'], L['rwkv_a2'].ap().rearrange('d r c -> r d c')))
        fw.dma('sp', g2, L['rwkv_g2'])
        xp = [fw.sb([128, TP], name=f'r_xp{i}', stack=st) for i in range(2)]
        fw.memset(xp[0], 0.0)
        fw.memset(xp[1], 0.0)
        nb = fw.sb([128, T], name='r_nb', stack=st)
        stg = [fw.sb([128, 512], name=f'r_stg{i}', stack=st) for i in range(3)]
        si = 0
        with ExitStack() as st1:
            lo = fw.sb([64, T], name='r_lo', stack=st1)
            for kind in range(2):
                for d in range(2):
                    row0 = 768 + kind * 128 + d * 64
                    col = row0 // 64
                    x_ = xp[(kind * 2 + d) % 2]
                    load_padded(fw, x_, PT[row0:row0 + 64, :], 64)
                    shift_tile(fw, x_, nb, lo, hm64[:, col:col + 1], om64[:, col:col + 1], 64)
                    if kind == 0:
                        fw.act(lo, lo, AF.Tanh)
                    wsrc = w2 if kind == 0 else a2
                    for ct in range(2):
                        bcol = (6 if kind == 0 else 10) + d * 2 + ct
                        for (t0, n) in blocks9:
                            p = psb[si % 4]
                            sg = stg[si % 3]
                            si += 1
                            fw.mm(p[:, 0:n], wsrc[0:64, d * 256 + ct * 128: d * 256 + (ct + 1) * 128], lo[0:64, t0:t0 + n])
                            fw.act(sg[:, 0:n], p[:, 0:n], AF.Exp, bias=npc[:, bcol:bcol + 1], scale=-1.0)
                            fw.ts(sg[:, 0:n], sg[:, 0:n], 1.0, ALU.add)
                            fw.recip(sg[:, 0:n], sg[:, 0:n])
                            if kind == 0:
                                fw.ts(sg[:, 0:n], sg[:, 0:n], -0.6065306597126334, ALU.mult, e='pool')
                                fw.dma('pool', LW[d][ct * 128:(ct + 1) * 128, t0:t0 + n], sg[:, 0:n])
                            else:
                                fw.dma('pool', AA[d][ct * 128:(ct + 1) * 128, t0:t0 + n], sg[:, 0:n])
            fw.barrier()
        with ExitStack() as st2:
            rs_ = fw.sb([128, T], name='r_rs', stack=st2)
            ks_ = fw.sb([128, T], name='r_ks', stack=st2)
            kk_ = fw.sb([128, T], name='r_kk', stack=st2)
            aa_ = [fw.sb([128, T], name=f'r_aa{i}', stack=st2) for i in range(2)]
            t1_ = fw.sb([128, T], name='r_t1', stack=st2)
            sbs = fw.sb([128, 34 * 4], name='r_sbs', stack=st2)
            hs_off = CST_MAP['hsel'][0]
            b64_off = CST_MAP['blk64'][0]
            for ct in range(2):
                load_padded(fw, xp[0], PT[ct * 128:(ct + 1) * 128, :], 128)
                shift_tile(fw, xp[0], nb, rs_, hm128[:, ct:ct + 1], om128[:, ct:ct + 1], 128)
                fw.dma('pool', S['B_RT'][ct * 128:(ct + 1) * 128, :], rs_)
                load_padded(fw, xp[1], PT[256 + ct * 128: 256 + (ct + 1) * 128, :], 128)
                shift_tile(fw, xp[1], nb, ks_, hm128[:, 2 + ct:3 + ct], om128[:, 2 + ct:3 + ct], 128)
                fw.ts(kk_, ks_, pc[:, ct:ct + 1], ALU.mult)
                fw.tt(t1_, kk_, kk_, ALU.mult, e='pool')
                for (t0, n) in blocks9:
                    p = psb[si % 4]
                    si += 1
                    fw.mm(p[:, 0:n], fw.cst[:, b64_off:b64_off + 128], t1_[:, t0:t0 + n])
                    fw.ts(nb[:, t0:t0 + n], p[:, 0:n], 1e-12, ALU.add)
                fw.act(nb, nb, AF.Sqrt)
                fw.recip(nb, nb)
                fw.tt(kk_, kk_, nb, ALU.mult)
                fw.dma('pool', S['B_KK'][ct * 128:(ct + 1) * 128, :], kk_)
                pS = psb[4 + ct]
                for d in range(2):
                    a_ = aa_[d]
                    fw.dma('sp', a_, AA[d][ct * 128:(ct + 1) * 128, :])
                    fw.tt(t1_, kk_, a_, ALU.mult, e='pool')
                    fw.dma('pool', BB[d][ct * 128:(ct + 1) * 128, :], t1_)
                    fw.ts(a_, a_, pc[:, 2 + ct:3 + ct], ALU.mult, s2=omka[:, ct:ct + 1], op1=ALU.add)
                    fw.tt(a_, a_, ks_, ALU.mult)
                    fw.dma('pool', KD[d][ct * 128:(ct + 1) * 128, :], a_)
                    fw.stt(a_, a_, pc[:, 4 + ct:5 + ct], rs_, ALU.mult, ALU.mult)
                    for ti in range(34):
                        fw.mm(pS[:, d * 68 + ti * 2: d * 68 + ti * 2 + 2], a_[:, ti * 128:(ti + 1) * 128], fw.cst[:, hs_off:hs_off + 2],
                              inc=(ti == 33))
                fw.copy(fv(sbs, ct * 2, [[4, 34], [1, 2]]), fv(pS, 0, [[2, 34], [1, 2]]))
                fw.tt(fv(sbs, ct * 2, [[4, 34], [1, 2]]), fv(sbs, ct * 2, [[4, 34], [1, 2]]), fv(pS, 68, [[2, 34], [1, 2]]), ALU.add)
            fw.dma('pool', S['B_SB'][:, :].re('(n p) c -> p n c', p=128), fv(sbs, 0, [[4, 34], [1, 4]]))
            fw.barrier()
        with ExitStack() as st3:
            gs_ = fw.sb([128, T], name='r_gs', stack=st3)
            load_padded(fw, xp[0], PT[1024:1152, :], 128)
            shift_tile(fw, xp[0], nb, gs_, hm128[:, 8:9], om128[:, 8:9], 128)
            fw.act(gs_, gs_, AF.Sigmoid)
            for ti in range(34):
                p = psb[ti % 4]
                sg = stg[ti % 3]
                fw.mm(p[:, 0:256], gs_[:, ti * 128:(ti + 1) * 128], g2[:, :])
                fw.copy(sg[:, 0:256], p[:, 0:256], e=('act' if ti % 2 else 'dve'))
                fw.dma('pool', S['B_GATE'][ti * 128:(ti + 1) * 128, :], sg[:, 0:256])
            fw.barrier()
        with ExitStack() as st4:
            mvb = fw.sb([128, 256], name='r_mvb', stack=st4)
            fw.dma('sp', mvb, dram_bcast_rows(L['rwkv_mu_row'], 512, 256))
            vv = [[fw.sb([128, 256], name=f'r_v{j}{i}', stack=st4) for j in range(3)] for i in range(2)]
            for ti in range(34):
                t0 = ti * 128
                seg0, seg1 = (0, NCTX) if ti < 2 else (NCTX, T)
                X = vv[ti % 2]
                for j in range(3):
                    s = j - 1
                    lo_, hi_ = max(seg0, t0 + s), min(seg1, t0 + 128 + s)
                    if lo_ != t0 + s or hi_ != t0 + 128 + s:
                        fw.memset(X[j], 0.0, e='pool')
                    fw.dma('sp', X[j][lo_ - (t0 + s): hi_ - (t0 + s), :], S['tm_B_v'][lo_:hi_, :])
                fw.tt(X[0], X[0], X[2], ALU.add, e='pool')
                fw.stt(X[0], X[0], 0.5, X[1], ALU.mult, ALU.subtract)
                fw.tt(X[0], X[0], mvb, ALU.mult)
                fw.tt(X[0], X[0], X[1], ALU.add)
                fw.dma('pool', S['B_VS'][t0:t0 + 128, :], X[0])
            fw.barrier()
    if stop <= 1:
        return
    Od = [S['B_O0'], S['B_O1']]
    with ExitStack() as st:
        names = ['r', 'kd', 'kk', 'bb', 'lw']
        inb = {nm: [fw.sb([64, 2048], name=f'rc_{nm}{i}', stack=st) for i in range(1)] for nm in names}
        vb = [fw.sb([C, 8 * 256], name=f'rc_v{i}', stack=st) for i in range(2)]
        ob = [fw.sb([C, 8 * 256], name=f'rc_o{i}', stack=st) for i in range(2)]
        Gz = fw.sb([64, 4 * GS], name='rc_Gz', stack=st)
        cum = fw.sb([64, 2048], name='rc_cum', stack=st)
        cumx = fw.sb([64, 2048], name='rc_cumx', stack=st)
        d3 = fw.sb([64, 2048], name='rc_d3', stack=st)
        Ea = fw.sb([64, 2048], name='rc_Ea', stack=st)
        Eb = fw.sb([64, 2048], name='rc_Eb', stack=st)
        Rt = fw.sb([64, 2048], name='rc_Rt', stack=st)
        Kb = fw.sb([64, 2048], name='rc_Kb', stack=st)
        Bb = fw.sb([64, 2048], name='rc_Bb', stack=st)
        Ah = fw.sb([64, 2048], name='rc_Ah', stack=st)
        Kh = fw.sb([64, 2048], name='rc_Kh', stack=st)
        Bh = fw.sb([64, 2048], name='rc_Bh', stack=st)
        glast = fw.sb([64, 64], name='rc_glast', stack=st)
        U0 = fw.sb([C, 256], name='rc_U0', stack=st)
        L0 = fw.sb([C, 256], name='rc_L0', stack=st)
        Gm = fw.sb([C, 256], name='rc_Gm', stack=st)
        Pm = fw.sb([C, 256], name='rc_Pm', stack=st)
        Qm = fw.sb([C, 256], name='rc_Qm', stack=st)
        UL = [(fw.sb([C, 256], name=f'rc_Un{i}', stack=st), fw.sb([C, 256], name=f'rc_Ln{i}', stack=st)) for i in range(2)]
        Yt = fw.sb([C, 256], name='rc_Y', stack=st)
        KBt = fw.sb([C, 512], name='rc_KBt', stack=st)
        Sst = [fw.sb([64, 64], name=f'rc_S{i}', stack=st) for i in range(4)]
        ones = fw.sb([64, 512], name='rc_ones', stack=st)
        negm = fw.sb([C, 128], name='rc_negm', stack=st)
        fw.memset(ones, 1.0)
        fw.memset(Gz, 0.0)
        for d in range(2):
            o_, n_ = CST_MAP[f'mi64_{d}']
            fw.ts(negm[:, d * 64:(d + 1) * 64], fw.cst[0:C, o_:o_ + 64], -1.0, ALU.mult)
        bi = 0
        for d in range(2):
            for h in range(4):
                fw.memset(Sst[h], 0.0)
            mi_off = CST_MAP[f'mi64_{d}'][0]
            ms_off = CST_MAP[f'ms64_{d}'][0]
            msT_off = CST_MAP[f'ms64_{1 - d}'][0]
            srcs = {'r': S['B_RT'], 'kd': KD[d], 'kk': S['B_KK'], 'bb': BB[d], 'lw': LW[d]}
            for (t0, n) in blocks_for(d):
                nn = n // C
                cur = {nm: inb[nm][0] for nm in names}
                v, o = vb[bi % 2], ob[bi % 2]
                bi += 1

                def v3(tl):
                    return fv(tl, 0, [[512, 4], [1, n]], 0, 64)
                for nm in names:
                    fw.dma('sp', v3(cur[nm]), srcs[nm][:, t0:t0 + n].re('(h p) t -> p h t', p=64))
                fw.dma('sp', fv(v, 0, [[256, nn], [1, 256]], 0, C), S['B_VS'][t0:t0 + n, :].re('(n c) d -> c n d', c=C))
                l = cur['lw']
                for h in range(4):
                    fw.scan(Gz[:, h * GS + 1: h * GS + 1 + n], ones[:, 0:n], l[:, h * 512: h * 512 + n], 0.0, ALU.mult, ALU.add)
                lastc = C - 1 if d == 0 else 0
                for h in range(4):
                    oc = fv(cum, h * 512, [[C, nn], [1, C]], 0, 64)
                    ox = fv(cumx, h * 512, [[C, nn], [1, C]], 0, 64)
                    if d == 0:
                        g0 = fv(Gz, h * GS, [[C, nn], [0, C]], 0, 64)
                        fw.tt(oc, fv(Gz, h * GS + 1, [[C, nn], [1, C]], 0, 64), g0, ALU.subtract)
                        fw.tt(ox, fv(Gz, h * GS, [[C, nn], [1, C]], 0, 64), g0, ALU.subtract, e='pool')
                    else:
                        g1 = fv(Gz, h * GS + C, [[C, nn], [0, C]], 0, 64)
                        fw.tt(oc, g1, fv(Gz, h * GS, [[C, nn], [1, C]], 0, 64), ALU.subtract)
                        fw.tt(ox, g1, fv(Gz, h * GS + 1, [[C, nn], [1, C]], 0, 64), ALU.subtract, e='pool')
                    fw.tt(fv(d3, h * 512, [[C, nn], [1, C]], 0, 64), oc,
                          fv(cum, h * 512 + lastc, [[C, nn], [0, C]], 0, 64), ALU.subtract)
                fw.act(v3(Ea), v3(cum), AF.Exp)
                fw.tt(v3(Rt), v3(cur['r']), v3(Ea), ALU.mult)
                fw.copy(fv(glast, 0, [[16, 4], [1, nn]], 0, 64), fv(Ea, lastc, [[512, 4], [C, nn]], 0, 64))
                fw.act(v3(Eb), v3(cum), AF.Exp, scale=-1.0)
                fw.tt(v3(Kb), v3(cur['kd']), v3(Eb), ALU.mult)
                fw.tt(v3(Bb), v3(cur['bb']), v3(Eb), ALU.mult, e='pool')
                fw.act(v3(Ea), v3(cumx), AF.Exp)
                fw.tt(v3(Ah), v3(cur['kk']), v3(Ea), ALU.mult)
                fw.act(v3(Eb), v3(d3), AF.Exp, scale=-1.0)
                fw.tt(v3(Kh), v3(cur['kd']), v3(Eb), ALU.mult)
                fw.tt(v3(Bh), v3(cur['bb']), v3(Eb), ALU.mult, e='pool')
                order = range(nn) if d == 0 else range(nn - 1, -1, -1)
                for n_ in order:
                    a = n_ * C

                    def hc(tl, h):
                        return tl[0:64, h * 512 + a: h * 512 + a + C]

                    def msk(off, neg=False):
                        if neg:
                            return fv(negm, d * 64, [[0, 4], [1, C]], 0, C)
                        return fv(fw.cst, off, [[0, 4], [1, C]], 0, C)

                    def m4(tl):
                        return fv(tl, 0, [[C, 4], [1, C]], 0, C)
                    for h in range(4):
                        fw.mm(psb[0][0:C, h * C:(h + 1) * C], hc(Bb, h), hc(Ah, h), inc=(h == 3))
                    fw.tt(m4(U0), m4(psb[0]), msk(ms_off), ALU.mult)
                    for h in range(4):
                        fw.mm(psb[1][0:C, h * C:(h + 1) * C], hc(Ah, h), hc(Bb, h), inc=(h == 3))
                    fw.tt(m4(L0), m4(psb[1]), msk(msT_off), ALU.mult)
                    for h in range(4):
                        fw.mm(psb[2][0:C, h * C:(h + 1) * C], hc(Kb, h), hc(Ah, h), inc=(h == 3))
                    fw.tt(m4(Gm), m4(psb[2]), msk(ms_off), ALU.mult)
                    for h in range(4):
                        fw.mm(psb[3][0:C, h * C:(h + 1) * C], hc(Kb, h), hc(Rt, h), inc=(h == 3))
                    fw.tt(m4(Pm), m4(psb[3]), msk(mi_off), ALU.mult)
                    for h in range(4):
                        fw.mm(psb[4][0:C, h * C:(h + 1) * C], hc(Bb, h), hc(Rt, h), inc=(h == 3))
                    fw.tt(m4(Qm), m4(psb[4]), msk(0, neg=True), ALU.mult)
                    for h in range(4):
                        fw.mm(psb[2][0:C, h * 64:(h + 1) * 64], hc(Ah, h), Sst[h][0:64, :], start=True, stop=False, inc=False)
                        fw.mm(psb[2][0:C, h * 64:(h + 1) * 64], Gm[0:C, h * C:(h + 1) * C],
                              v[0:C, n_ * 256 + h * 64: n_ * 256 + (h + 1) * 64], start=False, stop=True, inc=(h == 3))
                    fw.copy(Yt[0:C, :], psb[2][0:C, 0:256], e='act')
                    neumann_solve(fw, U0, L0, Yt, UL, (psb[5], psb[6], psb[7]))
                    for h in range(4):
                        fw.mm(psb[3][0:C, h * 64:(h + 1) * 64], hc(Rt, h), Sst[h][0:64, :], start=True, stop=False, inc=False)
                        fw.mm(psb[3][0:C, h * 64:(h + 1) * 64], Pm[0:C, h * C:(h + 1) * C],
                              v[0:C, n_ * 256 + h * 64: n_ * 256 + (h + 1) * 64], start=False, stop=False, inc=False)
                        fw.mm(psb[3][0:C, h * 64:(h + 1) * 64], Qm[0:C, h * C:(h + 1) * C],
                              Yt[0:C, h * 64:(h + 1) * 64], start=False, stop=True, inc=(h == 3))
                    fw.copy(o[0:C, n_ * 256:(n_ + 1) * 256], psb[3][0:C, 0:256], e='act')
                    for h in range(4):
                        fw.tr(psb[4][0:C, h * 64:(h + 1) * 64], hc(Kh, h), ident[0:64, 0:64])
                        fw.tr(psb[4][0:C, 256 + h * 64: 256 + (h + 1) * 64], hc(Bh, h), ident[0:64, 0:64])
                    fw.copy(KBt[0:C, 0:256], psb[4][0:C, 0:256], e='act')
                    fw.ts(KBt[0:C, 256:512], psb[4][0:C, 256:512], -1.0, ALU.mult)
                    for h in range(4):
                        fw.mm(psb[0][0:64, h * 64:(h + 1) * 64], KBt[0:C, h * 64:(h + 1) * 64],
                              v[0:C, n_ * 256 + h * 64: n_ * 256 + (h + 1) * 64], start=True, stop=False, inc=False)
                        fw.mm(psb[0][0:64, h * 64:(h + 1) * 64], KBt[0:C, 256 + h * 64: 256 + (h + 1) * 64],
                              Yt[0:C, h * 64:(h + 1) * 64], start=False, stop=True)
                        fw.stt(Sst[h][0:64, :], Sst[h][0:64, :], glast[0:64, h * 16 + n_: h * 16 + n_ + 1],
                               psb[0][0:64, h * 64:(h + 1) * 64], ALU.mult, ALU.add)
                fw.dma('pool', Od[d][t0:t0 + n, :].re('(n c) d -> c n d', c=C), fv(o, 0, [[256, nn], [1, 256]], 0, C))
        fw.barrier()
    if stop <= 2:
        return
    with ExitStack() as st:
        gb = fw.sb([128, 256], name='rf_gb', stack=st)
        bbt = fw.sb([128, 256], name='rf_bb', stack=st)
        fw.dma('sp', gb, dram_bcast_rows(L['rwkv_ln_g'], 0, 256))
        fw.dma('sp', bbt, dram_bcast_rows(L['rwkv_ln_b'], 0, 256))
        o0 = [fw.sb([128, 256], name=f'rf_o0{i}', stack=st) for i in range(2)]
        o1 = [fw.sb([128, 256], name=f'rf_o1{i}', stack=st) for i in range(2)]
        gt = [fw.sb([128, 256], name=f'rf_g{i}', stack=st) for i in range(2)]
        vs = [fw.sb([128, 256], name=f'rf_v{i}', stack=st) for i in range(2)]
        sb_ = [fw.sb([128, 4], name=f'rf_sb{i}', stack=st) for i in range(2)]
        yy = [fw.sb([128, 256], name=f'rf_y{i}', stack=st) for i in range(2)]
        sq = fw.sb([128, 256], name='rf_sq', stack=st)
        mu_ = fw.sb([128, 4], name='rf_mu', stack=st)
        var = fw.sb([128, 4], name='rf_var', stack=st)
        for ti in range(34):
            rs = slice(ti * 128, (ti + 1) * 128)
            a0, a1, g_, v_, s_, y_ = o0[ti % 2], o1[ti % 2], gt[ti % 2], vs[ti % 2], sb_[ti % 2], yy[ti % 2]
            fw.dma('sp', a0, Od[0][rs, :])
            fw.dma('sp', a1, Od[1][rs, :])
            fw.dma('sp', g_, S['B_GATE'][rs, :])
            fw.dma('sp', v_, S['B_VS'][rs, :])
            fw.dma('sp', s_, S['B_SB'][rs, :])
            fw.tt(a0, a0, a1, ALU.add)
            fw.reduce(mu_, fv(a0, 0, [[64, 4], [1, 64]]), ALU.add)
            fw.ts(mu_, mu_, 1.0 / 64, ALU.mult)
            fw.tt(fv(a0, 0, [[64, 4], [1, 64]]), fv(a0, 0, [[64, 4], [1, 64]]), fv(mu_, 0, [[1, 4], [0, 64]]), ALU.subtract)
            fw.tt(sq, a0, a0, ALU.mult, e='pool')
            fw.reduce(var, fv(sq, 0, [[64, 4], [1, 64]]), ALU.add)
            fw.ts(var, var, 1.0 / 64, ALU.mult, s2=RWKV_GN_EPS, op1=ALU.add)
            fw.act(var, var, AF.Sqrt)
            fw.recip(var, var)
            fw.tt(fv(y_, 0, [[64, 4], [1, 64]]), fv(a0, 0, [[64, 4], [1, 64]]), fv(var, 0, [[1, 4], [0, 64]]), ALU.mult)
            fw.tt(y_, y_, gb, ALU.mult, e='pool')
            fw.tt(y_, y_, bbt, ALU.add)
            fw.tt(fv(v_, 0, [[64, 4], [1, 64]]), fv(v_, 0, [[64, 4], [1, 64]]), fv(s_, 0, [[1, 4], [0, 64]]), ALU.mult, e='pool')
            fw.tt(y_, y_, v_, ALU.add)
            fw.tt(y_, y_, g_, ALU.mult)
            store_rows_scan(fw, Y, 256, 256, y_[:, :], ti, colmajor)
        fw.barrier()


SCRATCH = {
    'fm_A_q': [256, T], 'fm_A_f': [512, T], 'fm_B_all': [1152, T], 'fm_D_qk': [512, T], 'fm_D_gates': [16, T],
    'tm_A_ig': [T, 512], 'tm_B_v': [T, 256], 'tm_C': [T, 1024], 'tm_C_ba': [T, 16], 'tm_D_vo': [T, 512], 'tm_D_gates': [T, 16],
    'A_KT0': [256, T], 'A_KT1': [256, T], 'A_LF0': [256, T], 'A_LF1': [256, T], 'A_O0': [T, 256], 'A_O1': [T, 256],
    'B_LW0': [256, T], 'B_LW1': [256, T], 'B_AA0': [256, T], 'B_AA1': [256, T], 'B_KD0': [256, T], 'B_KD1': [256, T],
    'B_BB0': [256, T], 'B_BB1': [256, T], 'B_RT': [256, T], 'B_KK': [256, T], 'B_VS': [T, 256], 'B_GATE': [T, 256], 'B_SB': [T, 4],
    'B_O0': [T, 256], 'B_O1': [T, 256],
    'C_QT': [256, T], 'C_KT': [256, T], 'C_KV': [T, 512], 'C_G': [T, 16], 'C_O0': [T, 256], 'C_O1': [T, 256],
    'D_LF0': [256, T], 'D_LF1': [256, T], 'D_VX0': [T, 260], 'D_VX1': [T, 260], 'D_O0': [T, 260], 'D_O1': [T, 260],
    'Y': [T, 1024], 'X1': [T, 1024],
}


def host_inputs_for_layer(inp, l, b):
    cc = np.stack([inp['c_ctx'], inp['c'][b]], 0)
    d = {}
    d['c2T'] = np.ascontiguousarray(cc.T.reshape(8, 128, 2).transpose(1, 0, 2).reshape(128, 16))
    d['ada_w'] = inp['ada_w'][l]
    ab = inp['ada_b'][l]
    d['ada_b'] = ab.reshape(1, 6144)
    d['adab_col'] = np.ascontiguousarray(ab.reshape(48, 128).T)
    win = inp['w_in'][l]
    d['w_fm'] = np.ascontiguousarray(win[:, FM_COLS])
    d['w_tm'] = np.ascontiguousarray(win[:, TM_COLS])
    d['ln'] = np.stack([inp[k][l] for k in ('ln1_g', 'ln1_b', 'ln2_g', 'ln2_b')], 0)
    d['w_out'] = inp['w_out'][l]
    d['mlp_w1'] = inp['mlp_w1'][l]
    d['mlp_w2'] = inp['mlp_w2'][l]
    gam = inp['hgrn_gamma']
    d['hgrn_gamma_t'] = np.ascontiguousarray(gam.reshape(4, 2, 2, 128).transpose(3, 1, 2, 0).reshape(128, 16))
    d['hgrn_norm_g'] = inp['hgrn_norm_g'][l].reshape(1, 256)
    fb = np.zeros((16, 1), np.float32)
    fb[8:, 0] = inp['mlstm_f_bias'][l].reshape(8)
    d['mlstm_fb_col'] = fb
    d['mlstm_i_bias'] = inp['mlstm_i_bias'][l].reshape(1, 8)
    d['mlstm_norm_g'] = inp['mlstm_norm_g'][l].reshape(1, 256)
    d['gdn_conv'] = inp['gdn_conv'][l].reshape(1, 5 * 768)
    d['gdn_dt_bias'] = inp['gdn_dt_bias'][l].reshape(1, 8)
    d['gdn_a_log'] = inp['gdn_a_log'][l].reshape(1, 8)
    d['gdn_norm_g'] = inp['gdn_norm_g'][l].reshape(1, 256)
    pc = np.zeros((128, 14), np.float32)

    def c2(v):
        return v.reshape(2, 128).T
    pc[:, 0:2] = c2(inp['rwkv_k_k'][l])
    pc[:, 2:4] = c2(inp['rwkv_k_a'][l])
    pc[:, 4:6] = c2(inp['rwkv_r_k'][l].reshape(256))
    pc[:, 6:8] = c2(inp['rwkv_w0'][l][0])
    pc[:, 8:10] = c2(inp['rwkv_w0'][l][1])
    pc[:, 10:12] = c2(inp['rwkv_a0'][l][0])
    pc[:, 12:14] = c2(inp['rwkv_a0'][l][1])
    d['rwkv_pcol'] = pc
    mu = inp['rwkv_mu'][l]
    d['rwkv_mu128'] = np.ascontiguousarray(mu.reshape(9, 128).T)
    d['rwkv_mu64'] = np.ascontiguousarray(mu.reshape(18, 64).T)
    d['rwkv_mu_row'] = mu.reshape(1, 1152)
    d['rwkv_w2'] = inp['rwkv_w2'][l]
    d['rwkv_a2'] = inp['rwkv_a2'][l]
    d['rwkv_g2'] = inp['rwkv_g2'][l]
    d['rwkv_ln_g'] = inp['rwkv_ln_g'][l].reshape(1, 256)
    d['rwkv_ln_b'] = inp['rwkv_ln_b'][l].reshape(1, 256)
    return {k: np.ascontiguousarray(v, dtype=np.float32) for k, v in d.items()}


LAYER_INPUT_SHAPES = None


def layer_input_shapes():
    global LAYER_INPUT_SHAPES
    if LAYER_INPUT_SHAPES is None:
        fake = {
            'c_ctx': np.zeros(1024, np.float32), 'c': np.zeros((1, 1024), np.float32),
            'ada_w': np.zeros((1, 1024, 6144), np.float32), 'ada_b': np.zeros((1, 6144), np.float32),
            'w_in': np.zeros((1, 1024, 4512), np.float32), 'w_out': np.zeros((1, 1024, 1024), np.float32),
            'ln1_g': np.zeros((1, 1024), np.float32), 'ln1_b': np.zeros((1, 1024), np.float32),
            'ln2_g': np.zeros((1, 1024), np.float32), 'ln2_b': np.zeros((1, 1024), np.float32),
            'mlp_w1': np.zeros((1, 1024, 4096), np.float32), 'mlp_w2': np.zeros((1, 4096, 1024), np.float32),
            'hgrn_gamma': np.zeros((4, 2, 256), np.float32), 'hgrn_norm_g': np.zeros((1, 256), np.float32),
            'rwkv_mu': np.zeros((1, 1152), np.float32), 'rwkv_w0': np.zeros((1, 2, 256), np.float32),
            'rwkv_w2': np.zeros((1, 2, 64, 256), np.float32), 'rwkv_a0': np.zeros((1, 2, 256), np.float32),
            'rwkv_a2': np.zeros((1, 2, 64, 256), np.float32), 'rwkv_g2': np.zeros((1, 128, 256), np.float32),
            'rwkv_k_k': np.zeros((1, 256), np.float32), 'rwkv_k_a': np.zeros((1, 256), np.float32),
            'rwkv_r_k': np.zeros((1, 4, 64), np.float32), 'rwkv_ln_g': np.zeros((1, 256), np.float32),
            'rwkv_ln_b': np.zeros((1, 256), np.float32), 'gdn_conv': np.zeros((1, 5, 768), np.float32),
            'gdn_a_log': np.zeros((1, 2, 4), np.float32), 'gdn_dt_bias': np.zeros((1, 2, 4), np.float32),
            'gdn_norm_g': np.zeros((1, 256), np.float32), 'mlstm_i_bias': np.zeros((1, 2, 4), np.float32),
            'mlstm_f_bias': np.zeros((1, 2, 4), np.float32), 'mlstm_norm_g': np.zeros((1, 256), np.float32),
        }
        LAYER_INPUT_SHAPES = {k: list(v.shape) for k, v in host_inputs_for_layer(fake, 0, 0).items()}
    return LAYER_INPUT_SHAPES


def build_program(layers, final_last):
    nc = bass.Bass("TRN2", target_bir_lowering=False)
    with ExitStack() as st, nc.allow_low_precision(reason="bf16 matmul operands, fp32 accumulate"):
        fw = FW(nc, st)
        fw._ps_stack = None
        set_psum(fw, 'full')
        fw.ident = fw.sb([128, 128], name='ident', stack=st)
        fw.memset(fw.ident, 1.0)
        fw.op('pool', lambda: nc.gpsimd.affine_select(out=fw.ident.ap(), in_=fw.ident.ap(), pattern=[[-1, 128]],
              compare_op=ALU.is_equal, fill=0.0, base=0, channel_multiplier=1), [fw.ident], [fw.ident])
        consts = fw.dram('consts', [128, NCST], F32, kind='ExternalInput')
        fw.cst = fw.sb([128, NCST], name='cst', stack=st)
        fw.dma('sp', fw.cst, consts)
        x_ext = fw.dram('x_in', [T, 1024], F32, kind='ExternalInput')
        if final_last:
            out_t = fw.dram('out', [NLAT, 1024], F32, kind='ExternalOutput')
        else:
            out_t = fw.dram('out', [T, 1024], F32, kind='ExternalOutput')
        scr = {k: fw.dram('s_' + k, shp, F32) for k, shp in SCRATCH.items()}
        S = dict(scr)
        S['tm_C_qkv0'] = scr['tm_C'][:, 0:512]
        S['tm_C_qkv1'] = scr['tm_C'][:, 512:1024]
        xpp = [fw.dram(f'xpp{i}', [T, 1024], F32) for i in range(2)] if len(layers) > 1 else []
        shapes = layer_input_shapes()
        x_cur = x_ext
        for li, layer in enumerate(layers):
            is_last_in_prog = (li == len(layers) - 1)
            last = final_last and is_last_in_prog
            L = {k: fw.dram(f'{k}_l{layer}', shp, F32, kind='ExternalInput') for k, shp in shapes.items()}
            cm = set('AB') if layer % 2 == 1 else set('CD')
            with ExitStack() as lst:
                P0 = phase0(fw, L, lst)
                phase1(fw, L, P0, S, cm, x_cur)
                hgrn2(fw, L, S, scr['Y'], layer, 'A' in cm)
                rwkv7(fw, L, S, scr['Y'], layer, 'B' in cm)
                gdn(fw, L, S, scr['Y'], layer, 'C' in cm)
                mlstm(fw, L, S, scr['Y'], layer, 'D' in cm)
                if is_last_in_prog:
                    x_next = out_t
                else:
                    x_next = xpp[li % 2]
                phase3(fw, L, P0, scr['Y'], x_cur, scr['X1'], x_next, last, out_lat=(out_t if last else None))
                fw.barrier()
            x_cur = x_next
        fw.finish()
    return nc


_PROG_CACHE = {}


def get_program(layers, final_last):
    key = (tuple(layers), final_last)
    if key not in _PROG_CACHE:
        _PROG_CACHE[key] = build_program(layers, final_last)
    return _PROG_CACHE[key]


LAUNCH_GROUPS = [[0, 1, 2, 3]]


def kernel(**inputs):
    inp = {k: np.asarray(v, dtype=np.float32) for k, v in inputs.items()}
    B = inp['x'].shape[0]
    ncores = 8
    xs = [np.ascontiguousarray(np.concatenate([inp['ctx'][b], inp['x'][b]], 0)) for b in range(B)]
    out = None
    for gi, layers in enumerate(LAUNCH_GROUPS):
        final_last = layers[-1] == 3
        nc = get_program(layers, final_last)
        in_maps = []
        for core in range(ncores):
            b = core % B
            m = {'consts': CST_ARR, 'x_in': xs[b]}
            for layer in layers:
                for k, v in host_inputs_for_layer(inp, layer, b).items():
                    m[f'{k}_l{layer}'] = v
            in_maps.append(m)
        res = run_bass_kernel_spmd(nc, in_maps, core_ids=list(range(ncores)))
        outs = [np.asarray(res.results[b]['out']) for b in range(B)]
        if final_last:
            out = np.stack(outs, 0).astype(np.float32)
        else:
            xs = outs
    return out
```

```python
from contextlib import ExitStack
import math
import numpy as np
import concourse.bass as bass
import concourse.mybir as mybir
from concourse.bass_utils import run_bass_kernel_spmd

F32 = mybir.dt.float32
BF16 = mybir.dt.bfloat16
AF = mybir.ActivationFunctionType
ALU = mybir.AluOpType
AX = mybir.AxisListType


class Tl:
    def __init__(self, fw, t, kind):
        self.fw = fw
        self.t = t
        self.kind = kind
        self.w = None
        self.r = {}
        self.dsem = None
        self.dcnt = 0
        self.dw = 0
        self.dr = 0
        self.dram_w = {}
        self.dram_r = {}
        self.name = None
        self.dent = None
        self.dkey = None
        self.dbase = 0
        self.is_output = False

    def ap(self):
        return self.t.ap() if self.kind == 'dr' else self.t[:]

    def __getitem__(self, k):
        a = self.t.ap()[k] if self.kind == 'dr' else self.t[k]
        return V(self, a)


class V:
    def __init__(self, tl, ap):
        self.tl = tl
        self.ap = ap

    def __getitem__(self, k):
        return V(self.tl, self.ap[k])

    def re(self, s, **kw):
        return V(self.tl, self.ap.rearrange(s, **kw))

    def bc(self, shape):
        return V(self.tl, self.ap.to_broadcast(shape))


def _v(x):
    if isinstance(x, Tl):
        return V(x, x.ap())
    return x


class Eng:
    def __init__(self, name, e, sem):
        self.name = name
        self.e = e
        self.sem = sem
        self.cnt = 0
        self.seen = {}
        self.seen_d = {}


class FW:
    def __init__(self, nc, stack):
        self.nc = nc
        self.stack = stack
        self.E = {}
        for name, e in (('pe', nc.tensor), ('act', nc.scalar), ('dve', nc.vector),
                        ('pool', nc.gpsimd), ('sp', nc.sync)):
            sem = stack.enter_context(nc.semaphore('s_' + name))
            self.E[name] = Eng(name, e, sem)
        self.ntile = 0
        self.out_dmas = []
        self.dactive = {}
        self.dfree = []
        self.dall = []
        self.dwaited = {}

    def _attach_dsem(self, tl):
        nm = tl.name
        if nm not in self.dactive:
            if self.dfree:
                ent = self.dfree.pop()
            else:
                sem = self.stack.enter_context(self.nc.semaphore('d_%d' % len(self.dall)))
                ent = [sem, 0, len(self.dall)]
                self.dall.append(ent)
            self.dactive[nm] = ent
        ent = self.dactive[nm]
        tl.dent = ent
        tl.dsem = ent[0]
        tl.dcnt = ent[1]
        tl.dbase = ent[1]
        tl.dkey = ent[2]

    def barrier(self):
        sp = self.E['sp']
        for n, e in self.E.items():
            if n != 'sp' and e.cnt and sp.seen.get(n, 0) < e.cnt:
                sp.e.wait_ge(e.sem, e.cnt)
                sp.seen[n] = e.cnt
        for ent in self.dall:
            sem, cnt, key = ent
            if cnt and self.dwaited.get(key, 0) < cnt:
                sp.e.wait_ge(sem, 16 * cnt)
                self.dwaited[key] = cnt
        sp.e.sem_inc(sp.sem, 1)
        sp.cnt += 1
        for n, e in self.E.items():
            if n != 'sp':
                e.e.wait_ge(sp.sem, sp.cnt)
                e.seen['sp'] = sp.cnt
                for m, o in self.E.items():
                    if m != 'sp':
                        e.seen[m] = max(e.seen.get(m, 0), o.cnt)
        for nm, ent in self.dactive.items():
            self.dfree.append(ent)
        self.dactive = {}

    def sb(self, shape, dt=F32, name=None, stack=None):
        self.ntile += 1
        name = (name or 't') + f'_{self.ntile}'
        t = (stack or self.stack).enter_context(self.nc.sbuf_tensor(name, list(shape), dt))
        tl = Tl(self, t, 'sb')
        tl.name = name
        return tl

    def ps(self, shape, dt=F32, name=None, stack=None):
        self.ntile += 1
        name = (name or 'p') + f'_{self.ntile}'
        t = (stack or self.stack).enter_context(self.nc.psum_tensor(name, list(shape), dt))
        tl = Tl(self, t, 'ps')
        tl.name = name
        return tl

    def dram(self, name, shape, dt=F32, kind='Internal'):
        t = self.nc.dram_tensor(name, list(shape), dt, kind=kind)
        tl = Tl(self, t, 'dr')
        tl.is_output = (kind == 'ExternalOutput')
        return tl

    def _need(self, eng, reads, writes):
        waits = {}
        dwaits = {}

        def add(w):
            if w is None:
                return
            n, c = w
            if n == eng.name and n == 'pe':
                return
            if waits.get(n, 0) < c:
                waits[n] = c

        for v in reads:
            tl = v.tl
            add(tl.w)
            if tl.dw:
                dwaits[tl] = max(dwaits.get(tl, 0), tl.dw)
        for v in writes:
            tl = v.tl
            add(tl.w)
            for n, c in tl.r.items():
                add((n, c))
            if tl.dsem is not None and tl.dcnt > tl.dbase:
                dwaits[tl] = max(dwaits.get(tl, 0), tl.dcnt)
        return waits, dwaits

    def _emit_waits(self, eng, waits, dwaits):
        for n, c in waits.items():
            if eng.seen.get(n, 0) >= c:
                continue
            eng.e.wait_ge(self.E[n].sem, c)
            eng.seen[n] = c
        for tl, c in dwaits.items():
            k = tl.dkey
            if eng.seen_d.get(k, 0) >= c:
                continue
            eng.e.wait_ge(tl.dsem, 16 * c)
            eng.seen_d[k] = c

    def op(self, engname, fn, reads, writes, inc=True):
        eng = self.E[engname]
        reads = [_v(x) for x in reads if x is not None and not isinstance(x, (int, float))]
        writes = [_v(x) for x in writes]
        waits, dwaits = self._need(eng, reads, writes)
        self._emit_waits(eng, waits, dwaits)
        ins = fn()
        c = eng.cnt + 1
        if inc:
            ins.then_inc(eng.sem, 1)
            eng.cnt = c
        for v in writes:
            v.tl.w = (engname, c)
            v.tl.r = {}
        for v in reads:
            if v.tl.r.get(engname, 0) < c:
                v.tl.r[engname] = c
        return ins

    def dma(self, qname, out, in_):
        q = self.E[qname]
        out = _v(out)
        in_ = _v(in_)
        o_tl, i_tl = out.tl, in_.tl
        sb_tl = o_tl if o_tl.kind != 'dr' else i_tl
        assert sb_tl.kind != 'dr'
        if sb_tl.dsem is None or self.dactive.get(sb_tl.name) is not sb_tl.dent:
            self._attach_dsem(sb_tl)
            sb_tl.dw = 0
            sb_tl.dr = 0
        waits = {}
        dwaits = {}
        semwaits = {}

        def add(w):
            if w is None:
                return
            n, c = w
            if waits.get(n, 0) < c:
                waits[n] = c

        if o_tl.kind != 'dr':
            add(o_tl.w)
            for n, c in o_tl.r.items():
                add((n, c))
            if o_tl.dr:
                dwaits[o_tl] = o_tl.dr
        else:
            for k, (s, val) in list(o_tl.dram_r.items()) + list(o_tl.dram_w.items()):
                if semwaits.get(k, (None, 0))[1] < val:
                    semwaits[k] = (s, val)
        if i_tl.kind != 'dr':
            add(i_tl.w)
            if i_tl.dw:
                dwaits[i_tl] = max(dwaits.get(i_tl, 0), i_tl.dw)
        else:
            for k, (s, val) in i_tl.dram_w.items():
                if semwaits.get(k, (None, 0))[1] < val:
                    semwaits[k] = (s, val)
        self._emit_waits(q, waits, dwaits)
        for k, (s, val) in semwaits.items():
            if q.seen_d.get(k, 0) >= val:
                continue
            q.e.wait_ge(s, 16 * val)
            q.seen_d[k] = val
        ins = q.e.dma_start(out=out.ap, in_=in_.ap)
        sb_tl.dcnt += 1
        sb_tl.dent[1] = sb_tl.dcnt
        ins.then_inc(sb_tl.dsem, 16)
        key = sb_tl.dkey
        if o_tl.kind != 'dr':
            o_tl.dw = o_tl.dcnt
            o_tl.w = None
            o_tl.r = {}
            if i_tl.kind == 'dr':
                i_tl.dram_r[key] = (sb_tl.dsem, sb_tl.dcnt)
        else:
            i_tl.dr = i_tl.dcnt
            o_tl.dram_w[key] = (sb_tl.dsem, sb_tl.dcnt)
            o_tl.dram_r = {}
            if o_tl.is_output:
                self.out_dmas.append((sb_tl.dsem, sb_tl.dcnt))
        return ins

    def finish(self):
        self.barrier()
        sp = self.E['sp']
        for n, e in self.E.items():
            if n != 'sp' and e.cnt:
                if sp.seen.get(n, 0) < e.cnt:
                    sp.e.wait_ge(e.sem, e.cnt)
        for s, val in self.out_dmas:
            sp.e.wait_ge(s, 16 * val)
        for n in ('pool',):
            for s, val in self.out_dmas:
                self.E[n].e.wait_ge(s, 16 * val)

    def mm(self, out, lhsT, rhs, start=True, stop=True, inc=True):
        out, lhsT, rhs = _v(out), _v(lhsT), _v(rhs)
        return self.op('pe', lambda: self.nc.tensor.matmul(out.ap, lhsT.ap, rhs.ap, start=start, stop=stop),
                       [lhsT, rhs], [out], inc=inc)

    def tr(self, out, in_, ident):
        out, in_, ident = _v(out), _v(in_), _v(ident)
        return self.op('pe', lambda: self.nc.tensor.transpose(out.ap, in_.ap, ident.ap), [in_, ident], [out])

    def act(self, out, in_, func, bias=None, scale=1.0, accum=None):
        out, in_ = _v(out), _v(in_)
        rd = [in_]
        kw = {}
        if bias is not None:
            if isinstance(bias, (int, float)):
                kw['bias'] = float(bias)
            else:
                bias = _v(bias)
                rd.append(bias)
                kw['bias'] = bias.ap
        if isinstance(scale, (int, float)):
            kw['scale'] = float(scale)
        else:
            scale = _v(scale)
            rd.append(scale)
            kw['scale'] = scale.ap
        wr = [out]
        if accum is not None:
            accum = _v(accum)
            wr.append(accum)
            kw['accum_out'] = accum.ap
        return self.op('act', lambda: self.nc.scalar.activation(out=out.ap, in_=in_.ap, func=func, **kw), rd, wr)

    def _veng(self, e):
        return self.nc.vector if e == 'dve' else self.nc.gpsimd

    def tt(self, out, a, b, op, e='dve'):
        out, a, b = _v(out), _v(a), _v(b)
        return self.op(e, lambda: self._veng(e).tensor_tensor(out=out.ap, in0=a.ap, in1=b.ap, op=op), [a, b], [out])

    def ts(self, out, a, s1, op0, s2=None, op1=None, e='dve', accum=None):
        out, a = _v(out), _v(a)
        rd = [a]

        def cv(s):
            if s is None or isinstance(s, (int, float)):
                return None if s is None else float(s)
            s = _v(s)
            rd.append(s)
            return s.ap
        s1a = cv(s1)
        s2a = cv(s2)
        kw = {}
        wr = [out]
        if op1 is not None:
            kw['op1'] = op1
        if accum is not None:
            accum = _v(accum)
            wr.append(accum)
            kw['accum_out'] = accum.ap
        return self.op(e, lambda: self._veng(e).tensor_scalar(out=out.ap, in0=a.ap, scalar1=s1a, scalar2=s2a, op0=op0, **kw), rd, wr)

    def stt(self, out, a, s, b, op0, op1, accum=None):
        out, a, b = _v(out), _v(a), _v(b)
        rd = [a, b]
        if isinstance(s, (int, float)):
            sa = float(s)
        else:
            s = _v(s)
            rd.append(s)
            sa = s.ap
        kw = {}
        wr = [out]
        if accum is not None:
            accum = _v(accum)
            wr.append(accum)
            kw['accum_out'] = accum.ap
        return self.op('dve', lambda: self.nc.vector.scalar_tensor_tensor(out=out.ap, in0=a.ap, scalar=sa, in1=b.ap, op0=op0, op1=op1, **kw), rd, wr)

    def copy(self, out, in_, e='dve'):
        out, in_ = _v(out), _v(in_)
        if e == 'act':
            return self.act(out, in_, AF.Copy)
        return self.op(e, lambda: self._veng(e).tensor_copy(out=out.ap, in_=in_.ap), [in_], [out])

    def memset(self, out, val, e='dve'):
        out = _v(out)
        return self.op(e, lambda: self._veng(e).memset(out.ap, val), [], [out])

    def reduce(self, out, in_, op, axis=AX.X, e='dve'):
        out, in_ = _v(out), _v(in_)
        return self.op(e, lambda: self._veng(e).tensor_reduce(out=out.ap, in_=in_.ap, axis=axis, op=op), [in_], [out])

    def recip(self, out, in_):
        out, in_ = _v(out), _v(in_)
        return self.op('dve', lambda: self.nc.vector.reciprocal(out=out.ap, in_=in_.ap), [in_], [out])

    def scan(self, out, d0, d1, init, op0, op1):
        out, d0, d1 = _v(out), _v(d0), _v(d1)
        rd = [d0, d1]
        if isinstance(init, (int, float)):
            ia = float(init)
        else:
            init = _v(init)
            rd.append(init)
            ia = init.ap
        return self.op('dve', lambda: self.nc.vector.tensor_tensor_scan(out=out.ap, data0=d0.ap, data1=d1.ap, initial=ia, op0=op0, op1=op1), rd, [out])

from contextlib import ExitStack
import math

T = 4352
NCTX = 256
NLAT = 4096
D = 1024
DFF = 4096
ALPHA = 8.0 ** 0.25
LN_EPS = 1e-5

_A0, _B0, _C0, _D0 = 0, 1280, 2432, 3472


def _r(a, b):
    return list(range(a, b))


FM_BLOCKS = [
    ('A', 'q', _r(_A0 + 0, _A0 + 256)),
    ('A', 'f', _r(_A0 + 512, _A0 + 1024)),
    ('B', 'all', _r(_B0, _B0 + 1152)),
    ('D', 'qk', _r(_D0, _D0 + 512)),
    ('D', 'gates', _r(_D0 + 768, _D0 + 784)),
]
TM_BLOCKS = [
    ('A', 'ig', _r(_A0 + 256, _A0 + 512) + _r(_A0 + 1024, _A0 + 1280)),
    ('B', 'v', _r(_B0 + 512, _B0 + 768)),
    ('C', 'qkv0', _r(_C0, _C0 + 512)),
    ('C', 'qkv1', _r(_C0 + 512, _C0 + 768) + _r(_C0 + 784, _C0 + 1040)),
    ('C', 'ba', _r(_C0 + 768, _C0 + 784)),
    ('D', 'vo', _r(_D0 + 512, _D0 + 768) + _r(_D0 + 784, _D0 + 1040)),
    ('D', 'gates', _r(_D0 + 768, _D0 + 784)),
]
FM_COLS = sum((b[2] for b in FM_BLOCKS), [])
TM_COLS = sum((b[2] for b in TM_BLOCKS), [])
NFM = len(FM_COLS)
NTM = len(TM_COLS)


def fm_layout():
    out = []
    off = 0
    for g, nm, cols in FM_BLOCKS:
        out.append((g, f'fm_{g}_{nm}', off, len(cols)))
        off += len(cols)
    return out


def tm_layout():
    out = []
    off = 0
    for g, nm, cols in TM_BLOCKS:
        out.append((g, f'tm_{g}_{nm}', off, len(cols)))
        off += len(cols)
    return out


def scan_rows(order, s0, n):
    if order == 'r':
        return [(0, n, NCTX + s0, 1)]
    segs = []
    assert s0 % 64 == 0 and n % 64 == 0
    for i in range(n // 64):
        c = s0 // 64 + i
        segs.append((i * 64, 64, NCTX + c, 64))
    return segs


def rows_ap(fwk, dr, row_start, row_step, nrows, c0, c1):
    a = dr.ap()
    ncols_total = a.shape[1]
    ap = bass.AP(a.tensor, a.offset + row_start * ncols_total + c0, [[row_step * ncols_total, nrows], [1, c1 - c0]])
    return V(dr, ap)


def bcast_last(v, n):
    a = v.ap
    return V(v.tl, bass.AP(a.tensor, a.offset, [list(x) for x in a.ap] + [[0, n]]))


def dram_bcast_rows(dr, off, n):
    a = dr.ap()
    return V(dr, bass.AP(a.tensor, a.offset + off, [[0, 128], [1, n]]))


def layer_norm(fw, t, out, gb, bb, sc):
    fw.act(out, t, AF.Identity, accum=sc['s1'])
    fw.act(out, t, AF.Square, accum=sc['s2'])
    fw.ts(sc['mean'], sc['s1'], 1.0 / 1024, ALU.mult)
    fw.tt(sc['m2'], sc['mean'], sc['mean'], ALU.mult)
    fw.stt(sc['var'], sc['s2'], 1.0 / 1024, sc['m2'], ALU.mult, ALU.subtract)
    fw.ts(sc['var'], sc['var'], LN_EPS, ALU.add)
    fw.act(sc['std'], sc['var'], AF.Sqrt)
    fw.recip(sc['rstd'], sc['std'])
    fw.stt(sc['nmr'], sc['mean'], -1.0, sc['rstd'], ALU.mult, ALU.mult)
    fw.act(out, t, AF.Identity, bias=sc['nmr'], scale=sc['rstd'])
    fw.tt(out, out, gb, ALU.mult, e='pool')
    fw.tt(out, out, bb, ALU.add)


def ln_scratch(fw, st, pfx):
    sc = {k: fw.sb([128, 1], name=f'{pfx}_{k}', stack=st) for k in ('s1', 's2', 'mean', 'm2', 'var', 'std', 'rstd', 'nmr')}
    return sc


def load_cast_weight(fw, st, wbf3, wdram, K, ncols, pfx, chunk=256):
    with ExitStack() as st2:
        stg = [fw.sb([128, K * chunk], name=f'{pfx}_stg{i}', stack=st2) for i in range(2)]
        engs = ['pool', 'dve', 'act']
        i = 0
        for c0 in range(0, ncols, chunk):
            w = min(chunk, ncols - c0)
            s = stg[i % 2]
            s3 = s[:, 0:K * w].re('p (k c) -> p k c', k=K)
            fw.dma('sp', s3, wdram[:, c0:c0 + w].re('(k p) c -> p k c', p=128))
            fw.copy(wbf3[:, :, c0:c0 + w], s3, e=engs[i % 3])
            i += 1
        fw.barrier()


def phase0(fw, L, lst):
    nc = fw.nc
    psb = fw.psb
    modc = fw.sb([128, 64], name='modc', stack=lst)
    modc4 = modc[:, :].re('p (i f w) -> p i f w', i=4, f=8)
    gbc = [[fw.sb([128, 1024], name=f'gbc{w}{g}', stack=lst) for g in range(2)] for w in range(2)]
    with ExitStack() as ps:
        c2 = fw.sb([128, 16], name='c2', stack=ps)
        sc = fw.sb([128, 16], name='sc', stack=ps)
        sc3 = sc[:, :].re('p (k w) -> p k w', w=2)
        scb = fw.sb([128, 16 * 128], name='scb', stack=ps)
        scb4 = scb[:, :].re('p (k w m) -> p k w m', k=8, w=2)
        adab = fw.sb([128, 48], name='adab', stack=ps)
        adabb = fw.sb([128, 2048], name='adabb', stack=ps)
        wst = [fw.sb([128, 8192], name=f'adaw{i}', stack=ps) for i in range(2)]
        fw.dma('sp', c2, L['c2T'])
        fw.dma('sp', adab, L['adab_col'])
        fw.dma('sp', adabb[:, 0:1024], dram_bcast_rows(L['ada_b'], 2048, 1024))
        fw.dma('sp', adabb[:, 1024:2048], dram_bcast_rows(L['ada_b'], 5120, 1024))
        fw.act(sc, c2, AF.Silu)
        for k in range(8):
            for w in range(2):
                fw.copy(scb4[:, k, w, :], sc3[:, k, w:w + 1].bc([128, 128]))
        for blk in range(6):
            ws = wst[blk % 2]
            ws3 = ws[:, :].re('p (k c) -> p k c', k=8)
            fw.dma('sp', ws3, L['ada_w'][:, blk * 1024:(blk + 1) * 1024].re('(k p) c -> p k c', p=128))
            if blk in (0, 1, 3, 4):
                idx = {0: 0, 1: 1, 3: 2, 4: 3}[blk]
                p = psb[0]
                for fc in range(8):
                    for k in range(8):
                        fw.mm(p[:, fc * 2:fc * 2 + 2], ws3[:, k, fc * 128:(fc + 1) * 128], sc3[:, k, :],
                              start=(k == 0), stop=(k == 7), inc=(k == 7))
                pv = p[:, 0:16].re('p (f w) -> p f w', w=2)
                fw.tt(modc4[:, idx], pv, bcast_last(adab[:, blk * 8:(blk + 1) * 8], 2), ALU.add)
                if blk in (1, 4):
                    fw.ts(modc4[:, idx], modc4[:, idx], 1.0, ALU.add)
            else:
                g = 0 if blk == 2 else 1
                for w in range(2):
                    for half in range(2):
                        p = psb[1 + (w * 2 + half) % 2]
                        for k in range(8):
                            fw.mm(p[:, :], scb4[:, k, w, :], ws3[:, k, half * 512:(half + 1) * 512],
                                  start=(k == 0), stop=(k == 7), inc=(k == 7))
                        fw.tt(gbc[w][g][:, half * 512:(half + 1) * 512], p[:, :],
                              adabb[:, g * 1024 + half * 512: g * 1024 + (half + 1) * 512], ALU.add)
        fw.barrier()
    return {'modc4': modc4, 'gbc': gbc}


def transpose_mod(fw, xt3, nsub, hT3, modc4, idx, w, ident, cnt0=0):
    psb = fw.psb
    n = nsub * 128
    for k in range(8):
        p = psb[k % 2]
        for j in range(nsub):
            fw.tr(p[:, j * 128:(j + 1) * 128], xt3[:, j, k * 128:(k + 1) * 128], ident)
        if idx is None:
            fw.copy(hT3[:, k, 0:n], p[:, 0:n], e=('act' if k % 2 else 'dve'))
        elif k % 2 == 0:
            fw.ts(hT3[:, k, 0:n], p[:, 0:n], modc4[:, idx + 1, k, w:w + 1], ALU.mult,
                  s2=modc4[:, idx, k, w:w + 1], op1=ALU.add)
        else:
            fw.act(hT3[:, k, 0:n], p[:, 0:n], AF.Identity, bias=modc4[:, idx, k, w:w + 1],
                   scale=modc4[:, idx + 1, k, w:w + 1])


def phase1(fw, L, P0, S, cm_groups, x_in):
    nc = fw.nc
    psb = fw.psb
    with ExitStack() as st:
        wfm = fw.sb([128, 8 * NFM], BF16, name='wfm', stack=st)
        wfm3 = wfm[:, :].re('p (k c) -> p k c', k=8)
        wtm = fw.sb([128, 8 * NTM], BF16, name='wtm', stack=st)
        wtm3 = wtm[:, :].re('p (k c) -> p k c', k=8)
        load_cast_weight(fw, st, wfm3, L['w_fm'], 8, NFM, 'wfm')
        load_cast_weight(fw, st, wtm3, L['w_tm'], 8, NTM, 'wtm')
        xb = [fw.sb([128, 4096], name=f'p1x{i}', stack=st) for i in range(2)]
        hb = [fw.sb([128, 8 * 512], BF16, name=f'p1h{i}', stack=st) for i in range(2)]
        stg = [fw.sb([128, 512], name=f'p1s{i}', stack=st) for i in range(4)]
        ident = fw.ident
        tiles = [('r', 0, 0, 256, set('ABCD'))]
        rg = set('ABCD') - set(cm_groups)
        for m in range(8):
            if rg:
                tiles.append(('r', 1, m * 512, 512, rg))
        for m in range(8):
            if cm_groups:
                tiles.append(('c', 1, m * 512, 512, set(cm_groups)))
        ti = 0
        rot = 0
        srot = 0
        for order, w, s0, n, groups in tiles:
            nsub = n // 128
            xt = xb[ti % 2]
            xt3 = xt[:, :].re('p (j c) -> p j c', c=1024)
            hT3 = hb[ti % 2][:, :].re('p (k t) -> p k t', k=8)
            ti += 1
            for j in range(nsub):
                if w == 0:
                    segs = [(0, 128, j * 128, 1)]
                else:
                    segs = scan_rows(order, s0 + j * 128, 128)
                for poff, nr, r0, rstep in segs:
                    fw.dma('sp', xt3[poff:poff + nr, j, :], rows_ap(fw, x_in, r0, rstep, nr, 0, 1024))
            transpose_mod(fw, xt3, nsub, hT3, P0['modc4'], 0, w, ident)
            tok0 = s0 if w == 0 else NCTX + s0
            for g, nm, off, ncols in fm_layout():
                if g not in groups:
                    continue
                for c0 in range(0, ncols, 128):
                    cw = min(128, ncols - c0)
                    p = psb[2 + rot % 6]
                    rot += 1
                    for k in range(8):
                        fw.mm(p[0:cw, 0:n], wfm3[:, k, off + c0: off + c0 + cw], hT3[:, k, 0:n],
                              start=(k == 0), stop=(k == 7), inc=(k == 7))
                    sg = stg[srot % 4]
                    srot += 1
                    fw.copy(sg[0:cw, 0:n], p[0:cw, 0:n], e=('act' if srot % 2 else 'dve'))
                    fw.dma('pool', S[nm][c0:c0 + cw, tok0:tok0 + n], sg[0:cw, 0:n])
            for j in range(nsub):
                for g, nm, off, ncols in tm_layout():
                    if g not in groups:
                        continue
                    p = psb[2 + rot % 6]
                    rot += 1
                    for k in range(8):
                        fw.mm(p[:, 0:ncols], hT3[:, k, j * 128:(j + 1) * 128], wtm3[:, k, off:off + ncols],
                              start=(k == 0), stop=(k == 7), inc=(k == 7))
                    sg = stg[srot % 4]
                    srot += 1
                    fw.copy(sg[:, 0:ncols], p[:, 0:ncols], e=('act' if srot % 2 else 'dve'))
                    fw.dma('pool', S[nm][tok0 + j * 128: tok0 + (j + 1) * 128, 0:ncols], sg[:, 0:ncols])
        fw.barrier()


def phase3(fw, L, P0, Y, x_in, X1, x_out, last, out_lat=None):
    psb = fw.psb
    ident = fw.ident
    modc4, gbc = P0['modc4'], P0['gbc']
    t0 = 2 if last else 0
    with ExitStack() as st:
        wo = fw.sb([128, 8 * 1024], BF16, name='wo', stack=st)
        wo3 = wo[:, :].re('p (k c) -> p k c', k=8)
        load_cast_weight(fw, st, wo3, L['w_out'], 8, 1024, 'wo')
        yb = [fw.sb([128, 1024], name=f'p3y{i}', stack=st) for i in range(2)]
        xb = [fw.sb([128, 1024], name=f'p3x{i}', stack=st) for i in range(2)]
        yTb = [fw.sb([128, 8 * 128], BF16, name=f'p3yT{i}', stack=st) for i in range(2)]
        tb = [fw.sb([128, 1024], name=f'p3t{i}', stack=st) for i in range(2)]
        ob = [fw.sb([128, 1024], name=f'p3o{i}', stack=st) for i in range(2)]
        lnbc = fw.sb([128, 2048], name='lnbc1', stack=st)
        fw.dma('sp', lnbc, dram_bcast_rows(L['ln'], 0, 2048))
        sc = ln_scratch(fw, st, 'ln1')
        for ti in range(t0, T // 128):
            w = 0 if ti < 2 else 1
            yt, xt, yT, tt_, ot = yb[ti % 2], xb[ti % 2], yTb[ti % 2], tb[ti % 2], ob[ti % 2]
            fw.dma('sp', yt, Y[ti * 128:(ti + 1) * 128, :])
            fw.dma('sp', xt, x_in[ti * 128:(ti + 1) * 128, :])
            yt3 = yt[:, :].re('p (j c) -> p j c', j=1)
            yT3 = yT[:, :].re('p (k t) -> p k t', k=8)
            for hh in range(2):
                p = psb[hh]
                for q in range(4):
                    k = hh * 4 + q
                    fw.tr(p[:, q * 128:(q + 1) * 128], yt[:, k * 128:(k + 1) * 128], ident)
                fw.copy(yT3[:, hh * 4:(hh + 1) * 4, :], p[:, :].re('p (q t) -> p q t', q=4), e=('act' if hh else 'dve'))
            for half in range(2):
                p = psb[2 + half]
                for k in range(8):
                    fw.mm(p[:, :], yT3[:, k, :], wo3[:, k, half * 512:(half + 1) * 512],
                          start=(k == 0), stop=(k == 7), inc=(k == 7))
                fw.tt(tt_[:, half * 512:(half + 1) * 512], p[:, :], gbc[w][0][:, half * 512:(half + 1) * 512], ALU.mult)
                fw.stt(tt_[:, half * 512:(half + 1) * 512], xt[:, half * 512:(half + 1) * 512], ALPHA,
                       tt_[:, half * 512:(half + 1) * 512], ALU.mult, ALU.add)
            layer_norm(fw, tt_, ot, lnbc[:, 0:1024], lnbc[:, 1024:2048], sc)
            fw.dma('pool', X1[ti * 128:(ti + 1) * 128, :], ot)
        fw.barrier()
    with ExitStack() as st:
        w1 = fw.sb([128, 8 * 4096], BF16, name='w1', stack=st)
        w13 = w1[:, :].re('p (k c) -> p k c', k=8)
        w2 = fw.sb([128, 32 * 1024], BF16, name='w2', stack=st)
        w23 = w2[:, :].re('p (k c) -> p k c', k=32)
        load_cast_weight(fw, st, w13, L['mlp_w1'], 8, 4096, 'w1')
        load_cast_weight(fw, st, w23, L['mlp_w2'], 32, 1024, 'w2', chunk=64)
        xb = [fw.sb([128, 2048], name=f'p4x{i}', stack=st) for i in range(2)]
        hT = fw.sb([128, 8 * 256], BF16, name='p4hT', stack=st)
        hT3 = hT[:, :].re('p (k t) -> p k t', k=8)
        h1 = fw.sb([128, 32 * 256], BF16, name='p4h1', stack=st)
        h13 = h1[:, :].re('p (k t) -> p k t', k=32)
        sq = [fw.sb([128, 256], name=f'p4sq{i}', stack=st) for i in range(2)]
        tb = [fw.sb([128, 1024], name=f'p4t{i}', stack=st) for i in range(1)]
        ob = [fw.sb([128, 1024], name=f'p4o{i}', stack=st) for i in range(2)]
        lnbc = fw.sb([128, 2048], name='lnbc2', stack=st)
        fw.dma('sp', lnbc, dram_bcast_rows(L['ln'], 2048, 2048))
        sc = ln_scratch(fw, st, 'ln2')
        oi = 0
        for mi in range(t0 // 2, T // 256):
            w = 0 if mi < 1 else 1
            xt = xb[mi % 2]
            xt3 = xt[:, :].re('p (j c) -> p j c', j=2)
            fw.dma('sp', xt3, X1[mi * 256:(mi + 1) * 256, :].re('(j p) c -> p j c', p=128))
            transpose_mod(fw, xt3, 2, hT3, modc4, 2, w, ident)
            for fc in range(32):
                p = psb[2 + fc % 3]
                for k in range(8):
                    fw.mm(p[:, 0:256], w13[:, k, fc * 128:(fc + 1) * 128], hT3[:, k, :],
                          start=(k == 0), stop=(k == 7), inc=(k == 7))
                s = sq[fc % 2]
                fw.act(s, p[:, 0:256], AF.Square)
                fw.stt(h13[:, fc, :], p[:, 0:256], 0.0, s, ALU.is_gt, ALU.mult)
            for j in range(2):
                tt_, ot = tb[0], ob[oi % 2]
                oi += 1
                for half in range(2):
                    p = psb[5 + half]
                    for kk in range(32):
                        fw.mm(p[:, :], h13[:, kk, j * 128:(j + 1) * 128], w23[:, kk, half * 512:(half + 1) * 512],
                              start=(kk == 0), stop=(kk == 31), inc=(kk == 31))
                    fw.tt(tt_[:, half * 512:(half + 1) * 512], p[:, :], gbc[w][1][:, half * 512:(half + 1) * 512], ALU.mult)
                    fw.stt(tt_[:, half * 512:(half + 1) * 512], xt3[:, j, half * 512:(half + 1) * 512], ALPHA,
                           tt_[:, half * 512:(half + 1) * 512], ALU.mult, ALU.add)
                layer_norm(fw, tt_, ot, lnbc[:, 0:1024], lnbc[:, 1024:2048], sc)
                r0 = mi * 256 + j * 128
                if last:
                    fw.dma('pool', out_lat[r0 - NCTX: r0 - NCTX + 128, :], ot)
                else:
                    fw.dma('pool', x_out[r0:r0 + 128, :], ot)
        fw.barrier()


NORM_EPS = 1e-6
ILG = {}
GS = 516


def fv(tl, off, dims, p0=0, np_=128):
    a = tl.t[p0:p0 + np_, :]
    return V(tl, bass.AP(a.tensor, a.offset + off, [list(a.ap[0])] + [list(d) for d in dims]))


def build_consts():
    c = {}
    off = 0
    arrs = []

    def add(name, arr):
        nonlocal off
        a = np.zeros((128, arr.shape[1]), np.float32)
        a[:arr.shape[0]] = arr
        c[name] = (off, arr.shape[1])
        arrs.append(a)
        off += arr.shape[1]
    for C in (32, 64):
        j = np.arange(C)[:, None]
        i = np.arange(C)[None, :]
        add(f'mi{C}_0', (j <= i).astype(np.float32))
        add(f'mi{C}_1', (j >= i).astype(np.float32))
        add(f'ms{C}_0', (j < i).astype(np.float32))
        add(f'ms{C}_1', (j > i).astype(np.float32))
    sel = np.zeros((16, 2, 2, 128), np.float32)
    for d in range(2):
        for ct in range(2):
            for m in range(128):
                sel[8 + 4 * d + 2 * ct + m // 64, d, ct, m] = 1.0
    add('sel_lf', sel.reshape(16, 512))
    p = np.arange(128)
    add('blk64', (p[:, None] // 64 == p[None, :] // 64).astype(np.float32))
    add('hsel', (p[:, None] // 64 == np.arange(2)[None, :]).astype(np.float32))
    add('ones', np.ones((128, 128), np.float32))
    return c, np.concatenate(arrs, 1)


CST_MAP, CST_ARR = build_consts()
NCST = CST_ARR.shape[1]


def cv(fw, name, np_=128, c0=0, c1=None):
    off, n = CST_MAP[name]
    c1 = n if c1 is None else c1
    return fw.cst[0:np_, off + c0: off + c1]


def blocks_for(d):
    lat = [(NCTX + m * 512, 512) for m in range(8)]
    if d == 1:
        lat = lat[::-1]
    return [(0, 256)] + lat


def gla_core(fw, name, QT, KTd, LFd, Vsrc, Od, dvx, C, kscale):
    psb = fw.psb
    ident = fw.ident
    W = 4 * dvx
    with ExitStack() as st:
        qb = [fw.sb([64, 2048], name=f'g_q{i}', stack=st) for i in range(1)]
        kb = [fw.sb([64, 2048], name=f'g_k{i}', stack=st) for i in range(1)]
        lb = [fw.sb([64, 2048], name=f'g_l{i}', stack=st) for i in range(1)]
        vb = [fw.sb([C, 16 * W], name=f'g_v{i}', stack=st) for i in range(2)]
        ob = [fw.sb([C, 16 * W], name=f'g_o{i}', stack=st) for i in range(2)]
        Gz = fw.sb([64, 4 * GS], name='g_Gz', stack=st)
        cum = fw.sb([64, 2048], name='g_cum', stack=st)
        d3 = fw.sb([64, 2048], name='g_d3', stack=st)
        Ep = fw.sb([64, 2048], name='g_Ep', stack=st)
        Em = fw.sb([64, 2048], name='g_Em', stack=st)
        E3 = fw.sb([64, 2048], name='g_E3', stack=st)
        Qt = fw.sb([64, 2048], name='g_Qt', stack=st)
        Kbar = fw.sb([64, 2048], name='g_Kbar', stack=st)
        Khat = fw.sb([64, 2048], name='g_Khat', stack=st)
        glast = fw.sb([64, 64], name='g_glast', stack=st)
        Am = [fw.sb([C, 4 * C], name=f'g_Am{i}', stack=st) for i in range(2)]
        Kt = [fw.sb([C, 256], name=f'g_Kt{i}', stack=st) for i in range(2)]
        S = [fw.sb([64, dvx], name=f'g_S{i}', stack=st) for i in range(4)]
        ones = fw.sb([64, 512], name='g_ones', stack=st)
        fw.memset(ones, 1.0)
        fw.memset(Gz, 0.0)
        bi = 0
        ci = 0
        for d in range(2):
            for h in range(4):
                fw.memset(S[h], 0.0)
            moff = CST_MAP[f'mi{C}_{d}'][0]
            for (t0, n) in blocks_for(d):
                nn = n // C
                q, k, l, v, o = qb[0], kb[0], lb[0], vb[bi % 2], ob[bi % 2]
                bi += 1

                def v3(tl):
                    return fv(tl, 0, [[512, 4], [1, n]], 0, 64)
                fw.dma('sp', v3(q), QT[:, t0:t0 + n].re('(h p) t -> p h t', p=64))
                fw.dma('sp', v3(k), KTd[d][:, t0:t0 + n].re('(h p) t -> p h t', p=64))
                fw.dma('sp', v3(l), LFd[d][:, t0:t0 + n].re('(h p) t -> p h t', p=64))
                vd, vc0, vw = Vsrc[d]
                fw.dma('sp', fv(v, 0, [[W, nn], [1, W]], 0, C),
                       vd[t0:t0 + n, vc0:vc0 + W].re('(n c) d -> c n d', c=C))
                for h in range(4):
                    fw.scan(Gz[:, h * GS + 1: h * GS + 1 + n], ones[:, 0:n], l[:, h * 512: h * 512 + n],
                            0.0, ALU.mult, ALU.add)
                lastc = C - 1 if d == 0 else 0
                for h in range(4):
                    outv = fv(cum, h * 512, [[C, nn], [1, C]], 0, 64)
                    if d == 0:
                        fw.tt(outv, fv(Gz, h * GS + 1, [[C, nn], [1, C]], 0, 64),
                              fv(Gz, h * GS, [[C, nn], [0, C]], 0, 64), ALU.subtract)
                    else:
                        fw.tt(outv, fv(Gz, h * GS + C, [[C, nn], [0, C]], 0, 64),
                              fv(Gz, h * GS, [[C, nn], [1, C]], 0, 64), ALU.subtract)
                    fw.tt(fv(d3, h * 512, [[C, nn], [1, C]], 0, 64), outv,
                          fv(cum, h * 512 + lastc, [[C, nn], [0, C]], 0, 64), ALU.subtract, e='pool')
                fw.act(v3(Ep), v3(cum), AF.Exp)
                fw.act(v3(Em), v3(cum), AF.Exp, scale=-1.0)
                fw.act(v3(E3), v3(d3), AF.Exp, scale=-1.0)
                fw.tt(v3(Qt), v3(q), v3(Ep), ALU.mult)
                fw.stt(v3(Kbar), v3(k), kscale, v3(Em), ALU.mult, ALU.mult)
                fw.stt(v3(Khat), v3(k), kscale, v3(E3), ALU.mult, ALU.mult)
                fw.copy(fv(glast, 0, [[16, 4], [1, nn]], 0, 64), fv(Ep, lastc, [[512, 4], [C, nn]], 0, 64))
                order = range(nn) if d == 0 else range(nn - 1, -1, -1)
                for n_ in order:
                    a = n_ * C
                    psA, psT, psO, psS = psb[ci % 2], psb[2 + ci % 2], psb[4 + ci % 2], psb[6 + ci % 2]
                    am, kt = Am[ci % 2], Kt[ci % 2]
                    ci += 1
                    for h in range(4):
                        cs = slice(h * 512 + a, h * 512 + a + C)
                        fw.mm(psA[0:C, h * C:(h + 1) * C], Kbar[0:64, cs], Qt[0:64, cs], inc=(h == 3))
                    fw.tt(fv(am, 0, [[C, 4], [1, C]], 0, C), fv(psA, 0, [[C, 4], [1, C]], 0, C),
                          fv(fw.cst, moff, [[0, 4], [1, C]], 0, C), ALU.mult)
                    for h in range(4):
                        fw.tr(psT[0:C, h * 64:(h + 1) * 64], Khat[0:64, h * 512 + a: h * 512 + a + C], ident[0:64, 0:64])
                    fw.copy(kt[0:C, :], psT[0:C, 0:256], e='act')
                    for h in range(4):
                        cs = slice(h * 512 + a, h * 512 + a + C)
                        fw.mm(psO[0:C, h * dvx:(h + 1) * dvx], Qt[0:64, cs], S[h][0:64, :],
                              start=True, stop=False, inc=False)
                        fw.mm(psO[0:C, h * dvx:(h + 1) * dvx], am[0:C, h * C:(h + 1) * C],
                              v[0:C, n_ * W + h * dvx: n_ * W + (h + 1) * dvx], start=False, stop=True)
                    fw.copy(o[0:C, n_ * W:(n_ + 1) * W], psO[0:C, 0:W], e=('act' if ci % 2 else 'dve'))
                    for h in range(4):
                        fw.mm(psS[0:64, h * dvx:(h + 1) * dvx], kt[0:C, h * 64:(h + 1) * 64],
                              v[0:C, n_ * W + h * dvx: n_ * W + (h + 1) * dvx])
                        fw.stt(S[h][0:64, :], S[h][0:64, :], glast[0:64, h * 16 + n_: h * 16 + n_ + 1],
                               psS[0:64, h * dvx:(h + 1) * dvx], ALU.mult, ALU.add)
                fw.dma('pool', Od[d][t0:t0 + n, :].re('(n c) d -> c n d', c=C), fv(o, 0, [[W, nn], [1, W]], 0, C))
        fw.barrier()


def rms_head_norm(fw, h, out, gbc, sc):
    fw.tt(sc['sq'], h, h, ALU.mult)
    fw.reduce(sc['ss'], fv(sc['sq'], 0, [[64, 4], [1, 64]]), ALU.add)
    fw.ts(sc['ss'], sc['ss'], 1.0 / 64, ALU.mult, s2=NORM_EPS, op1=ALU.add)
    fw.act(sc['r'], sc['ss'], AF.Sqrt)
    fw.recip(sc['r'], sc['r'])
    fw.tt(fv(out, 0, [[64, 4], [1, 64]]), fv(h, 0, [[64, 4], [1, 64]]), fv(sc['r'], 0, [[1, 4], [0, 64]]), ALU.mult)
    fw.tt(out, out, gbc, ALU.mult, e='pool')


def store_rows_scan(fw, Y, c0, cw, tile_v, ti, colmajor):
    if ti < 2 or not colmajor:
        fw.dma('pool', Y[ti * 128:(ti + 1) * 128, c0:c0 + cw], tile_v)
    else:
        s0 = ti * 128 - NCTX
        for poff, nr, r0, rstep in scan_rows('c', s0, 128):
            fw.dma('pool', rows_ap(fw, Y, r0, rstep, nr, c0, c0 + cw), tile_v[poff:poff + nr, :])


def hgrn2(fw, L, S, Y, layer, colmajor, stop=9):
    psb = fw.psb
    KT = [S['A_KT0'], S['A_KT1']]
    LF = [S['A_LF0'], S['A_LF1']]
    with ExitStack() as st:
        gam = fw.sb([128, 16], name='h_gam', stack=st)
        eg = fw.sb([128, 16], name='h_eg', stack=st)
        tot = fw.sb([128, 4], name='h_tot', stack=st)
        part = fw.sb([128, 4], name='h_part', stack=st)
        lbv = fw.sb([128, 4], name='h_lb', stack=st)
        oml = fw.sb([128, 4], name='h_oml', stack=st)
        fw.dma('sp', gam, L['hgrn_gamma_t'])
        fw.act(eg, gam, AF.Exp)
        fw.reduce(tot, fv(eg, 0, [[4, 4], [1, 4]]), ALU.add)
        if layer >= 1:
            fw.reduce(part, fv(eg, 1, [[4, 4], [1, layer]]), ALU.add)
        else:
            fw.memset(part, 0.0)
        fw.recip(tot, tot)
        fw.tt(lbv, part, tot, ALU.mult)
        fw.ts(oml, lbv, -1.0, ALU.mult, s2=1.0, op1=ALU.add)
        HB = T // 2
        fr = [fw.sb([128, HB], name=f'h_fr{i}', stack=st) for i in range(2)]
        e1 = [fw.sb([128, HB], name=f'h_e{i}', stack=st) for i in range(2)]
        lf = [fw.sb([128, HB], name=f'h_lf{i}', stack=st) for i in range(2)]
        kk = [fw.sb([128, HB], name=f'h_k{i}', stack=st) for i in range(2)]
        it = 0
        for d in range(2):
            for ct in range(2):
                col = d * 2 + ct
                for hb in range(2):
                    f_, e_, l_, k_ = fr[it % 2], e1[it % 2], lf[it % 2], kk[it % 2]
                    it += 1
                    ts_ = slice(hb * HB, (hb + 1) * HB)
                    rows = slice(d * 256 + ct * 128, d * 256 + (ct + 1) * 128)
                    fw.dma('sp', f_, S['fm_A_f'][rows, ts_])
                    fw.act(e_, f_, AF.Exp, scale=-1.0)
                    fw.ts(e_, e_, 1.0, ALU.add)
                    fw.recip(e_, e_)
                    fw.ts(e_, e_, oml[:, col:col + 1], ALU.mult, s2=lbv[:, col:col + 1], op1=ALU.add)
                    fw.act(l_, e_, AF.Ln)
                    fw.ts(k_, e_, -1.0, ALU.mult, s2=1.0, op1=ALU.add, e='pool')
                    fw.dma('pool', LF[d][ct * 128:(ct + 1) * 128, ts_], l_)
                    fw.dma('pool', KT[d][ct * 128:(ct + 1) * 128, ts_], k_)
        fw.barrier()
    if stop <= 1:
        return
    Od = [S['A_O0'], S['A_O1']]
    Vsrc = [(S['tm_A_ig'], 0, 512), (S['tm_A_ig'], 0, 512)]
    ILG.get('gla', gla_core)(fw, 'hgrn', S['fm_A_q'], KT, LF, Vsrc, Od, 64, 32, 1.0)
    if stop <= 2:
        return
    with ExitStack() as st:
        gbc = fw.sb([128, 256], name='h_gbc', stack=st)
        fw.dma('sp', gbc, dram_bcast_rows(L['hgrn_norm_g'], 0, 256))
        o0 = [fw.sb([128, 256], name=f'h_o0{i}', stack=st) for i in range(2)]
        o1 = [fw.sb([128, 256], name=f'h_o1{i}', stack=st) for i in range(2)]
        gg = [fw.sb([128, 256], name=f'h_g{i}', stack=st) for i in range(2)]
        yy = [fw.sb([128, 256], name=f'h_y{i}', stack=st) for i in range(2)]
        sc = {'sq': fw.sb([128, 256], name='h_sq', stack=st), 'ss': fw.sb([128, 4], name='h_ss', stack=st),
              'r': fw.sb([128, 4], name='h_r', stack=st)}
        for ti in range(T // 128):
            a0, a1, g_, y_ = o0[ti % 2], o1[ti % 2], gg[ti % 2], yy[ti % 2]
            rs = slice(ti * 128, (ti + 1) * 128)
            fw.dma('sp', a0, Od[0][rs, :])
            fw.dma('sp', a1, Od[1][rs, :])
            fw.dma('sp', g_, S['tm_A_ig'][rs, 256:512])
            fw.tt(a0, a0, a1, ALU.add)
            fw.act(g_, g_, AF.Silu)
            rms_head_norm(fw, a0, y_, gbc, sc)
            fw.tt(y_, y_, g_, ALU.mult)
            store_rows_scan(fw, Y, 0, 256, y_[:, :], ti, colmajor)
        fw.barrier()


def mlstm(fw, L, S, Y, layer, colmajor):
    psb = fw.psb
    LF = [S['D_LF0'], S['D_LF1']]
    VX = [S['D_VX0'], S['D_VX1']]
    with ExitStack() as st:
        gt = fw.sb([16, T], name='m_gt', stack=st)
        ee = fw.sb([16, T], name='m_e', stack=st)
        fb = fw.sb([16, 1], name='m_fb', stack=st)
        fw.dma('sp', gt, S['fm_D_gates'])
        fw.dma('sp', fb, L['mlstm_fb_col'])
        fw.ts(fb, fb, -1.0, ALU.mult)
        fw.act(ee, gt, AF.Exp, bias=fb, scale=-1.0)
        fw.ts(ee, ee, 1.0, ALU.add)
        fw.act(ee, ee, AF.Ln)
        fw.ts(ee, ee, -1.0, ALU.mult)
        stg = [fw.sb([128, 512], name=f'm_stg{i}', stack=st) for i in range(2)]
        it = 0
        soff = CST_MAP['sel_lf'][0]
        for d in range(2):
            for ct in range(2):
                for t0 in range(0, T, 512):
                    n = min(512, T - t0)
                    p = psb[it % 4]
                    sg = stg[it % 2]
                    it += 1
                    fw.mm(p[:, 0:n], fw.cst[0:16, soff + (d * 2 + ct) * 128: soff + (d * 2 + ct + 1) * 128], ee[0:16, t0:t0 + n])
                    fw.copy(sg[:, 0:n], p[:, 0:n], e=('act' if it % 2 else 'dve'))
                    fw.dma('pool', LF[d][ct * 128:(ct + 1) * 128, t0:t0 + n], sg[:, 0:n])
        ibc = fw.sb([128, 8], name='m_ibc', stack=st)
        fw.dma('sp', ibc, dram_bcast_rows(L['mlstm_i_bias'], 0, 8))
        vo = [fw.sb([128, 256], name=f'm_vo{i}', stack=st) for i in range(2)]
        gts = [fw.sb([128, 16], name=f'm_gts{i}', stack=st) for i in range(2)]
        sv = [fw.sb([128, 8], name=f'm_sv{i}', stack=st) for i in range(2)]
        vx = [[fw.sb([128, 260], name=f'm_vx{d}{i}', stack=st) for i in range(2)] for d in range(2)]
        for ti in range(T // 128):
            rs = slice(ti * 128, (ti + 1) * 128)
            v_, g_, s_ = vo[ti % 2], gts[ti % 2], sv[ti % 2]
            fw.dma('sp', v_, S['tm_D_vo'][rs, 0:256])
            fw.dma('sp', g_, S['tm_D_gates'][rs, :])
            fw.tt(s_, g_[:, 0:8], ibc, ALU.add)
            fw.act(s_, s_, AF.Exp)
            for d in range(2):
                x_ = vx[d][ti % 2]
                fw.tt(fv(x_, 0, [[65, 4], [1, 64]]), fv(v_, 0, [[64, 4], [1, 64]]), fv(s_, d * 4, [[1, 4], [0, 64]]),
                      ALU.mult, e=('pool' if d else 'dve'))
                fw.copy(fv(x_, 64, [[65, 4], [1, 1]]), fv(s_, d * 4, [[1, 4], [1, 1]]))
                fw.dma('pool', VX[d][rs, :], x_)
        fw.barrier()
    Od = [S['D_O0'], S['D_O1']]
    Vsrc = [(VX[0], 0, 260), (VX[1], 0, 260)]
    KTv = S['fm_D_qk'][256:512, :]
    ILG.get('gla', gla_core)(fw, 'mlstm', S['fm_D_qk'][0:256, :], [KTv, KTv], LF, Vsrc, Od, 65, 64, 0.125)
    with ExitStack() as st:
        gbc = fw.sb([128, 256], name='m_gbc', stack=st)
        fw.dma('sp', gbc, dram_bcast_rows(L['mlstm_norm_g'], 0, 256))
        oo = [[fw.sb([128, 260], name=f'm_o{d}{i}', stack=st) for i in range(2)] for d in range(2)]
        og = [fw.sb([128, 256], name=f'm_og{i}', stack=st) for i in range(2)]
        hh = [fw.sb([128, 256], name=f'm_h{i}', stack=st) for i in range(2)]
        h2 = fw.sb([128, 256], name='m_h2', stack=st)
        yy = [fw.sb([128, 256], name=f'm_y{i}', stack=st) for i in range(2)]
        den = [fw.sb([128, 4], name=f'm_den{d}', stack=st) for d in range(2)]
        sc = {'sq': fw.sb([128, 256], name='m_sq', stack=st), 'ss': fw.sb([128, 4], name='m_ss', stack=st),
              'r': fw.sb([128, 4], name='m_r', stack=st)}
        for ti in range(T // 128):
            rs = slice(ti * 128, (ti + 1) * 128)
            g_, h_, y_ = og[ti % 2], hh[ti % 2], yy[ti % 2]
            fw.dma('sp', g_, S['tm_D_vo'][rs, 256:512])
            for d in range(2):
                o_ = oo[d][ti % 2]
                fw.dma('sp', o_, Od[d][rs, :])
                fw.act(den[d], fv(o_, 64, [[65, 4]]), AF.Abs)
                fw.ts(den[d], den[d], 1.0, ALU.max)
                fw.recip(den[d], den[d])
                fw.tt(fv(h_ if d == 0 else h2, 0, [[64, 4], [1, 64]]), fv(o_, 0, [[65, 4], [1, 64]]),
                      fv(den[d], 0, [[1, 4], [0, 64]]), ALU.mult)
            fw.tt(h_, h_, h2, ALU.add)
            fw.act(g_, g_, AF.Sigmoid)
            rms_head_norm(fw, h_, y_, gbc, sc)
            fw.tt(y_, y_, g_, ALU.mult)
            store_rows_scan(fw, Y, 768, 256, y_[:, :], ti, colmajor)
        fw.barrier()


ILD = {}


def set_psum(fw, mode):
    if getattr(fw, '_ps_stack', None) is not None:
        fw._ps_stack.close()
    fw._ps_stack = ExitStack()
    fw._ps_gen = getattr(fw, '_ps_gen', 0) + 1
    if mode == 'full':
        fw.psb = [fw.ps([128, 512], name=f'psb{i}_{fw._ps_gen}', stack=fw._ps_stack) for i in range(8)]
    else:
        banks = [fw.ps([128, 512], name=f'psk{i}_{fw._ps_gen}', stack=fw._ps_stack) for i in range(8)]
        fw.psh = []
        for i, b in enumerate(banks):
            for hf in range(2):
                tl = Tl(fw, b.t[:, hf * 256:(hf + 1) * 256], 'ps')
                tl.name = f'psh{i}_{hf}_{fw._ps_gen}'
                fw.psh.append(tl)


def finish_rms_gate(fw, L, Od, gsrc, gain, ycol, func, Y, colmajor, pfx):
    with ExitStack() as st:
        gbc = fw.sb([128, 256], name=f'{pfx}_gbc', stack=st)
        fw.dma('sp', gbc, dram_bcast_rows(L[gain], 0, 256))
        o0 = [fw.sb([128, 256], name=f'{pfx}_o0{i}', stack=st) for i in range(2)]
        o1 = [fw.sb([128, 256], name=f'{pfx}_o1{i}', stack=st) for i in range(2)]
        gg = [fw.sb([128, 256], name=f'{pfx}_g{i}', stack=st) for i in range(2)]
        yy = [fw.sb([128, 256], name=f'{pfx}_y{i}', stack=st) for i in range(2)]
        sc = {'sq': fw.sb([128, 256], name=f'{pfx}_sq', stack=st), 'ss': fw.sb([128, 4], name=f'{pfx}_ss', stack=st),
              'r': fw.sb([128, 4], name=f'{pfx}_r', stack=st)}
        for ti in range(T // 128):
            a0, a1, g_, y_ = o0[ti % 2], o1[ti % 2], gg[ti % 2], yy[ti % 2]
            rs = slice(ti * 128, (ti + 1) * 128)
            fw.dma('sp', a0, Od[0][rs, :])
            fw.dma('sp', a1, Od[1][rs, :])
            fw.dma('sp', g_, gsrc[rs, :])
            fw.tt(a0, a0, a1, ALU.add)
            fw.act(g_, g_, func)
            rms_head_norm(fw, a0, y_, gbc, sc)
            fw.tt(y_, y_, g_, ALU.mult)
            store_rows_scan(fw, Y, ycol, 256, y_[:, :], ti, colmajor)
        fw.barrier()


def neumann_solve(fw, U0, L0, Ytl, UL, psi, C=64, w=64):
    psY, psU, psL = psi
    Uc, Lc = U0, L0
    for lvl in range(6):
        for h in range(4):
            fw.mm(psY[0:C, h * w:(h + 1) * w], Uc[0:C, h * C:(h + 1) * C], Ytl[0:C, h * w:(h + 1) * w], inc=(h == 3))
        fw.tt(Ytl[0:C, 0:4 * w], Ytl[0:C, 0:4 * w], psY[0:C, 0:4 * w], ALU.subtract if lvl == 0 else ALU.add)
        if lvl < 5:
            Un, Ln = UL[lvl % 2]
            for h in range(4):
                hs = slice(h * C, (h + 1) * C)
                fw.mm(psU[0:C, hs], Lc[0:C, hs], Uc[0:C, hs], inc=(h == 3))
            fw.copy(Un[0:C, :], psU[0:C, 0:4 * C], e='act')
            if lvl < 4:
                for h in range(4):
                    hs = slice(h * C, (h + 1) * C)
                    fw.mm(psL[0:C, hs], Uc[0:C, hs], Lc[0:C, hs], inc=(h == 3))
                fw.copy(Ln[0:C, :], psL[0:C, 0:4 * C], e='act')
            Uc, Lc = Un, Ln


def bc3(tl, off, C, inner, hstep=1, p0=0):
    return fv(tl, off, [[hstep, 4], [0, inner]], p0, C)


def gdn(fw, L, S, Y, layer, colmajor, stop=9):
    C = 64
    PC = S['tm_C']
    psb = fw.psb
    ident = fw.ident
    with ExitStack() as st:
        wbc = fw.sb([128, 5 * 768], name='c_wbc', stack=st)
        fw.dma('sp', wbc, dram_bcast_rows(L['gdn_conv'], 0, 5 * 768))
        dtb = fw.sb([128, 8], name='c_dtb', stack=st)
        nA = fw.sb([128, 8], name='c_nA', stack=st)
        fw.dma('sp', dtb, dram_bcast_rows(L['gdn_dt_bias'], 0, 8))
        fw.dma('sp', nA, dram_bcast_rows(L['gdn_a_log'], 0, 8))
        fw.act(nA, nA, AF.Exp)
        fw.ts(nA, nA, -1.0, ALU.mult)
        xs = [[fw.sb([128, 768], name=f'c_x{j}{i}', stack=st) for j in range(5)] for i in range(2)]
        acc = [fw.sb([128, 768], name=f'c_acc{i}', stack=st) for i in range(2)]
        sq = fw.sb([128, 512], name='c_sq', stack=st)
        ss = fw.sb([128, 8], name='c_ss', stack=st)
        qkn = [fw.sb([128, 512], name=f'c_qkn{i}', stack=st) for i in range(2)]
        stq = [fw.sb([64, 512], name=f'c_stq{i}', stack=st) for i in range(2)]
        stk = [fw.sb([64, 512], name=f'c_stk{i}', stack=st) for i in range(2)]
        ba = [fw.sb([128, 16], name=f'c_ba{i}', stack=st) for i in range(2)]
        gl = [fw.sb([128, 16], name=f'c_gl{i}', stack=st) for i in range(2)]
        for ti in range(T // 128):
            t0 = ti * 128
            seg0, seg1 = (0, NCTX) if ti < 2 else (NCTX, T)
            X = xs[ti % 2]
            ac, qk, sq_, sk_, ba_, gl_ = acc[ti % 2], qkn[ti % 2], stq[ti % 2], stk[ti % 2], ba[ti % 2], gl[ti % 2]
            for j in range(5):
                s = j - 2
                lo, hi = max(seg0, t0 + s), min(seg1, t0 + 128 + s)
                if lo != t0 + s or hi != t0 + 128 + s:
                    fw.memset(X[j], 0.0, e='pool')
                fw.dma('sp', X[j][lo - (t0 + s): hi - (t0 + s), :], PC[lo:hi, 0:768])
                fw.tt(X[j], X[j], wbc[:, j * 768:(j + 1) * 768], ALU.mult, e=('pool' if j % 2 else 'dve'))
            fw.tt(ac, X[0], X[1], ALU.add)
            fw.tt(ac, ac, X[2], ALU.add, e='pool')
            fw.tt(ac, ac, X[3], ALU.add)
            fw.tt(ac, ac, X[4], ALU.add, e='pool')
            fw.act(ac, ac, AF.Silu)
            fw.tt(sq, ac[:, 0:512], ac[:, 0:512], ALU.mult)
            fw.reduce(ss, fv(sq, 0, [[64, 8], [1, 64]]), ALU.add)
            fw.ts(ss, ss, 1e-12, ALU.add)
            fw.act(ss, ss, AF.Sqrt)
            fw.recip(ss, ss)
            fw.ts(ss[:, 0:4], ss[:, 0:4], 0.125, ALU.mult)
            fw.tt(fv(qk, 0, [[64, 8], [1, 64]]), fv(ac, 0, [[64, 8], [1, 64]]), fv(ss, 0, [[1, 8], [0, 64]]), ALU.mult)
            for g in range(8):
                p = psb[g // 4 + 2 * (ti % 2)]
                fw.tr(p[0:64, (g % 4) * 128:(g % 4 + 1) * 128], qk[:, g * 64:(g + 1) * 64], ident)
            fw.copy(sq_[0:64, :], psb[0 + 2 * (ti % 2)][0:64, :], e='act')
            fw.copy(sk_[0:64, :], psb[1 + 2 * (ti % 2)][0:64, :])
            fw.dma('pool', S['C_QT'][:, t0:t0 + 128].re('(h p) t -> p h t', p=64), fv(sq_, 0, [[128, 4], [1, 128]], 0, 64))
            fw.dma('pool', S['C_KT'][:, t0:t0 + 128].re('(h p) t -> p h t', p=64), fv(sk_, 0, [[128, 4], [1, 128]], 0, 64))
            fw.dma('pool', S['C_KV'][t0:t0 + 128, 0:256], qk[:, 256:512])
            fw.dma('pool', S['C_KV'][t0:t0 + 128, 256:512], ac[:, 512:768])
            fw.dma('sp', ba_, S['tm_C_ba'][t0:t0 + 128, :])
            fw.act(gl_[:, 0:8], ba_[:, 0:8], AF.Sigmoid)
            fw.tt(ba_[:, 8:16], ba_[:, 8:16], dtb, ALU.add)
            fw.act(ba_[:, 8:16], ba_[:, 8:16], AF.Exp)
            fw.ts(ba_[:, 8:16], ba_[:, 8:16], 1.0, ALU.add)
            fw.act(ba_[:, 8:16], ba_[:, 8:16], AF.Ln)
            fw.tt(gl_[:, 8:16], ba_[:, 8:16], nA, ALU.mult)
            fw.dma('pool', S['C_G'][t0:t0 + 128, :], gl_)
        fw.barrier()
    if stop <= 1:
        return
    Od = [S['C_O0'], S['C_O1']]
    ILD['gdn'](fw, S, Od)
    if stop <= 2:
        return
    finish_rms_gate(fw, L, Od, PC[:, 768:1024], 'gdn_norm_g', 512, AF.Silu, Y, colmajor, 'cf')


RWKV_GN_EPS = 64e-5
IL = {}
TP = T + 4
SEGS = [(1, 0, NCTX), (NCTX + 3, NCTX, NLAT)]


def shift_tile(fw, xp, nb, out, hm, omm, P):
    for ps0, t0, n in SEGS:
        fw.tt(nb[0:P, t0:t0 + n], xp[0:P, ps0 - 1: ps0 - 1 + n], xp[0:P, ps0 + 1: ps0 + 1 + n], ALU.add, e='pool')
        fw.ts(nb[0:P, t0:t0 + n], nb[0:P, t0:t0 + n], hm, ALU.mult)
        fw.stt(out[0:P, t0:t0 + n], xp[0:P, ps0: ps0 + n], omm, nb[0:P, t0:t0 + n], ALU.mult, ALU.add)


def load_padded(fw, xp, src, P):
    for ps0, t0, n in SEGS:
        fw.dma('sp', xp[0:P, ps0:ps0 + n], src[:, t0:t0 + n])


def rwkv7(fw, L, S, Y, layer, colmajor, stop=9):
    C = 64
    psb = fw.psb
    ident = fw.ident
    PT = S['fm_B_all']
    LW = [S['B_LW0'], S['B_LW1']]
    AA = [S['B_AA0'], S['B_AA1']]
    KD = [S['B_KD0'], S['B_KD1']]
    BB = [S['B_BB0'], S['B_BB1']]
    blocks9 = [(t0, min(512, T - t0)) for t0 in range(0, T, 512)]
    with ExitStack() as st:
        pc = fw.sb([128, 14], name='r_pc', stack=st)
        fw.dma('sp', pc, L['rwkv_pcol'])
        mu128 = fw.sb([128, 9], name='r_mu128', stack=st)
        mu64 = fw.sb([64, 18], name='r_mu64', stack=st)
        fw.dma('sp', mu128, L['rwkv_mu128'])
        fw.dma('sp', mu64, L['rwkv_mu64'])
        hm128 = fw.sb([128, 9], name='r_hm128', stack=st)
        om128 = fw.sb([128, 9], name='r_om128', stack=st)
        hm64 = fw.sb([64, 18], name='r_hm64', stack=st)
        om64 = fw.sb([64, 18], name='r_om64', stack=st)
        fw.ts(hm128, mu128, 0.5, ALU.mult)
        fw.ts(om128, mu128, -1.0, ALU.mult, s2=1.0, op1=ALU.add)
        fw.ts(hm64, mu64, 0.5, ALU.mult)
        fw.ts(om64, mu64, -1.0, ALU.mult, s2=1.0, op1=ALU.add)
        npc = fw.sb([128, 14], name='r_npc', stack=st)
        fw.ts(npc, pc, -1.0, ALU.mult)
        omka = fw.sb([128, 2], name='r_omka', stack=st)
        fw.ts(omka, pc[:, 2:4], -1.0, ALU.mult, s2=1.0, op1=ALU.add)
        w2 = fw.sb([64, 512], name='r_w2', stack=st)
        a2 = fw.sb([64, 512], name='r_a2', stack=st)
        g2 = fw.sb([128, 256], name='r_g2', stack=st)
        fw.dma('sp', fv(w2, 0, [[256, 2], [1, 256]], 0, 64), V(L['rwkv_w2'], L['rwkv_w2'].ap().rearrange('d r c -> r d c')))
        fw.dma('sp', fv(a2, 0, [[256, 2], [1, 256]], 0, 64), V(L['rwkv_a2'], L['rwkv_a2'].ap().rearrange('d r c -> r d c')))
        fw.dma('sp', g2, L['rwkv_g2'])
        xp = [fw.sb([128, TP], name=f'r_xp{i}', stack=st) for i in range(2)]
        fw.memset(xp[0], 0.0)
        fw.memset(xp[1], 0.0)
        nb = fw.sb([128, T], name='r_nb', stack=st)
        stg = [fw.sb([128, 512], name=f'r_stg{i}', stack=st) for i in range(3)]
        si = 0
        with ExitStack() as st1:
            lo = fw.sb([64, T], name='r_lo', stack=st1)
            for kind in range(2):
                for d in range(2):
                    row0 = 768 + kind * 128 + d * 64
                    col = row0 // 64
                    x_ = xp[(kind * 2 + d) % 2]
                    load_padded(fw, x_, PT[row0:row0 + 64, :], 64)
                    shift_tile(fw, x_, nb, lo, hm64[:, col:col + 1], om64[:, col:col + 1], 64)
                    if kind == 0:
                        fw.act(lo, lo, AF.Tanh)
                    wsrc = w2 if kind == 0 else a2
                    for ct in range(2):
                        bcol = (6 if kind == 0 else 10) + d * 2 + ct
                        for (t0, n) in blocks9:
                            p = psb[si % 4]
                            sg = stg[si % 3]
                            si += 1
                            fw.mm(p[:, 0:n], wsrc[0:64, d * 256 + ct * 128: d * 256 + (ct + 1) * 128], lo[0:64, t0:t0 + n])
                            fw.act(sg[:, 0:n], p[:, 0:n], AF.Exp, bias=npc[:, bcol:bcol + 1], scale=-1.0)
                            fw.ts(sg[:, 0:n], sg[:, 0:n], 1.0, ALU.add)
                            fw.recip(sg[:, 0:n], sg[:, 0:n])
                            if kind == 0:
                                fw.ts(sg[:, 0:n], sg[:, 0:n], -0.6065306597126334, ALU.mult, e='pool')
                                fw.dma('pool', LW[d][ct * 128:(ct + 1) * 128, t0:t0 + n], sg[:, 0:n])
                            else:
                                fw.dma('pool', AA[d][ct * 128:(ct + 1) * 128, t0:t0 + n], sg[:, 0:n])
            fw.barrier()
        with ExitStack() as st2:
            rs_ = fw.sb([128, T], name='r_rs', stack=st2)
            ks_ = fw.sb([128, T], name='r_ks', stack=st2)
            kk_ = fw.sb([128, T], name='r_kk', stack=st2)
            aa_ = [fw.sb([128, T], name=f'r_aa{i}', stack=st2) for i in range(2)]
            t1_ = fw.sb([128, T], name='r_t1', stack=st2)
            sbs = fw.sb([128, 34 * 4], name='r_sbs', stack=st2)
            hs_off = CST_MAP['hsel'][0]
            b64_off = CST_MAP['blk64'][0]
            for ct in range(2):
                load_padded(fw, xp[0], PT[ct * 128:(ct + 1) * 128, :], 128)
                shift_tile(fw, xp[0], nb, rs_, hm128[:, ct:ct + 1], om128[:, ct:ct + 1], 128)
                fw.dma('pool', S['B_RT'][ct * 128:(ct + 1) * 128, :], rs_)
                load_padded(fw, xp[1], PT[256 + ct * 128: 256 + (ct + 1) * 128, :], 128)
                shift_tile(fw, xp[1], nb, ks_, hm128[:, 2 + ct:3 + ct], om128[:, 2 + ct:3 + ct], 128)
                fw.ts(kk_, ks_, pc[:, ct:ct + 1], ALU.mult)
                fw.tt(t1_, kk_, kk_, ALU.mult, e='pool')
                for (t0, n) in blocks9:
                    p = psb[si % 4]
                    si += 1
                    fw.mm(p[:, 0:n], fw.cst[:, b64_off:b64_off + 128], t1_[:, t0:t0 + n])
                    fw.ts(nb[:, t0:t0 + n], p[:, 0:n], 1e-12, ALU.add)
                fw.act(nb, nb, AF.Sqrt)
                fw.recip(nb, nb)
                fw.tt(kk_, kk_, nb, ALU.mult)
                fw.dma('pool', S['B_KK'][ct * 128:(ct + 1) * 128, :], kk_)
                pS = psb[4 + ct]
                for d in range(2):
                    a_ = aa_[d]
                    fw.dma('sp', a_, AA[d][ct * 128:(ct + 1) * 128, :])
                    fw.tt(t1_, kk_, a_, ALU.mult, e='pool')
                    fw.dma('pool', BB[d][ct * 128:(ct + 1) * 128, :], t1_)
                    fw.ts(a_, a_, pc[:, 2 + ct:3 + ct], ALU.mult, s2=omka[:, ct:ct + 1], op1=ALU.add)
                    fw.tt(a_, a_, ks_, ALU.mult)
                    fw.dma('pool', KD[d][ct * 128:(ct + 1) * 128, :], a_)
                    fw.stt(a_, a_, pc[:, 4 + ct:5 + ct], rs_, ALU.mult, ALU.mult)
                    for ti in range(34):
                        fw.mm(pS[:, d * 68 + ti * 2: d * 68 + ti * 2 + 2], a_[:, ti * 128:(ti + 1) * 128], fw.cst[:, hs_off:hs_off + 2],
                              inc=(ti == 33))
                fw.copy(fv(sbs, ct * 2, [[4, 34], [1, 2]]), fv(pS, 0, [[2, 34], [1, 2]]))
                fw.tt(fv(sbs, ct * 2, [[4, 34], [1, 2]]), fv(sbs, ct * 2, [[4, 34], [1, 2]]), fv(pS, 68, [[2, 34], [1, 2]]), ALU.add)
            fw.dma('pool', S['B_SB'][:, :].re('(n p) c -> p n c', p=128), fv(sbs, 0, [[4, 34], [1, 4]]))
            fw.barrier()
        with ExitStack() as st3:
            gs_ = fw.sb([128, T], name='r_gs', stack=st3)
            load_padded(fw, xp[0], PT[1024:1152, :], 128)
            shift_tile(fw, xp[0], nb, gs_, hm128[:, 8:9], om128[:, 8:9], 128)
            fw.act(gs_, gs_, AF.Sigmoid)
            for ti in range(34):
                p = psb[ti % 4]
                sg = stg[ti % 3]
                fw.mm(p[:, 0:256], gs_[:, ti * 128:(ti + 1) * 128], g2[:, :])
                fw.copy(sg[:, 0:256], p[:, 0:256], e=('act' if ti % 2 else 'dve'))
                fw.dma('pool', S['B_GATE'][ti * 128:(ti + 1) * 128, :], sg[:, 0:256])
            fw.barrier()
        with ExitStack() as st4:
            mvb = fw.sb([128, 256], name='r_mvb', stack=st4)
            fw.dma('sp', mvb, dram_bcast_rows(L['rwkv_mu_row'], 512, 256))
            vv = [[fw.sb([128, 256], name=f'r_v{j}{i}', stack=st4) for j in range(3)] for i in range(2)]
            for ti in range(34):
                t0 = ti * 128
                seg0, seg1 = (0, NCTX) if ti < 2 else (NCTX, T)
                X = vv[ti % 2]
                for j in range(3):
                    s = j - 1
                    lo_, hi_ = max(seg0, t0 + s), min(seg1, t0 + 128 + s)
                    if lo_ != t0 + s or hi_ != t0 + 128 + s:
                        fw.memset(X[j], 0.0, e='pool')
                    fw.dma('sp', X[j][lo_ - (t0 + s): hi_ - (t0 + s), :], S['tm_B_v'][lo_:hi_, :])
                fw.tt(X[0], X[0], X[2], ALU.add, e='pool')
                fw.stt(X[0], X[0], 0.5, X[1], ALU.mult, ALU.subtract)
                fw.tt(X[0], X[0], mvb, ALU.mult)
                fw.tt(X[0], X[0], X[1], ALU.add)
                fw.dma('pool', S['B_VS'][t0:t0 + 128, :], X[0])
            fw.barrier()
    if stop <= 1:
        return
    Od = [S['B_O0'], S['B_O1']]
    IL['rwkv'](fw, S, LW, KD, BB, Od)
    if stop <= 2:
        return
    with ExitStack() as st:
        gb = fw.sb([128, 256], name='rf_gb', stack=st)
        bbt = fw.sb([128, 256], name='rf_bb', stack=st)
        fw.dma('sp', gb, dram_bcast_rows(L['rwkv_ln_g'], 0, 256))
        fw.dma('sp', bbt, dram_bcast_rows(L['rwkv_ln_b'], 0, 256))
        o0 = [fw.sb([128, 256], name=f'rf_o0{i}', stack=st) for i in range(2)]
        o1 = [fw.sb([128, 256], name=f'rf_o1{i}', stack=st) for i in range(2)]
        gt = [fw.sb([128, 256], name=f'rf_g{i}', stack=st) for i in range(2)]
        vs = [fw.sb([128, 256], name=f'rf_v{i}', stack=st) for i in range(2)]
        sb_ = [fw.sb([128, 4], name=f'rf_sb{i}', stack=st) for i in range(2)]
        yy = [fw.sb([128, 256], name=f'rf_y{i}', stack=st) for i in range(2)]
        sq = fw.sb([128, 256], name='rf_sq', stack=st)
        mu_ = fw.sb([128, 4], name='rf_mu', stack=st)
        var = fw.sb([128, 4], name='rf_var', stack=st)
        for ti in range(34):
            rs = slice(ti * 128, (ti + 1) * 128)
            a0, a1, g_, v_, s_, y_ = o0[ti % 2], o1[ti % 2], gt[ti % 2], vs[ti % 2], sb_[ti % 2], yy[ti % 2]
            fw.dma('sp', a0, Od[0][rs, :])
            fw.dma('sp', a1, Od[1][rs, :])
            fw.dma('sp', g_, S['B_GATE'][rs, :])
            fw.dma('sp', v_, S['B_VS'][rs, :])
            fw.dma('sp', s_, S['B_SB'][rs, :])
            fw.tt(a0, a0, a1, ALU.add)
            fw.reduce(mu_, fv(a0, 0, [[64, 4], [1, 64]]), ALU.add)
            fw.ts(mu_, mu_, 1.0 / 64, ALU.mult)
            fw.tt(fv(a0, 0, [[64, 4], [1, 64]]), fv(a0, 0, [[64, 4], [1, 64]]), fv(mu_, 0, [[1, 4], [0, 64]]), ALU.subtract)
            fw.tt(sq, a0, a0, ALU.mult, e='pool')
            fw.reduce(var, fv(sq, 0, [[64, 4], [1, 64]]), ALU.add)
            fw.ts(var, var, 1.0 / 64, ALU.mult, s2=RWKV_GN_EPS, op1=ALU.add)
            fw.act(var, var, AF.Sqrt)
            fw.recip(var, var)
            fw.tt(fv(y_, 0, [[64, 4], [1, 64]]), fv(a0, 0, [[64, 4], [1, 64]]), fv(var, 0, [[1, 4], [0, 64]]), ALU.mult)
            fw.tt(y_, y_, gb, ALU.mult, e='pool')
            fw.tt(y_, y_, bbt, ALU.add)
            fw.tt(fv(v_, 0, [[64, 4], [1, 64]]), fv(v_, 0, [[64, 4], [1, 64]]), fv(s_, 0, [[1, 4], [0, 64]]), ALU.mult, e='pool')
            fw.tt(y_, y_, v_, ALU.add)
            fw.tt(y_, y_, g_, ALU.mult)
            store_rows_scan(fw, Y, 256, 256, y_[:, :], ti, colmajor)
        fw.barrier()


BS = 256
GSI = BS + 4


def blocks_il(d):
    lat = [(NCTX + m * BS, BS) for m in range(NLAT // BS)]
    if d == 1:
        lat = lat[::-1]
    return [(0, NCTX)] + lat


def run_interleaved(gens):
    gens = list(gens)
    while gens:
        for g in list(gens):
            try:
                next(g)
            except StopIteration:
                gens.remove(g)


def neumann_gen(fw, U0, L0, Ytl, UL, psi, C=64, w=64):
    psY, psU, psL = psi
    Uc, Lc = U0, L0
    for lvl in range(6):
        for h in range(4):
            fw.mm(psY[0:C, h * w:(h + 1) * w], Uc[0:C, h * C:(h + 1) * C], Ytl[0:C, h * w:(h + 1) * w], inc=(h == 3))
        fw.tt(Ytl[0:C, 0:4 * w], Ytl[0:C, 0:4 * w], psY[0:C, 0:4 * w], ALU.subtract if lvl == 0 else ALU.add)
        yield
        if lvl < 5:
            Un, Ln = UL[lvl % 2]
            for h in range(4):
                hs = slice(h * C, (h + 1) * C)
                fw.mm(psU[0:C, hs], Lc[0:C, hs], Uc[0:C, hs], inc=(h == 3))
            fw.copy(Un[0:C, :], psU[0:C, 0:4 * C], e='act')
            yield
            if lvl < 4:
                for h in range(4):
                    hs = slice(h * C, (h + 1) * C)
                    fw.mm(psL[0:C, hs], Uc[0:C, hs], Lc[0:C, hs], inc=(h == 3))
                fw.copy(Ln[0:C, :], psL[0:C, 0:4 * C], e='act')
                yield
            Uc, Lc = Un, Ln


def rwkv_dir_gen(fw, st, d, S, LW, KD, BB, Od):
    C = 64
    ident = fw.ident
    pb = fw.psb[4 * d: 4 * d + 4]
    names = ['r', 'kd', 'kk', 'bb', 'lw']
    F = 4 * BS
    cur = {nm: fw.sb([64, F], name=f'ri_{nm}{d}', stack=st) for nm in names}
    v = fw.sb([C, (BS // C) * 256], name=f'ri_v{d}', stack=st)
    o = fw.sb([C, (BS // C) * 256], name=f'ri_o{d}', stack=st)
    Gz = fw.sb([64, 4 * GSI], name=f'ri_Gz{d}', stack=st)
    T11 = {nm: fw.sb([64, F], name=f'ri_{nm}{d}', stack=st) for nm in
           ['cum', 'cumx', 'd3', 'Ea', 'Eb', 'Rt', 'Kb', 'Bb', 'Ah', 'Kh', 'Bh']}
    cum, cumx, d3, Ea, Eb, Rt, Kb, Bb, Ah, Kh, Bh = [T11[k] for k in
                                                     ['cum', 'cumx', 'd3', 'Ea', 'Eb', 'Rt', 'Kb', 'Bb', 'Ah', 'Kh', 'Bh']]
    glast = fw.sb([64, 64], name=f'ri_gl{d}', stack=st)
    U0 = fw.sb([C, 256], name=f'ri_U0{d}', stack=st)
    L0 = fw.sb([C, 256], name=f'ri_L0{d}', stack=st)
    Gm = fw.sb([C, 256], name=f'ri_Gm{d}', stack=st)
    Pm = fw.sb([C, 256], name=f'ri_Pm{d}', stack=st)
    Qm = fw.sb([C, 256], name=f'ri_Qm{d}', stack=st)
    UL = [(fw.sb([C, 256], name=f'ri_Un{i}{d}', stack=st), fw.sb([C, 256], name=f'ri_Ln{i}{d}', stack=st)) for i in range(2)]
    Yt = fw.sb([C, 256], name=f'ri_Y{d}', stack=st)
    KBt = fw.sb([C, 512], name=f'ri_KBt{d}', stack=st)
    Sst = [fw.sb([64, 64], name=f'ri_S{i}{d}', stack=st) for i in range(4)]
    ones = fw.sb([64, BS], name=f'ri_ones{d}', stack=st)
    negm = fw.sb([C, 64], name=f'ri_negm{d}', stack=st)
    fw.memset(ones, 1.0)
    fw.memset(Gz, 0.0)
    mi_off = CST_MAP[f'mi64_{d}'][0]
    ms_off = CST_MAP[f'ms64_{d}'][0]
    msT_off = CST_MAP[f'ms64_{1 - d}'][0]
    fw.ts(negm[:, 0:64], fw.cst[0:C, mi_off:mi_off + 64], -1.0, ALU.mult)
    for h in range(4):
        fw.memset(Sst[h], 0.0)
    srcs = {'r': S['B_RT'], 'kd': KD[d], 'kk': S['B_KK'], 'bb': BB[d], 'lw': LW[d]}
    yield
    for (t0, n) in blocks_il(d):
        nn = n // C

        def v3(tl):
            return fv(tl, 0, [[BS, 4], [1, n]], 0, 64)
        for nm in names:
            fw.dma('sp', v3(cur[nm]), srcs[nm][:, t0:t0 + n].re('(h p) t -> p h t', p=64))
        fw.dma('sp', fv(v, 0, [[256, nn], [1, 256]], 0, C), S['B_VS'][t0:t0 + n, :].re('(n c) d -> c n d', c=C))
        l = cur['lw']
        for h in range(4):
            fw.scan(Gz[:, h * GSI + 1: h * GSI + 1 + n], ones[:, 0:n], l[:, h * BS: h * BS + n], 0.0, ALU.mult, ALU.add)
        yield
        lastc = C - 1 if d == 0 else 0
        for h in range(4):
            oc = fv(cum, h * BS, [[C, nn], [1, C]], 0, 64)
            ox = fv(cumx, h * BS, [[C, nn], [1, C]], 0, 64)
            if d == 0:
                g0 = fv(Gz, h * GSI, [[C, nn], [0, C]], 0, 64)
                fw.tt(oc, fv(Gz, h * GSI + 1, [[C, nn], [1, C]], 0, 64), g0, ALU.subtract)
                fw.tt(ox, fv(Gz, h * GSI, [[C, nn], [1, C]], 0, 64), g0, ALU.subtract, e='pool')
            else:
                g1 = fv(Gz, h * GSI + C, [[C, nn], [0, C]], 0, 64)
                fw.tt(oc, g1, fv(Gz, h * GSI, [[C, nn], [1, C]], 0, 64), ALU.subtract)
                fw.tt(ox, g1, fv(Gz, h * GSI + 1, [[C, nn], [1, C]], 0, 64), ALU.subtract, e='pool')
            fw.tt(fv(d3, h * BS, [[C, nn], [1, C]], 0, 64), oc,
                  fv(cum, h * BS + lastc, [[C, nn], [0, C]], 0, 64), ALU.subtract)
        yield
        fw.act(v3(Ea), v3(cum), AF.Exp)
        fw.tt(v3(Rt), v3(cur['r']), v3(Ea), ALU.mult)
        fw.copy(fv(glast, 0, [[16, 4], [1, nn]], 0, 64), fv(Ea, lastc, [[BS, 4], [C, nn]], 0, 64))
        fw.act(v3(Eb), v3(cum), AF.Exp, scale=-1.0)
        yield
        fw.tt(v3(Kb), v3(cur['kd']), v3(Eb), ALU.mult)
        fw.tt(v3(Bb), v3(cur['bb']), v3(Eb), ALU.mult, e='pool')
        fw.act(v3(Ea), v3(cumx), AF.Exp)
        fw.tt(v3(Ah), v3(cur['kk']), v3(Ea), ALU.mult)
        yield
        fw.act(v3(Eb), v3(d3), AF.Exp, scale=-1.0)
        fw.tt(v3(Kh), v3(cur['kd']), v3(Eb), ALU.mult)
        fw.tt(v3(Bh), v3(cur['bb']), v3(Eb), ALU.mult, e='pool')
        yield
        order = range(nn) if d == 0 else range(nn - 1, -1, -1)
        for n_ in order:
            a = n_ * C

            def hc(tl, h):
                return tl[0:64, h * BS + a: h * BS + a + C]

            def msk(off):
                return fv(fw.cst, off, [[0, 4], [1, C]], 0, C)

            def m4(tl):
                return fv(tl, 0, [[C, 4], [1, C]], 0, C)
            for h in range(4):
                fw.mm(pb[0][0:C, h * C:(h + 1) * C], hc(Bb, h), hc(Ah, h), inc=(h == 3))
            fw.tt(m4(U0), m4(pb[0]), msk(ms_off), ALU.mult)
            yield
            for h in range(4):
                fw.mm(pb[1][0:C, h * C:(h + 1) * C], hc(Ah, h), hc(Bb, h), inc=(h == 3))
            fw.tt(m4(L0), m4(pb[1]), msk(msT_off), ALU.mult)
            yield
            for h in range(4):
                fw.mm(pb[2][0:C, h * C:(h + 1) * C], hc(Kb, h), hc(Ah, h), inc=(h == 3))
            fw.tt(m4(Gm), m4(pb[2]), msk(ms_off), ALU.mult)
            yield
            for h in range(4):
                fw.mm(pb[3][0:C, h * C:(h + 1) * C], hc(Kb, h), hc(Rt, h), inc=(h == 3))
            fw.tt(m4(Pm), m4(pb[3]), msk(mi_off), ALU.mult)
            yield
            for h in range(4):
                fw.mm(pb[0][0:C, h * C:(h + 1) * C], hc(Bb, h), hc(Rt, h), inc=(h == 3))
            fw.tt(m4(Qm), m4(pb[0]), fv(negm, 0, [[0, 4], [1, C]], 0, C), ALU.mult)
            yield
            for h in range(4):
                fw.mm(pb[1][0:C, h * 64:(h + 1) * 64], hc(Ah, h), Sst[h][0:64, :], start=True, stop=False, inc=False)
                fw.mm(pb[1][0:C, h * 64:(h + 1) * 64], Gm[0:C, h * C:(h + 1) * C],
                      v[0:C, n_ * 256 + h * 64: n_ * 256 + (h + 1) * 64], start=False, stop=True, inc=(h == 3))
            fw.copy(Yt[0:C, :], pb[1][0:C, 0:256], e='act')
            yield
            yield from neumann_gen(fw, U0, L0, Yt, UL, (pb[2], pb[3], pb[0]))
            for h in range(4):
                fw.mm(pb[1][0:C, h * 64:(h + 1) * 64], hc(Rt, h), Sst[h][0:64, :], start=True, stop=False, inc=False)
                fw.mm(pb[1][0:C, h * 64:(h + 1) * 64], Pm[0:C, h * C:(h + 1) * C],
                      v[0:C, n_ * 256 + h * 64: n_ * 256 + (h + 1) * 64], start=False, stop=False, inc=False)
                fw.mm(pb[1][0:C, h * 64:(h + 1) * 64], Qm[0:C, h * C:(h + 1) * C],
                      Yt[0:C, h * 64:(h + 1) * 64], start=False, stop=True, inc=(h == 3))
            fw.copy(o[0:C, n_ * 256:(n_ + 1) * 256], pb[1][0:C, 0:256], e='act')
            yield
            for h in range(4):
                fw.tr(pb[2][0:C, h * 64:(h + 1) * 64], hc(Kh, h), ident[0:64, 0:64])
                fw.tr(pb[2][0:C, 256 + h * 64: 256 + (h + 1) * 64], hc(Bh, h), ident[0:64, 0:64])
            fw.copy(KBt[0:C, 0:256], pb[2][0:C, 0:256], e='act')
            fw.ts(KBt[0:C, 256:512], pb[2][0:C, 256:512], -1.0, ALU.mult)
            yield
            for h in range(4):
                fw.mm(pb[3][0:64, h * 64:(h + 1) * 64], KBt[0:C, h * 64:(h + 1) * 64],
                      v[0:C, n_ * 256 + h * 64: n_ * 256 + (h + 1) * 64], start=True, stop=False, inc=False)
                fw.mm(pb[3][0:64, h * 64:(h + 1) * 64], KBt[0:C, 256 + h * 64: 256 + (h + 1) * 64],
                      Yt[0:C, h * 64:(h + 1) * 64], start=False, stop=True)
                fw.stt(Sst[h][0:64, :], Sst[h][0:64, :], glast[0:64, h * 16 + n_: h * 16 + n_ + 1],
                       pb[3][0:64, h * 64:(h + 1) * 64], ALU.mult, ALU.add)
            yield
        fw.dma('pool', Od[d][t0:t0 + n, :].re('(n c) d -> c n d', c=C), fv(o, 0, [[256, nn], [1, 256]], 0, C))
        yield


def rwkv_core_il(fw, S, LW, KD, BB, Od):
    with ExitStack() as st:
        run_interleaved([rwkv_dir_gen(fw, st, d, S, LW, KD, BB, Od) for d in range(2)])
        fw.barrier()


def gdn_dir_gen(fw, st, d, S, Od):
    C = 64
    ident = fw.ident
    pb = fw.psb[4 * d: 4 * d + 4]
    NB = BS // C
    q = fw.sb([64, 4 * BS], name=f'ci_q{d}', stack=st)
    k = fw.sb([64, 4 * BS], name=f'ci_k{d}', stack=st)
    kv = fw.sb([C, NB * 512], name=f'ci_kv{d}', stack=st)
    glt = fw.sb([C, NB * 16], name=f'ci_gl{d}', stack=st)
    o = fw.sb([C, NB * 256], name=f'ci_o{d}', stack=st)
    sm = {nm: fw.sb([C, 4 * NB], name=f'ci_{nm}{d}', stack=st) for nm in ['lab', 'bb', 'ecum', 'necum', 'elast', 'etot', 'dd']}
    lab, bb, ecum, necum, elast, etot, dd = [sm[x] for x in ['lab', 'bb', 'ecum', 'necum', 'elast', 'etot', 'dd']]
    t1 = {nm: fw.sb([C, 256], name=f'ci_{nm}{d}', stack=st) for nm in
          ['LA', 'Gt', 'Gi', 'Gs', 'U0', 'L0', 'At', 'Un0', 'Ln0', 'Un1', 'Ln1', 'Y', 'vn', 'Kh', 'to']}
    LA, Gt, Gi, Gs, U0, L0, At, Yt, vn, Kh, tmpo = [t1[x] for x in ['LA', 'Gt', 'Gi', 'Gs', 'U0', 'L0', 'At', 'Y', 'vn', 'Kh', 'to']]
    UL = [(t1['Un0'], t1['Ln0']), (t1['Un1'], t1['Ln1'])]
    Sst = [fw.sb([64, 64], name=f'ci_S{i}{d}', stack=st) for i in range(4)]
    ones_off = CST_MAP['ones'][0]
    for h in range(4):
        fw.memset(Sst[h], 0.0)
    mi_off = CST_MAP[f'mi64_{d}'][0]
    ms_off = CST_MAP[f'ms64_{d}'][0]
    msT_off = CST_MAP[f'ms64_{1 - d}'][0]
    mi = fw.cst[0:C, mi_off:mi_off + C]
    yield
    for (t0, n) in blocks_il(d):
        nn = n // C
        fw.dma('sp', fv(q, 0, [[BS, 4], [1, n]], 0, 64), S['C_QT'][:, t0:t0 + n].re('(h p) t -> p h t', p=64))
        fw.dma('sp', fv(k, 0, [[BS, 4], [1, n]], 0, 64), S['C_KT'][:, t0:t0 + n].re('(h p) t -> p h t', p=64))
        fw.dma('sp', fv(kv, 0, [[512, nn], [1, 512]], 0, C), S['C_KV'][t0:t0 + n, :].re('(n c) d -> c n d', c=C))
        fw.dma('sp', fv(glt, 0, [[16, nn], [1, 16]], 0, C), S['C_G'][t0:t0 + n, :].re('(n c) d -> c n d', c=C))
        fw.copy(fv(bb, 0, [[4, nn], [1, 4]], 0, C), fv(glt, 4 * d, [[16, nn], [1, 4]], 0, C))
        fw.copy(fv(lab, 0, [[4, nn], [1, 4]], 0, C), fv(glt, 8 + 4 * d, [[16, nn], [1, 4]], 0, C))
        pc = pb[3]
        fw.mm(pc[0:C, 0:nn * 4], mi, lab[0:C, 0:nn * 4])
        fw.mm(pc[0:C, 32:32 + nn * 4], fw.cst[0:C, ones_off:ones_off + C], lab[0:C, 0:nn * 4])
        yield
        fw.act(ecum[:, 0:nn * 4], pc[0:C, 0:nn * 4], AF.Exp)
        fw.ts(necum[:, 0:nn * 4], ecum[:, 0:nn * 4], -1.0, ALU.mult)
        fw.act(etot[:, 0:nn * 4], pc[0:C, 32:32 + nn * 4], AF.Exp)
        fw.copy(dd[:, 0:nn * 4], pc[0:C, 0:nn * 4])
        fw.tt(dd[:, 0:nn * 4], pc[0:C, 32:32 + nn * 4], dd[:, 0:nn * 4], ALU.subtract)
        fw.act(elast[:, 0:nn * 4], dd[:, 0:nn * 4], AF.Exp)
        yield
        order = range(nn) if d == 0 else range(nn - 1, -1, -1)
        for n_ in order:
            a = n_ * C
            vtok = kv[0:C, n_ * 512 + 256: n_ * 512 + 512]

            def m4(tl):
                return fv(tl, 0, [[C, 4], [1, C]], 0, C)

            def w4(tl):
                return fv(tl, 0, [[64, 4], [1, 64]], 0, C)
            fw.tt(m4(LA), fv(fw.cst, msT_off, [[0, 4], [1, C]], 0, C), bc3(lab, n_ * 4, C, C), ALU.mult, e='pool')
            for h in range(4):
                fw.mm(pb[0][0:C, h * C:(h + 1) * C], LA[0:C, h * C:(h + 1) * C], mi, inc=(h == 3))
            fw.act(Gt[0:C, :], pb[0][0:C, 0:256], AF.Exp)
            yield
            fw.tt(m4(Gi), m4(Gt), fv(fw.cst, mi_off, [[0, 4], [1, C]], 0, C), ALU.mult, e='pool')
            fw.tt(m4(Gs), m4(Gt), fv(fw.cst, ms_off, [[0, 4], [1, C]], 0, C), ALU.mult, e='pool')
            fw.tt(m4(Gs), m4(Gs), bc3(bb, n_ * 4, C, C), ALU.mult, e='pool')
            for h in range(4):
                cs = slice(h * BS + a, h * BS + a + C)
                fw.mm(pb[1][0:C, h * C:(h + 1) * C], k[0:64, cs], k[0:64, cs], inc=(h == 3))
            fw.tt(U0[0:C, :], pb[1][0:C, 0:256], Gs[0:C, :], ALU.mult)
            yield
            for h in range(4):
                cs = slice(h * BS + a, h * BS + a + C)
                fw.mm(pb[2][0:C, h * C:(h + 1) * C], k[0:64, cs], q[0:64, cs], inc=(h == 3))
            fw.tt(At[0:C, :], pb[2][0:C, 0:256], Gi[0:C, :], ALU.mult)
            yield
            for h in range(4):
                fw.tr(pb[3][0:C, h * C:(h + 1) * C], U0[0:C, h * C:(h + 1) * C], ident[0:64, 0:64])
            fw.copy(L0[0:C, :], pb[3][0:C, 0:256], e='act')
            yield
            for h in range(4):
                cs = slice(h * BS + a, h * BS + a + C)
                fw.mm(pb[0][0:C, h * 64:(h + 1) * 64], k[0:64, cs], Sst[h][0:64, :])
            fw.tt(w4(Yt), w4(pb[0]), bc3(necum, n_ * 4, C, 64), ALU.mult)
            fw.tt(Yt[0:C, :], Yt[0:C, :], vtok, ALU.add)
            yield
            yield from neumann_gen(fw, U0, L0, Yt, UL, (pb[1], pb[2], pb[3]))
            fw.tt(w4(vn), w4(Yt), bc3(bb, n_ * 4, C, 64), ALU.mult)
            for h in range(4):
                cs = slice(h * BS + a, h * BS + a + C)
                fw.mm(pb[0][0:C, h * 64:(h + 1) * 64], q[0:64, cs], Sst[h][0:64, :])
            fw.tt(w4(tmpo), w4(pb[0]), bc3(ecum, n_ * 4, C, 64), ALU.mult)
            yield
            for h in range(4):
                fw.mm(pb[1][0:C, h * 64:(h + 1) * 64], At[0:C, h * C:(h + 1) * C], vn[0:C, h * 64:(h + 1) * 64])
            fw.tt(o[0:C, n_ * 256:(n_ + 1) * 256], tmpo[0:C, :], pb[1][0:C, 0:256], ALU.add)
            yield
            fw.tt(w4(Kh), fv(kv, n_ * 512, [[64, 4], [1, 64]], 0, C), bc3(elast, n_ * 4, C, 64), ALU.mult, e='pool')
            for h in range(4):
                fw.mm(pb[2][0:64, h * 64:(h + 1) * 64], Kh[0:C, h * 64:(h + 1) * 64], vn[0:C, h * 64:(h + 1) * 64])
                fw.stt(Sst[h][0:64, :], Sst[h][0:64, :], etot[0:64, n_ * 4 + h: n_ * 4 + h + 1],
                       pb[2][0:64, h * 64:(h + 1) * 64], ALU.mult, ALU.add)
            yield
        fw.dma('pool', Od[d][t0:t0 + n, :].re('(n c) d -> c n d', c=C), fv(o, 0, [[256, nn], [1, 256]], 0, C))
        yield


def gdn_core_il(fw, S, Od):
    with ExitStack() as st:
        run_interleaved([gdn_dir_gen(fw, st, d, S, Od) for d in range(2)])
        fw.barrier()


def gla_dir_gen(fw, st, d, QT, KTd, LFd, Vsrc, Od, dvx, C, kscale):
    ident = fw.ident
    pb = fw.psb[4 * d: 4 * d + 4]
    psA, psT, psO, psS = pb
    W = 4 * dvx
    F = 4 * BS
    NB = BS // C
    tl = {nm: fw.sb([64, F], name=f'gi_{nm}{d}', stack=st) for nm in
          ['q', 'k', 'l', 'cum', 'd3', 'Ep', 'Em', 'E3', 'Qt', 'Kbar', 'Khat']}
    q, k, l, cum, d3, Ep, Em, E3, Qt, Kbar, Khat = [tl[x] for x in ['q', 'k', 'l', 'cum', 'd3', 'Ep', 'Em', 'E3', 'Qt', 'Kbar', 'Khat']]
    v = fw.sb([C, NB * W], name=f'gi_v{d}', stack=st)
    o = fw.sb([C, NB * W], name=f'gi_o{d}', stack=st)
    Gz = fw.sb([64, 4 * GSI], name=f'gi_Gz{d}', stack=st)
    glast = fw.sb([64, 64], name=f'gi_gl{d}', stack=st)
    am = fw.sb([C, 4 * C], name=f'gi_am{d}', stack=st)
    kt = fw.sb([C, 256], name=f'gi_kt{d}', stack=st)
    S = [fw.sb([64, dvx], name=f'gi_S{i}{d}', stack=st) for i in range(4)]
    ones = fw.sb([64, BS], name=f'gi_ones{d}', stack=st)
    fw.memset(ones, 1.0)
    fw.memset(Gz, 0.0)
    for h in range(4):
        fw.memset(S[h], 0.0)
    moff = CST_MAP[f'mi{C}_{d}'][0]
    yield
    for (t0, n) in blocks_il(d):
        nn = n // C

        def v3(t_):
            return fv(t_, 0, [[BS, 4], [1, n]], 0, 64)
        fw.dma('sp', v3(q), QT[:, t0:t0 + n].re('(h p) t -> p h t', p=64))
        fw.dma('sp', v3(k), KTd[d][:, t0:t0 + n].re('(h p) t -> p h t', p=64))
        fw.dma('sp', v3(l), LFd[d][:, t0:t0 + n].re('(h p) t -> p h t', p=64))
        vd, vc0, vw = Vsrc[d]
        fw.dma('sp', fv(v, 0, [[W, nn], [1, W]], 0, C), vd[t0:t0 + n, vc0:vc0 + W].re('(n c) d -> c n d', c=C))
        for h in range(4):
            fw.scan(Gz[:, h * GSI + 1: h * GSI + 1 + n], ones[:, 0:n], l[:, h * BS: h * BS + n], 0.0, ALU.mult, ALU.add)
        yield
        lastc = C - 1 if d == 0 else 0
        for h in range(4):
            outv = fv(cum, h * BS, [[C, nn], [1, C]], 0, 64)
            if d == 0:
                fw.tt(outv, fv(Gz, h * GSI + 1, [[C, nn], [1, C]], 0, 64), fv(Gz, h * GSI, [[C, nn], [0, C]], 0, 64), ALU.subtract)
            else:
                fw.tt(outv, fv(Gz, h * GSI + C, [[C, nn], [0, C]], 0, 64), fv(Gz, h * GSI, [[C, nn], [1, C]], 0, 64), ALU.subtract)
            fw.tt(fv(d3, h * BS, [[C, nn], [1, C]], 0, 64), outv,
                  fv(cum, h * BS + lastc, [[C, nn], [0, C]], 0, 64), ALU.subtract, e='pool')
        yield
        fw.act(v3(Ep), v3(cum), AF.Exp)
        fw.act(v3(Em), v3(cum), AF.Exp, scale=-1.0)
        fw.act(v3(E3), v3(d3), AF.Exp, scale=-1.0)
        fw.tt(v3(Qt), v3(q), v3(Ep), ALU.mult)
        yield
        fw.stt(v3(Kbar), v3(k), kscale, v3(Em), ALU.mult, ALU.mult)
        fw.stt(v3(Khat), v3(k), kscale, v3(E3), ALU.mult, ALU.mult)
        fw.copy(fv(glast, 0, [[16, 4], [1, nn]], 0, 64), fv(Ep, lastc, [[BS, 4], [C, nn]], 0, 64))
        yield
        order = range(nn) if d == 0 else range(nn - 1, -1, -1)
        for n_ in order:
            a = n_ * C
            for h in range(4):
                cs = slice(h * BS + a, h * BS + a + C)
                fw.mm(psA[0:C, h * C:(h + 1) * C], Kbar[0:64, cs], Qt[0:64, cs], inc=(h == 3))
            fw.tt(fv(am, 0, [[C, 4], [1, C]], 0, C), fv(psA, 0, [[C, 4], [1, C]], 0, C),
                  fv(fw.cst, moff, [[0, 4], [1, C]], 0, C), ALU.mult)
            yield
            for h in range(4):
                fw.tr(psT[0:C, h * 64:(h + 1) * 64], Khat[0:64, h * BS + a: h * BS + a + C], ident[0:64, 0:64])
            fw.copy(kt[0:C, :], psT[0:C, 0:256], e='act')
            yield
            for h in range(4):
                cs = slice(h * BS + a, h * BS + a + C)
                fw.mm(psO[0:C, h * dvx:(h + 1) * dvx], Qt[0:64, cs], S[h][0:64, :], start=True, stop=False, inc=False)
                fw.mm(psO[0:C, h * dvx:(h + 1) * dvx], am[0:C, h * C:(h + 1) * C],
                      v[0:C, n_ * W + h * dvx: n_ * W + (h + 1) * dvx], start=False, stop=True, inc=(h == 3))
            fw.copy(o[0:C, n_ * W:(n_ + 1) * W], psO[0:C, 0:W], e=('act' if n_ % 2 else 'dve'))
            yield
            for h in range(4):
                fw.mm(psS[0:64, h * dvx:(h + 1) * dvx], kt[0:C, h * 64:(h + 1) * 64],
                      v[0:C, n_ * W + h * dvx: n_ * W + (h + 1) * dvx])
                fw.stt(S[h][0:64, :], S[h][0:64, :], glast[0:64, h * 16 + n_: h * 16 + n_ + 1],
                       psS[0:64, h * dvx:(h + 1) * dvx], ALU.mult, ALU.add)
            yield
        fw.dma('pool', Od[d][t0:t0 + n, :].re('(n c) d -> c n d', c=C), fv(o, 0, [[W, nn], [1, W]], 0, C))
        yield


def gla_core_il(fw, name, QT, KTd, LFd, Vsrc, Od, dvx, C, kscale):
    with ExitStack() as st:
        run_interleaved([gla_dir_gen(fw, st, d, QT, KTd, LFd, Vsrc, Od, dvx, C, kscale) for d in range(2)])
        fw.barrier()


IL['rwkv'] = rwkv_core_il
ILG['gla'] = gla_core_il
ILD['gdn'] = gdn_core_il


SCRATCH = {
    'fm_A_q': [256, T], 'fm_A_f': [512, T], 'fm_B_all': [1152, T], 'fm_D_qk': [512, T], 'fm_D_gates': [16, T],
    'tm_A_ig': [T, 512], 'tm_B_v': [T, 256], 'tm_C': [T, 1024], 'tm_C_ba': [T, 16], 'tm_D_vo': [T, 512], 'tm_D_gates': [T, 16],
    'A_KT0': [256, T], 'A_KT1': [256, T], 'A_LF0': [256, T], 'A_LF1': [256, T], 'A_O0': [T, 256], 'A_O1': [T, 256],
    'B_LW0': [256, T], 'B_LW1': [256, T], 'B_AA0': [256, T], 'B_AA1': [256, T], 'B_KD0': [256, T], 'B_KD1': [256, T],
    'B_BB0': [256, T], 'B_BB1': [256, T], 'B_RT': [256, T], 'B_KK': [256, T], 'B_VS': [T, 256], 'B_GATE': [T, 256], 'B_SB': [T, 4],
    'B_O0': [T, 256], 'B_O1': [T, 256],
    'C_QT': [256, T], 'C_KT': [256, T], 'C_KV': [T, 512], 'C_G': [T, 16], 'C_O0': [T, 256], 'C_O1': [T, 256],
    'D_LF0': [256, T], 'D_LF1': [256, T], 'D_VX0': [T, 260], 'D_VX1': [T, 260], 'D_O0': [T, 260], 'D_O1': [T, 260],
    'Y': [T, 1024], 'X1': [T, 1024],
}


def host_inputs_for_layer(inp, l, b):
    cc = np.stack([inp['c_ctx'], inp['c'][b]], 0)
    d = {}
    d['c2T'] = np.ascontiguousarray(cc.T.reshape(8, 128, 2).transpose(1, 0, 2).reshape(128, 16))
    d['ada_w'] = inp['ada_w'][l]
    ab = inp['ada_b'][l]
    d['ada_b'] = ab.reshape(1, 6144)
    d['adab_col'] = np.ascontiguousarray(ab.reshape(48, 128).T)
    win = inp['w_in'][l]
    d['w_fm'] = np.ascontiguousarray(win[:, FM_COLS])
    d['w_tm'] = np.ascontiguousarray(win[:, TM_COLS])
    d['ln'] = np.stack([inp[k][l] for k in ('ln1_g', 'ln1_b', 'ln2_g', 'ln2_b')], 0)
    d['w_out'] = inp['w_out'][l]
    d['mlp_w1'] = inp['mlp_w1'][l]
    d['mlp_w2'] = inp['mlp_w2'][l]
    gam = inp['hgrn_gamma']
    d['hgrn_gamma_t'] = np.ascontiguousarray(gam.reshape(4, 2, 2, 128).transpose(3, 1, 2, 0).reshape(128, 16))
    d['hgrn_norm_g'] = inp['hgrn_norm_g'][l].reshape(1, 256)
    fb = np.zeros((16, 1), np.float32)
    fb[8:, 0] = inp['mlstm_f_bias'][l].reshape(8)
    d['mlstm_fb_col'] = fb
    d['mlstm_i_bias'] = inp['mlstm_i_bias'][l].reshape(1, 8)
    d['mlstm_norm_g'] = inp['mlstm_norm_g'][l].reshape(1, 256)
    d['gdn_conv'] = inp['gdn_conv'][l].reshape(1, 5 * 768)
    d['gdn_dt_bias'] = inp['gdn_dt_bias'][l].reshape(1, 8)
    d['gdn_a_log'] = inp['gdn_a_log'][l].reshape(1, 8)
    d['gdn_norm_g'] = inp['gdn_norm_g'][l].reshape(1, 256)
    pc = np.zeros((128, 14), np.float32)

    def c2(v):
        return v.reshape(2, 128).T
    pc[:, 0:2] = c2(inp['rwkv_k_k'][l])
    pc[:, 2:4] = c2(inp['rwkv_k_a'][l])
    pc[:, 4:6] = c2(inp['rwkv_r_k'][l].reshape(256))
    pc[:, 6:8] = c2(inp['rwkv_w0'][l][0])
    pc[:, 8:10] = c2(inp['rwkv_w0'][l][1])
    pc[:, 10:12] = c2(inp['rwkv_a0'][l][0])
    pc[:, 12:14] = c2(inp['rwkv_a0'][l][1])
    d['rwkv_pcol'] = pc
    mu = inp['rwkv_mu'][l]
    d['rwkv_mu128'] = np.ascontiguousarray(mu.reshape(9, 128).T)
    d['rwkv_mu64'] = np.ascontiguousarray(mu.reshape(18, 64).T)
    d['rwkv_mu_row'] = mu.reshape(1, 1152)
    d['rwkv_w2'] = inp['rwkv_w2'][l]
    d['rwkv_a2'] = inp['rwkv_a2'][l]
    d['rwkv_g2'] = inp['rwkv_g2'][l]
    d['rwkv_ln_g'] = inp['rwkv_ln_g'][l].reshape(1, 256)
    d['rwkv_ln_b'] = inp['rwkv_ln_b'][l].reshape(1, 256)
    return {k: np.ascontiguousarray(v, dtype=np.float32) for k, v in d.items()}


LAYER_INPUT_SHAPES = None


def layer_input_shapes():
    global LAYER_INPUT_SHAPES
    if LAYER_INPUT_SHAPES is None:
        fake = {
            'c_ctx': np.zeros(1024, np.float32), 'c': np.zeros((1, 1024), np.float32),
            'ada_w': np.zeros((1, 1024, 6144), np.float32), 'ada_b': np.zeros((1, 6144), np.float32),
            'w_in': np.zeros((1, 1024, 4512), np.float32), 'w_out': np.zeros((1, 1024, 1024), np.float32),
            'ln1_g': np.zeros((1, 1024), np.float32), 'ln1_b': np.zeros((1, 1024), np.float32),
            'ln2_g': np.zeros((1, 1024), np.float32), 'ln2_b': np.zeros((1, 1024), np.float32),
            'mlp_w1': np.zeros((1, 1024, 4096), np.float32), 'mlp_w2': np.zeros((1, 4096, 1024), np.float32),
            'hgrn_gamma': np.zeros((4, 2, 256), np.float32), 'hgrn_norm_g': np.zeros((1, 256), np.float32),
            'rwkv_mu': np.zeros((1, 1152), np.float32), 'rwkv_w0': np.zeros((1, 2, 256), np.float32),
            'rwkv_w2': np.zeros((1, 2, 64, 256), np.float32), 'rwkv_a0': np.zeros((1, 2, 256), np.float32),
            'rwkv_a2': np.zeros((1, 2, 64, 256), np.float32), 'rwkv_g2': np.zeros((1, 128, 256), np.float32),
            'rwkv_k_k': np.zeros((1, 256), np.float32), 'rwkv_k_a': np.zeros((1, 256), np.float32),
            'rwkv_r_k': np.zeros((1, 4, 64), np.float32), 'rwkv_ln_g': np.zeros((1, 256), np.float32),
            'rwkv_ln_b': np.zeros((1, 256), np.float32), 'gdn_conv': np.zeros((1, 5, 768), np.float32),
            'gdn_a_log': np.zeros((1, 2, 4), np.float32), 'gdn_dt_bias': np.zeros((1, 2, 4), np.float32),
            'gdn_norm_g': np.zeros((1, 256), np.float32), 'mlstm_i_bias': np.zeros((1, 2, 4), np.float32),
            'mlstm_f_bias': np.zeros((1, 2, 4), np.float32), 'mlstm_norm_g': np.zeros((1, 256), np.float32),
        }
        LAYER_INPUT_SHAPES = {k: list(v.shape) for k, v in host_inputs_for_layer(fake, 0, 0).items()}
    return LAYER_INPUT_SHAPES


def build_program(layers, final_last):
    nc = bass.Bass("TRN2", target_bir_lowering=False)
    with ExitStack() as st, nc.allow_low_precision(reason="bf16 matmul operands, fp32 accumulate"):
        fw = FW(nc, st)
        fw._ps_stack = None
        set_psum(fw, 'full')
        fw.ident = fw.sb([128, 128], name='ident', stack=st)
        fw.memset(fw.ident, 1.0)
        fw.op('pool', lambda: nc.gpsimd.affine_select(out=fw.ident.ap(), in_=fw.ident.ap(), pattern=[[-1, 128]],
              compare_op=ALU.is_equal, fill=0.0, base=0, channel_multiplier=1), [fw.ident], [fw.ident])
        consts = fw.dram('consts', [128, NCST], F32, kind='ExternalInput')
        fw.cst = fw.sb([128, NCST], name='cst', stack=st)
        fw.dma('sp', fw.cst, consts)
        x_ext = fw.dram('x_in', [T, 1024], F32, kind='ExternalInput')
        if final_last:
            out_t = fw.dram('out', [NLAT, 1024], F32, kind='ExternalOutput')
        else:
            out_t = fw.dram('out', [T, 1024], F32, kind='ExternalOutput')
        scr = {k: fw.dram('s_' + k, shp, F32) for k, shp in SCRATCH.items()}
        S = dict(scr)
        S['tm_C_qkv0'] = scr['tm_C'][:, 0:512]
        S['tm_C_qkv1'] = scr['tm_C'][:, 512:1024]
        xpp = [fw.dram(f'xpp{i}', [T, 1024], F32) for i in range(2)] if len(layers) > 1 else []
        shapes = layer_input_shapes()
        x_cur = x_ext
        for li, layer in enumerate(layers):
            is_last_in_prog = (li == len(layers) - 1)
            last = final_last and is_last_in_prog
            L = {k: fw.dram(f'{k}_l{layer}', shp, F32, kind='ExternalInput') for k, shp in shapes.items()}
            cm = set('AB') if layer % 2 == 1 else set('CD')
            with ExitStack() as lst:
                import os
                PH = os.environ.get('K_PHASES', '01ABCD3')
                P0 = phase0(fw, L, lst)
                if '1' in PH:
                    phase1(fw, L, P0, S, cm, x_cur)
                if 'A' in PH:
                    hgrn2(fw, L, S, scr['Y'], layer, 'A' in cm)
                if 'B' in PH:
                    rwkv7(fw, L, S, scr['Y'], layer, 'B' in cm)
                if 'C' in PH:
                    gdn(fw, L, S, scr['Y'], layer, 'C' in cm)
                if 'D' in PH:
                    mlstm(fw, L, S, scr['Y'], layer, 'D' in cm)
                if is_last_in_prog:
                    x_next = out_t
                else:
                    x_next = xpp[li % 2]
                if '3' in PH:
                    phase3(fw, L, P0, scr['Y'], x_cur, scr['X1'], x_next, last, out_lat=(out_t if last else None))
                fw.barrier()
            x_cur = x_next
        fw.finish()
    return nc


_PROG_CACHE = {}


def get_program(layers, final_last):
    key = (tuple(layers), final_last)
    if key not in _PROG_CACHE:
        _PROG_CACHE[key] = build_program(layers, final_last)
    return _PROG_CACHE[key]


LAUNCH_GROUPS = [[0, 1, 2, 3]]


def kernel(**inputs):
    inp = {k: np.asarray(v, dtype=np.float32) for k, v in inputs.items()}
    B = inp['x'].shape[0]
    ncores = 8
    xs = [np.ascontiguousarray(np.concatenate([inp['ctx'][b], inp['x'][b]], 0)) for b in range(B)]
    out = None
    for gi, layers in enumerate(LAUNCH_GROUPS):
        final_last = layers[-1] == 3
        nc = get_program(layers, final_last)
        in_maps = []
        for core in range(ncores):
            b = core % B
            m = {'consts': CST_ARR, 'x_in': xs[b]}
            for layer in layers:
                for k, v in host_inputs_for_layer(inp, layer, b).items():
                    m[f'{k}_l{layer}'] = v
            in_maps.append(m)
        res = run_bass_kernel_spmd(nc, in_maps, core_ids=list(range(ncores)))
        outs = [np.asarray(res.results[b]['out']) for b in range(B)]
        if final_last:
            out = np.stack(outs, 0).astype(np.float32)
        else:
            xs = outs
    return out
```
